# Optimizing a Trainium2 kernel written in Bass

```python
import jax, jax.numpy as jnp
from jax import lax
import numpy as np

D_MODEL = 1024
BATCH = 8
SEQ = 2048
DEPTH = 1
DEC_BATCH = 128
DEC_SEQ = 1
PAST_LEN = 16384
PAGE_SIZE = 128

MIX_WIDTH = D_MODEL
N_HEADS = 8
QK_NOPE = 64
QK_ROPE = 32
V_DIM = 64
MLA_WIDTH = N_HEADS * V_DIM
Q_RANK = D_MODEL // 4
KV_RANK = D_MODEL // 8
CHUNK = 128
N_CG = 8
CG_WIDTH = MIX_WIDTH - MLA_WIDTH
CG_DIM = CG_WIDTH // N_CG
D_FF = 2816
CONV_W = 3
PLE_DIM = 256
ROPE_THETA = 10000.0
EPS = 1e-6
Q_BLOCK = 128
SCALE = (QK_NOPE + QK_ROPE) ** -0.5
OFF_KV = Q_RANK
OFF_KR = Q_RANK + KV_RANK
OFF_U = OFF_KR + QK_ROPE
OFF_V = OFF_U + CG_WIDTH
IN_WIDTH = OFF_V + CG_WIDTH

kernel_name = 'hybrid_mla_chunkmlp_convffn_ple_step'


def rmsnorm(x, g):
    xf = x.astype(jnp.float32)
    y = xf * lax.rsqrt(jnp.mean(xf * xf, axis=-1, keepdims=True) + EPS)
    return y.astype(x.dtype) * g


def rope(x, pos):
    d = x.shape[-1]
    half = d // 2
    inv = ROPE_THETA ** (-jnp.arange(half, dtype=jnp.float32) * (2.0 / d))
    ang = pos.astype(jnp.float32)[:, None] * inv[None, :]
    cos = jnp.cos(ang)[None, :, None, :]
    sin = jnp.sin(ang)[None, :, None, :]
    x1 = x[..., :half].astype(jnp.float32)
    x2 = x[..., half:].astype(jnp.float32)
    return jnp.concatenate([x1 * cos - x2 * sin, x1 * sin + x2 * cos], axis=-1).astype(x.dtype)


def keys_nope(c, w_uk, k_norm_w):
    return rmsnorm(jnp.einsum('bkr,rhd->bkhd', c, w_uk), k_norm_w)


def attend(q_nope, q_rope, k_nope, k_rope, c, w_uv, q_pos, k_pos):
    s = (jnp.einsum('bqhd,bkhd->bhqk', q_nope, k_nope)
         + jnp.einsum('bqhd,bkd->bhqk', q_rope, k_rope)).astype(jnp.float32) * SCALE
    s = jnp.where(k_pos[None, :] <= q_pos[:, None], s, -jnp.inf)
    p = jax.nn.softmax(s, axis=-1).astype(c.dtype)
    o_lat = jnp.einsum('bhqk,bkr->bqhr', p, c)
    return jnp.einsum('bqhr,rhd->bqhd', o_lat, w_uv)


def attend_prompt(q_nope, q_rope, c_kv, k_rope, w_uk, k_norm_w, w_uv):
    B, T = q_nope.shape[0], q_nope.shape[1]
    k_n = keys_nope(c_kv, w_uk, k_norm_w)
    pos = jnp.arange(T, dtype=jnp.int32)
    nb = T // Q_BLOCK
    qn_b = q_nope.reshape(B, nb, Q_BLOCK, N_HEADS, QK_NOPE).transpose(1, 0, 2, 3, 4)
    qr_b = q_rope.reshape(B, nb, Q_BLOCK, N_HEADS, QK_ROPE).transpose(1, 0, 2, 3, 4)

    def blk(args):
        qn, qr, qp = args
        return attend(qn, qr, k_n, k_rope, c_kv, w_uv, qp, pos)

    out = lax.map(blk, (qn_b, qr_b, pos.reshape(nb, Q_BLOCK)))
    return out.transpose(1, 0, 2, 3, 4).reshape(B, T, N_HEADS, V_DIM)


def make_attend_sample(cache_c, cache_r, page_table):
    past_len = page_table.shape[1] * PAGE_SIZE

    def attend_sample(q_nope, q_rope, c_kv, k_rope, w_uk, k_norm_w, w_uv):
        T = q_nope.shape[1]
        k_pos = jnp.arange(past_len + T, dtype=jnp.int32)
        q_pos = past_len + jnp.arange(T, dtype=jnp.int32)

        def one(args):
            qn, qr, cn, rn, pages = args
            past_c = cache_c[pages].reshape(past_len, KV_RANK)
            past_r = cache_r[pages].reshape(past_len, QK_ROPE)
            c_all = jnp.concatenate([past_c.astype(cn.dtype), cn], axis=0)[None]
            r_all = jnp.concatenate([past_r.astype(rn.dtype), rn], axis=0)[None]
            k_n = keys_nope(c_all, w_uk, k_norm_w)
            return attend(qn[None], qr[None], k_n, r_all, c_all, w_uv, q_pos, k_pos)[0]

        return lax.map(one, (q_nope, q_rope, c_kv, k_rope, page_table))

    return attend_sample


def chunk_mix(u, v, w_s, b_s):
    B, T, _ = v.shape
    n_ch = -(-T // CHUNK)
    pad = n_ch * CHUNK - T
    vp = jnp.pad(v, ((0, 0), (0, pad), (0, 0))).reshape(B, n_ch, CHUNK, N_CG, CG_DIM)
    w = w_s * jnp.tril(jnp.ones((CHUNK, CHUNK), w_s.dtype))[None]
    mixed = jnp.einsum('gts,bcsgd->bctgd', w, vp) + b_s.T[None, None, :, :, None]
    mixed = mixed.reshape(B, n_ch * CHUNK, CG_WIDTH)[:, :T]
    return u * mixed


def conv_ffn(hn, hist, w_ff_in, conv_w, conv_b, w_ff_out):
    a = hn @ w_ff_in
    T = a.shape[1]
    ap = jnp.concatenate([hist.astype(a.dtype), a], axis=1)
    c = conv_b
    for k in range(CONV_W):
        c = c + conv_w[k] * ap[:, k:k + T]
    g, up = c[..., :D_FF], c[..., D_FF:]
    return (jax.nn.silu(g) * up) @ w_ff_out, ap[:, ap.shape[1] - (CONV_W - 1):]


def layer(x, p_emb, pos, attend_fn, conv_hist, lw):
    B, T, _ = x.shape
    hn = rmsnorm(x, lw['attn_norm_w'])
    z = hn @ lw['w_in']
    q_lat = z[..., :OFF_KV]
    c_kv = rmsnorm(z[..., OFF_KV:OFF_KR], lw['kv_norm_w'])
    k_r = z[..., OFF_KR:OFF_U]
    u = jax.nn.gelu(z[..., OFF_U:OFF_V])
    v = jax.nn.gelu(z[..., OFF_V:])
    q = (rmsnorm(q_lat, lw['q_norm_w']) @ lw['w_uq']).reshape(B, T, N_HEADS, QK_NOPE + QK_ROPE)
    q_nope = rmsnorm(q[..., :QK_NOPE], lw['q_nope_norm_w'])
    q_rope = rope(rmsnorm(q[..., QK_NOPE:], lw['q_rope_norm_w']), pos)
    k_rope = rope(rmsnorm(k_r, lw['k_rope_norm_w'])[:, :, None, :], pos)[:, :, 0, :]
    attn = attend_fn(q_nope, q_rope, c_kv, k_rope, lw['w_uk'], lw['k_nope_norm_w'], lw['w_uv'])
    sg = chunk_mix(u, v, lw['w_s'], lw['b_s'])
    h = x + jnp.concatenate([attn.reshape(B, T, MLA_WIDTH), sg], axis=-1) @ lw['w_o']
    f, new_hist = conv_ffn(rmsnorm(h, lw['ffn_norm_w']), conv_hist, lw['w_ff_in'],
                           lw['conv_w'], lw['conv_b'], lw['w_ff_out'])
    h = h + f
    gate = jax.nn.sigmoid(rmsnorm(h, lw['ple_norm_w']) @ lw['w_ple_gate'])
    e = rmsnorm(p_emb @ lw['w_ple_proj'], lw['ple_post_norm_w'])
    h = h + gate * e
    n_keep = (T - 1) % CHUNK + 1
    return h, c_kv, k_rope, v[:, T - n_keep:], new_hist


def setup_inputs(seed: int = 0) -> dict:
    key = jax.random.key(seed)
    ks = jax.random.split(key, 40)
    f32 = jnp.float32

    def nrm(k, shape, scale):
        return jax.random.normal(k, shape, f32) * scale

    def gain(k, shape):
        return 1.0 + 0.05 * jax.random.normal(k, shape, f32)

    n_pages = PAST_LEN // PAGE_SIZE
    n_used = DEC_BATCH * n_pages
    n_pool = (n_used * 5) // 4
    page_table = jax.random.permutation(ks[5], n_pool)[:n_used].reshape(DEC_BATCH, n_pages).astype(jnp.int32)
    L = DEPTH
    return {
        'x_prompt': nrm(ks[0], (BATCH, SEQ, D_MODEL), 1.0),
        'x_sample': nrm(ks[1], (DEC_BATCH, DEC_SEQ, D_MODEL), 1.0),
        'cache_ckv': nrm(ks[2], (L, n_pool, PAGE_SIZE, KV_RANK), 1.0),
        'cache_krope': nrm(ks[3], (L, n_pool, PAGE_SIZE, QK_ROPE), 1.0),
        'state_conv': nrm(ks[4], (L, DEC_BATCH, CONV_W - 1, 2 * D_FF), 1.0),
        'page_table': page_table,
        'p_prompt': nrm(ks[6], (L, BATCH, SEQ, PLE_DIM), 1.0),
        'p_sample': nrm(ks[7], (L, DEC_BATCH, DEC_SEQ, PLE_DIM), 1.0),
        'attn_norm_w': gain(ks[8], (L, D_MODEL)),
        'w_in': nrm(ks[9], (L, D_MODEL, IN_WIDTH), D_MODEL ** -0.5),
        'q_norm_w': gain(ks[10], (L, Q_RANK)),
        'w_uq': nrm(ks[11], (L, Q_RANK, N_HEADS * (QK_NOPE + QK_ROPE)), Q_RANK ** -0.5),
        'q_nope_norm_w': gain(ks[12], (L, QK_NOPE)),
        'q_rope_norm_w': gain(ks[13], (L, QK_ROPE)),
        'kv_norm_w': gain(ks[14], (L, KV_RANK)),
        'k_rope_norm_w': gain(ks[15], (L, QK_ROPE)),
        'w_uk': nrm(ks[16], (L, KV_RANK, N_HEADS, QK_NOPE), KV_RANK ** -0.5),
        'k_nope_norm_w': gain(ks[17], (L, QK_NOPE)),
        'w_uv': nrm(ks[18], (L, KV_RANK, N_HEADS, V_DIM), KV_RANK ** -0.5),
        'w_s': nrm(ks[19], (L, N_CG, CHUNK, CHUNK), 0.5 * CHUNK ** -0.5),
        'b_s': 1.0 + 0.1 * jax.random.normal(ks[20], (L, N_CG, CHUNK), f32),
        'w_o': nrm(ks[21], (L, MIX_WIDTH, D_MODEL), MIX_WIDTH ** -0.5),
        'ffn_norm_w': gain(ks[22], (L, D_MODEL)),
        'w_ff_in': nrm(ks[23], (L, D_MODEL, 2 * D_FF), D_MODEL ** -0.5),
        'conv_w': nrm(ks[24], (L, CONV_W, 2 * D_FF), CONV_W ** -0.5),
        'conv_b': nrm(ks[25], (L, 2 * D_FF), 0.01),
        'w_ff_out': nrm(ks[26], (L, D_FF, D_MODEL), D_FF ** -0.5),
        'ple_norm_w': gain(ks[27], (L, D_MODEL)),
        'w_ple_gate': nrm(ks[28], (L, D_MODEL, D_MODEL), D_MODEL ** -0.5),
        'w_ple_proj': nrm(ks[29], (L, PLE_DIM, D_MODEL), PLE_DIM ** -0.5),
        'ple_post_norm_w': gain(ks[30], (L, D_MODEL)),
    }


def reference(x_prompt, x_sample, cache_ckv, cache_krope, state_conv, page_table, p_prompt, p_sample,
              attn_norm_w, w_in, q_norm_w, w_uq, q_nope_norm_w, q_rope_norm_w, kv_norm_w, k_rope_norm_w,
              w_uk, k_nope_norm_w, w_uv, w_s, b_s, w_o, ffn_norm_w, w_ff_in, conv_w, conv_b, w_ff_out,
              ple_norm_w, w_ple_gate, w_ple_proj, ple_post_norm_w):
    past_len = page_table.shape[1] * PAGE_SIZE
    T_p = x_prompt.shape[1]
    T_s = x_sample.shape[1]
    pos_p = jnp.arange(T_p, dtype=jnp.int32)
    pos_s = past_len + jnp.arange(T_s, dtype=jnp.int32)
    hp, hs = x_prompt, x_sample
    ckv_p, kr_p, ckv_s, kr_s, v_p, v_s, cv_p, cv_s = [], [], [], [], [], [], [], []
    for i in range(DEPTH):
        lw = {
            'attn_norm_w': attn_norm_w[i], 'w_in': w_in[i], 'q_norm_w': q_norm_w[i], 'w_uq': w_uq[i],
            'q_nope_norm_w': q_nope_norm_w[i], 'q_rope_norm_w': q_rope_norm_w[i],
            'kv_norm_w': kv_norm_w[i], 'k_rope_norm_w': k_rope_norm_w[i], 'w_uk': w_uk[i],
            'k_nope_norm_w': k_nope_norm_w[i], 'w_uv': w_uv[i], 'w_s': w_s[i], 'b_s': b_s[i],
            'w_o': w_o[i], 'ffn_norm_w': ffn_norm_w[i], 'w_ff_in': w_ff_in[i], 'conv_w': conv_w[i],
            'conv_b': conv_b[i], 'w_ff_out': w_ff_out[i], 'ple_norm_w': ple_norm_w[i],
            'w_ple_gate': w_ple_gate[i], 'w_ple_proj': w_ple_proj[i], 'ple_post_norm_w': ple_post_norm_w[i],
        }
        zero_hist = jnp.zeros((hp.shape[0], CONV_W - 1, 2 * D_FF), hp.dtype)
        hp, c1, r1, v1, h1 = layer(hp, p_prompt[i], pos_p, attend_prompt, zero_hist, lw)
        attend_s = make_attend_sample(cache_ckv[i], cache_krope[i], page_table)
        hs, c2, r2, v2, h2 = layer(hs, p_sample[i], pos_s, attend_s, state_conv[i], lw)
        ckv_p.append(c1); kr_p.append(r1); v_p.append(v1); cv_p.append(h1)
        ckv_s.append(c2); kr_s.append(r2); v_s.append(v2); cv_s.append(h2)
    new_ckv_prompt = jnp.stack(ckv_p, 0)
    new_krope_prompt = jnp.stack(kr_p, 0)
    new_ckv_sample = jnp.stack(ckv_s, 0)
    new_krope_sample = jnp.stack(kr_s, 0)
    new_v_prompt = jnp.stack(v_p, 0)
    new_v_sample = jnp.stack(v_s, 0)
    new_conv_prompt = jnp.stack(cv_p, 0)
    new_conv_sample = jnp.stack(cv_s, 0)
    return (hp, hs, new_ckv_prompt, new_krope_prompt, new_ckv_sample, new_krope_sample,
            new_v_prompt, new_v_sample, new_conv_prompt, new_conv_sample)
```

```python
import os
from contextlib import ExitStack
import numpy as np
import concourse.bass as bass
import concourse.mybir as mybir
from concourse.bass_utils import run_bass_kernel_spmd

F32 = mybir.dt.float32
BF16 = mybir.dt.bfloat16
I32 = mybir.dt.int32
AF = mybir.ActivationFunctionType
ALU = mybir.AluOpType
AX = mybir.AxisListType

NCORES = 8
T = 2048
DM = 1024
NS = 16
NPOOL = 20480
DFF = 2816
EPS = 1e-6
SCALE = 96 ** -0.5
NT = T // 128
NQB = 4

C_ANW, C_QNW, C_FNW, C_PNW, C_CW, C_CB, C_BST, C_KNW, C_BLK, C_EPS, NCOL = 0, 8, 10, 18, 26, 158, 202, 210, 211, 215, 216
R_KVW, R_KRW, R_QHW, R_KNW, R_PPW, R_INVD, R_WS00, R_BS0, R_INVH, NROW = 0, 128, 160, 928, 1440, 2464, 2467, 2475, 2483, 2499


class Tile:
    __slots__ = ("ap", "name", "w", "r", "sem", "cnt")

    def __init__(self, ap, name):
        self.ap = ap
        self.name = name
        self.w = None
        self.r = {}
        self.sem = None
        self.cnt = 0

    def __getitem__(self, k):
        return self.ap[k]


class Sched:
    ENGS = ("pe", "act", "dve", "pool", "sp")

    def __init__(self, nc, es):
        self.nc = nc
        self.es = es
        self.streams = {k: [] for k in self.ENGS}
        self.cnt = {k: 0 for k in self.ENGS}
        self.waited = {k: {} for k in self.ENGS}
        self.prog = {k: es.enter_context(nc.semaphore("prog_" + k)) for k in ("pe", "act", "dve", "pool")}
        self.final = {}
        self.nsem = 4
        self.dtiles = {}

    def _deps(self, eng, reads, writes, nosync=False):
        deps = {}

        def add(tok):
            if tok is None:
                return
            s, v = tok
            k = id(s)
            if k not in deps or deps[k][1] < v:
                deps[k] = (s, v)

        own = self.prog.get(eng)
        for t in reads:
            if nosync and t.w is not None and t.w[0] is own:
                continue
            add(t.w)
        for t in writes:
            if t.w is not None and t.w[0] is not own:
                add(t.w)
            for tok in t.r.values():
                if tok[0] is not own:
                    add(tok)
        out = []
        for k, (s, v) in deps.items():
            if eng == "pe" and s is own:
                continue
            if self.waited[eng].get(k, 0) >= v:
                continue
            self.waited[eng][k] = v
            out.append((s, v))
        return out

    def _post(self, tok, reads, writes):
        for t in reads:
            t.r[id(tok[0])] = tok
        for t in writes:
            t.w = tok
            t.r = {}

    def task(self, eng, fn, reads=(), writes=(), nosync=False):
        waits = self._deps(eng, reads, writes, nosync)
        self.cnt[eng] += 1
        tok = (self.prog[eng], self.cnt[eng])
        self.streams[eng].append((waits, fn, tok[0], 1))
        self._post(tok, reads, writes)
        return tok

    def chain(self, eng, fns, reads=(), writes=(), nosync=False):
        rd = list(reads)
        for k, fn in enumerate(fns):
            self.task(eng, fn, reads=rd if k == 0 else rd + list(writes), writes=writes, nosync=(nosync and k > 0))

    def dma(self, eng, fn, sb, reads=(), writes=(), final=False):
        waits = self._deps(eng, reads, writes)
        if sb.sem is None:
            sb.sem = self.es.enter_context(self.nc.semaphore("d_" + sb.name))
            self.nsem += 1
        sb.cnt += 16
        tok = (sb.sem, sb.cnt)
        self.dtiles[id(sb)] = sb
        self.streams[eng].append((waits, fn, tok[0], 16))
        self._post(tok, reads, writes)
        if final:
            self.final[id(sb.sem)] = tok
        return tok

    def barrier(self):
        toks = {}
        for k in ("pe", "act", "dve", "pool"):
            if self.cnt[k] > 0:
                toks[id(self.prog[k])] = (self.prog[k], self.cnt[k])
        for t in self.dtiles.values():
            toks[id(t.sem)] = (t.sem, t.cnt)
        for eng in self.ENGS:
            own = self.prog.get(eng)
            waits = []
            for k, (s, v) in toks.items():
                if s is own:
                    continue
                if self.waited[eng].get(k, 0) >= v:
                    continue
                self.waited[eng][k] = v
                waits.append((s, v))
            self.streams[eng].append((waits, None, None, 0))

    def check_deadlock(self):
        val = {}
        ptr = {k: 0 for k in self.ENGS}
        progress = True
        while progress:
            progress = False
            for k in self.ENGS:
                st = self.streams[k]
                while ptr[k] < len(st):
                    waits, fn, sem, inc = st[ptr[k]]
                    if any(val.get(id(s), 0) < v for s, v in waits):
                        break
                    if fn is not None:
                        val[id(sem)] = val.get(id(sem), 0) + inc
                    ptr[k] += 1
                    progress = True
        stuck = {k: (ptr[k], len(self.streams[k])) for k in self.ENGS if ptr[k] < len(self.streams[k])}
        assert not stuck, "semaphore deadlock: %s" % stuck

    def replay(self, eng, e):
        for waits, fn, sem, inc in self.streams[eng]:
            for s, v in waits:
                e.wait_ge(s, v)
            if fn is None:
                continue
            ins = fn(e)
            ins.then_inc(sem, inc)
        if eng == "sp":
            for s, v in self.final.values():
                e.wait_ge(s, v)


def build_program(npool=NPOOL):
    nc = bass.Bass("TRN2", target_bir_lowering=False)
    es = ExitStack()
    D = {}

    def din(name, shape, dt=F32):
        D[name] = nc.dram_tensor(name, list(shape), dt, kind="ExternalInput").ap()

    def dout(name, shape):
        D[name] = nc.dram_tensor(name, list(shape), F32, kind="ExternalOutput").ap()

    din("xp", [T, DM]); din("pp", [T, 256]); din("xs", [NS, DM]); din("ps", [NS, 256])
    din("cache_c", [npool, 128 * 128]); din("cache_r", [npool, 128 * 32])
    din("pt", [128, NS], I32); din("hist", [NS * 2, 2 * DFF])
    din("w_in", [DM, 1440]); din("w_uq", [256, 768]); din("w_uk", [128, 512]); din("w_uv", [128, 512])
    din("w_s", [8, 128, 128]); din("w_o", [DM, DM]); din("w_ff_in", [DM, 2 * DFF]); din("w_ff_out", [DFF, DM])
    din("w_gate", [DM, DM]); din("w_proj", [256, DM])
    din("cols", [128, NCOL]); din("rows", [1, NROW]); din("ident", [128, 128]); din("tri", [128, 128])
    din("cosp", [T, 16]); din("sinp", [T, 16]); din("coss", [NS, 16]); din("sins", [NS, 16])
    din("eye16", [NS, NS])
    dout("y_p", [T, DM]); dout("y_s", [NS, DM]); dout("ckv_p", [T, 128]); dout("kr_p", [T, 32])
    dout("ckv_s", [NS, 128]); dout("kr_s", [NS, 32]); dout("v_p", [128, 512]); dout("v_s", [NS, 512])
    dout("conv_p", [2, 2 * DFF]); dout("conv_s", [NS, 2, 2 * DFF])

    S = Sched(nc, es)
    tiles = {}

    GBYTES = 43392
    UBYTES = 168576
    arenaG = es.enter_context(nc.sbuf_tensor("arenaG", [128, GBYTES // 2], BF16))
    arenaU = es.enter_context(nc.sbuf_tensor("arenaU", [128, UBYTES // 2], BF16))
    apos = {"G": 0, "U": 0}
    cur = ["G"]
    peak = {"G": 0, "U": 0}

    def sb(name, shape, dt=F32):
        esz = 4 if dt in (F32, I32) else 2
        n = 1
        for s_ in shape[1:]:
            n *= s_
        nbytes = (n * esz + 63) // 64 * 64
        a = cur[0]
        base = arenaG if a == "G" else arenaU
        off = apos[a]
        apos[a] += nbytes
        peak[a] = max(peak[a], apos[a])
        lim = GBYTES if a == "G" else UBYTES
        assert apos[a] <= lim, (name, a, apos[a], lim)
        ap = base[:shape[0], off // 2:(off + n * esz) // 2]
        if dt != BF16:
            ap = ap.bitcast(dt)
        if len(shape) == 3:
            ap = ap.rearrange("p (a b) -> p a b", a=shape[1])
        elif len(shape) == 4:
            ap = ap.rearrange("p (a b c) -> p a b c", a=shape[1], b=shape[2])
        t = Tile(ap, name)
        tiles[name] = t
        return t

    def phase(a):
        print("arena peaks so far", peak, "pos", apos)
        cur[0] = a
        if a == "U":
            apos["U"] = 0

    PB = [Tile(es.enter_context(nc.psum_tensor("pb%d" % i, [128, 512], F32)), "pb%d" % i) for i in range(8)]

    def pbf(i):
        return PB[i].ap[:, :].bitcast(BF16)

    phase("G")
    ident_f = sb("ident_f", [128, 128]); ident_b = sb("ident_b", [128, 128], BF16)
    tri_f = sb("tri_f", [128, 128]); tri_b = sb("tri_b", [128, 128], BF16)
    cols = sb("cols", [128, NCOL]); rows = sb("rows", [128, NROW])
    cosp = sb("cosp", [128, NT, 16]); sinp = sb("sinp", [128, NT, 16])
    coss = sb("coss", [NS, 16]); sins = sb("sins", [NS, 16])
    eye16 = sb("eye16", [NS, NS])
    idx = sb("idx", [128, NS], I32)
    ones_b = sb("ones_b", [128, 128], BF16)
    ones_f = sb("ones_f", [128, 64])

    def ld(eng, dst, dst_ap, src_ap, reads=(), **kw):
        S.dma(eng, lambda e: e.dma_start(out=dst_ap, in_=src_ap, **kw), dst, reads=list(reads), writes=[dst])

    ld("sp", ident_f, ident_f[:, :], D["ident"][:, :])
    ld("sp", tri_f, tri_f[:, :], D["tri"][:, :])
    ld("sp", cols, cols[:, :], D["cols"][:, :])
    ld("sp", rows, rows[:, :], D["rows"][0:1, :].partition_broadcast(128))
    ld("sp", cosp, cosp[:, :, :], D["cosp"].rearrange("(n p) j -> p n j", p=128))
    ld("sp", sinp, sinp[:, :, :], D["sinp"].rearrange("(n p) j -> p n j", p=128))
    ld("sp", coss, coss[:, :], D["coss"][:, :])
    ld("sp", sins, sins[:, :], D["sins"][:, :])
    ld("sp", eye16, eye16[:, :], D["eye16"][:, :])
    ld("sp", idx, idx[:, :], D["pt"][:, :])
    S.task("dve", lambda e: e.tensor_copy(out=ident_b[:, :], in_=ident_f[:, :]), reads=[ident_f], writes=[ident_b])
    S.task("dve", lambda e: e.tensor_copy(out=tri_b[:, :], in_=tri_f[:, :]), reads=[tri_f], writes=[tri_b])
    S.task("pool", lambda e: e.memset(ones_b[:, :], 1.0), writes=[ones_b])
    S.task("pool", lambda e: e.memset(ones_f[:, :], 1.0), writes=[ones_f])

    w_uk = sb("w_uk", [128, 512], BF16); w_uv = sb("w_uv", [128, 512], BF16)
    w_uk_f = sb("w_uk_f", [128, 512]); w_ukT = sb("w_ukT", [64, 8, 128], BF16)
    QT_s = sb("QT_s", [96, 8, NS], BF16)
    Qcat_s = sb("Qcat_s", [NS, 8, 96]); Kcat_s = sb("Kcat_s", [NS, 8, 96])
    cown_b = sb("cown_b", [NS, 128], BF16)
    sgT_s = sb("sgT_s", [128, 4, NS], BF16)
    attnT_s = sb("attnT_s", [128, 4, NS], BF16)
    h_s = sb("h_s", [NS, DM])
    x_t = [sb("x_t%d" % i, [128, DM]) for i in range(2)]
    junk = sb("junk", [128, DM], BF16)
    st = [sb("st%d" % i, [128, 64]) for i in range(2)]
    stp = [sb("stp%d" % i, [128, 16]) for i in range(2)]
    ld("pool", w_uk, w_uk[:, :], D["w_uk"][:, :])
    ld("pool", w_uv, w_uv[:, :], D["w_uv"][:, :])
    ld("sp", w_uk_f, w_uk_f[:, :], D["w_uk"][:, :])
    for g4 in range(2):
        def f(e, g4=g4):
            ins = None
            for gg in range(4):
                h = g4 * 4 + gg
                ins = e.transpose(out=PB[1].ap[:64, gg * 128:(gg + 1) * 128], in_=w_uk_f[:, h * 64:(h + 1) * 64], identity=ident_f[:, :])
            return ins
        S.task("pe", f, reads=[w_uk_f, ident_f], writes=[PB[1]])
        S.task("dve", lambda e, g4=g4: e.tensor_copy(
            out=w_ukT[:, g4 * 4:(g4 + 1) * 4, :], in_=PB[1].ap[:64, :].rearrange("p (g t) -> p g t", g=4)),
            reads=[PB[1]], writes=[w_ukT])
    h_scr = nc.dram_tensor("h_scr", [T, DM], F32).ap()
    h_dr = [Tile(None, "h_dr%d" % i) for i in range(NT)]

    def col(c0, n=1):
        return cols[:, c0:c0 + n]

    def rstd_chain(ss_tile, ss_ap, out_tile, out_ap, tmp_tile, tmp_ap, np_):
        S.task("act", lambda e: e.activation(out=tmp_ap, in_=ss_ap, func=AF.Sqrt, bias=cols[:np_, C_EPS:C_EPS + 1], scale=1.0),
               reads=[ss_tile, cols], writes=[tmp_tile])
        S.task("dve", lambda e: e.reciprocal(out=out_ap, in_=tmp_ap), reads=[tmp_tile], writes=[out_tile])

    def rope_apply(dst_tile, dst1, dst2, src_tile, x1, x2, cos_ap, sin_ap, cs_tiles, tmp_tile, tmpv):
        def f(e):
            e.tensor_tensor(out=tmpv[0], in0=x1, in1=cos_ap, op=ALU.mult)
            e.tensor_tensor(out=tmpv[1], in0=x2, in1=sin_ap, op=ALU.mult)
            e.tensor_tensor(out=tmpv[2], in0=x1, in1=sin_ap, op=ALU.mult)
            return e.tensor_tensor(out=tmpv[3], in0=x2, in1=cos_ap, op=ALU.mult)
        S.task("dve", f, reads=[src_tile] + cs_tiles, writes=[tmp_tile])

        def g(e):
            e.tensor_tensor(out=dst1, in0=tmpv[0], in1=tmpv[1], op=ALU.subtract)
            return e.tensor_tensor(out=dst2, in0=tmpv[2], in1=tmpv[3], op=ALU.add)
        S.task("dve", g, reads=[tmp_tile], writes=[dst_tile])

    phase("U")
    w_in = sb("w_in", [128, 8, 1440], BF16); w_uq = sb("w_uq", [128, 2, 768], BF16)
    w_s_f = sb("w_s_f", [128, 8, 128]); wsT = sb("wsT", [128, 8, 128], BF16)
    w_oH = sb("w_oH", [64, 8, DM], BF16)
    w_oS = sb("w_oS", [128, 4, DM], BF16)
    Kop = [sb("Kop%d" % q, [96, 8, 512], BF16) for q in range(NQB)]
    Vaug = [sb("Vaug%d" % q, [128, 4, 8, 65], BF16) for q in range(NQB)]
    attnT = sb("attnT", [64, 8, 512], BF16)
    sgT = sb("sgT", [128, 4, 512], BF16)
    xT = [sb("xT%d" % i, [128, 8, 128], BF16) for i in range(2)]
    zq_b = sb("zq_b", [128, 256], BF16)
    qlT = sb("qlT", [128, 2, 128], BF16)
    q_sq = sb("q_sq", [128, 768])
    q_f = sb("q_f", [128, 8, 96])
    ropet = sb("ropet", [128, 4, 8, 16])
    Qcat_b = sb("Qcat_b", [128, 8, 96], BF16)
    QT = sb("QT", [96, 8, 512], BF16)
    c_f = [sb("c_f%d" % i, [128, 128]) for i in range(2)]
    ckvT = sb("ckvT", [128, 128], BF16)
    kn_f = sb("kn_f", [128, 512])
    Kcat_b = sb("Kcat_b", [128, 8, 96], BF16)
    kr_f = [sb("kr_f%d" % i, [128, 32]) for i in range(2)]
    krt = sb("krt", [128, 6, 16])
    u_f = sb("u_f", [128, 512])
    v_f = [sb("v_f%d" % i, [128, 512]) for i in range(2)]
    v_b = sb("v_b", [128, 512], BF16)
    mixt = sb("mixt", [128, 512])
    sg_b = sb("sg_b", [128, 512], BF16)
    pT = [sb("pT%d" % i, [128, 512], BF16) for i in range(3)]
    rden = mixt; rden_bc = kn_f
    h_t = [sb("h_t0", [128, DM])] * 2

    ld("pool", w_in, w_in[:, :, :], D["w_in"].rearrange("(c p) n -> p c n", p=128))
    ld("pool", w_uq, w_uq[:, :, :], D["w_uq"].rearrange("(c p) n -> p c n", p=128))
    ld("sp", w_s_f, w_s_f[:, :, :], D["w_s"].rearrange("g t s -> t g s"))
    ld("pool", w_oH, w_oH[:, :, :], D["w_o"][0:512, :].rearrange("(h p) n -> p h n", p=64))
    ld("pool", w_oS, w_oS[:, :, :], D["w_o"][512:1024, :].rearrange("(c p) n -> p c n", p=128))
    for g4 in range(2):
        def f(e, g4=g4):
            ins = None
            for gg in range(4):
                g = g4 * 4 + gg
                ins = e.transpose(out=PB[0].ap[:, gg * 128:(gg + 1) * 128], in_=w_s_f[:, g, :], identity=ident_f[:, :])
            return ins
        S.task("pe", f, reads=[w_s_f, ident_f], writes=[PB[0]])
        S.task("dve", lambda e, g4=g4: e.tensor_tensor(
            out=wsT[:, g4 * 4:(g4 + 1) * 4, :], in0=PB[0].ap[:, :].rearrange("p (g t) -> p g t", g=4),
            in1=tri_f[:, :].unsqueeze(1).broadcast_to([128, 4, 128]), op=ALU.mult), reads=[PB[0], tri_f], writes=[wsT])
    for q in range(NQB):
        S.task("pool", lambda e, q=q: e.memset(Vaug[q][:, :, :, :], 1.0), writes=[Vaug[q]])

    def phase_a_tile(i, sample=False):
        np_ = NS if sample else 128
        par = i % 2
        xt, xTt, stt = x_t[par], xT[par], st[par]
        qb, tl = (0, 0) if sample else (i // 4, i % 4)
        src_x = D["xs"][:, :] if sample else D["xp"][i * 128:(i + 1) * 128, :]
        ld("sp", xt, xt[:np_, :], src_x)
        S.task("act", lambda e: e.activation(out=junk[:np_, :], in_=xt[:np_, :], func=AF.Square, accum_out=stt[:np_, 0:1]),
               reads=[xt], writes=[junk, stt])

        def f(e):
            ins = None
            for c in range(8):
                ins = e.transpose(out=PB[c // 4].ap[:, (c % 4) * 128:(c % 4) * 128 + np_], in_=xt[:np_, c * 128:(c + 1) * 128],
                                  identity=ident_f[:np_, :np_])
            return ins
        S.task("pe", f, reads=[xt, ident_f], writes=[PB[0], PB[1]])
        for hh in range(2):
            S.task("dve", lambda e, hh=hh: e.tensor_tensor(
                out=xTt[:, hh * 4:(hh + 1) * 4, :np_], in0=PB[hh].ap[:, :].rearrange("p (c t) -> p c t", c=4)[:, :, :np_],
                in1=cols[:, C_ANW + hh * 4:C_ANW + hh * 4 + 4].unsqueeze(2).broadcast_to([128, 4, np_]), op=ALU.mult),
                reads=[PB[hh], cols], writes=[xTt])

        def f(e):
            ins = None
            for nb, (c0, c1) in enumerate(((0, 512), (512, 1024), (1024, 1440))):
                for c in range(8):
                    ins = e.matmul(PB[2 + nb].ap[:np_, 0:c1 - c0], lhsT=xTt[:, c, :np_], rhs=w_in[:, c, c0:c1], start=(c == 0), stop=(c == 7))
            return ins
        S.task("pe", f, reads=[xTt, w_in], writes=[PB[2], PB[3], PB[4]])
        Z0, Z1, Z2 = PB[2], PB[3], PB[4]
        S.task("act", lambda e: e.activation(out=junk[:np_, 0:256], in_=Z0.ap[:np_, 0:256], func=AF.Square, accum_out=stt[:np_, 1:2]),
               reads=[Z0], writes=[junk, stt])
        S.task("act", lambda e: e.activation(out=junk[:np_, 0:128], in_=Z0.ap[:np_, 256:384], func=AF.Square, accum_out=stt[:np_, 2:3]),
               reads=[Z0], writes=[junk, stt])
        S.task("act", lambda e: e.activation(out=junk[:np_, 0:32], in_=Z0.ap[:np_, 384:416], func=AF.Square, accum_out=stt[:np_, 3:4]),
               reads=[Z0], writes=[junk, stt])
        S.task("dve", lambda e: e.tensor_scalar(out=stt[:np_, 14:15], in0=stt[:np_, 0:1], scalar1=1.0 / DM, scalar2=None, op0=ALU.mult),
               reads=[stt], writes=[stt])
        rstd_chain(stt, stt[:np_, 14:15], stt, stt[:np_, 4:5], stt, stt[:np_, 15:16], np_)
        rx = stt[:np_, 4:5]

        S.chain("dve", [
            lambda e: e.tensor_scalar(out=stt[:np_, 5:8], in0=stt[:np_, 1:4], scalar1=rx, scalar2=rx, op0=ALU.mult, op1=ALU.mult),
            lambda e: e.tensor_tensor(out=stt[:np_, 5:8], in0=stt[:np_, 5:8], in1=rows[:np_, R_INVD:R_INVD + 3], op=ALU.mult),
        ], reads=[stt, rows], writes=[stt])
        rstd_chain(stt, stt[:np_, 5:8], stt, stt[:np_, 8:11], stt, stt[:np_, 16:19], np_)
        S.task("dve", lambda e: e.tensor_scalar(out=stt[:np_, 11:14], in0=stt[:np_, 8:11], scalar1=rx, scalar2=None, op0=ALU.mult),
               reads=[stt], writes=[stt])
        s_q, s_c, s_kr = stt[:np_, 11:12], stt[:np_, 12:13], stt[:np_, 13:14]

        vf = v_f[par]
        S.task("act", lambda e: e.activation(out=u_f[:np_, 0:96], in_=Z0.ap[:np_, 416:512], func=AF.Gelu_apprx_tanh, scale=rx),
               reads=[Z0, stt], writes=[u_f])
        S.task("act", lambda e: e.activation(out=u_f[:np_, 96:512], in_=Z1.ap[:np_, 0:416], func=AF.Gelu_apprx_tanh, scale=rx),
               reads=[Z1, stt], writes=[u_f])
        S.task("act", lambda e: e.activation(out=vf[:np_, 0:96], in_=Z1.ap[:np_, 416:512], func=AF.Gelu_apprx_tanh, scale=rx),
               reads=[Z1, stt], writes=[vf])
        S.task("act", lambda e: e.activation(out=vf[:np_, 96:512], in_=Z2.ap[:np_, 0:416], func=AF.Gelu_apprx_tanh, scale=rx),
               reads=[Z2, stt], writes=[vf])
        if sample:
            S.dma("sp", lambda e: e.dma_start(out=D["v_s"][:, :], in_=vf[:np_, :]), vf, reads=[vf], final=True)
        elif i == NT - 1:
            S.dma("sp", lambda e: e.dma_start(out=D["v_p"][:, :], in_=vf[:, :]), vf, reads=[vf], final=True)

        S.task("dve", lambda e: e.tensor_copy(out=zq_b[:np_, :], in_=Z0.ap[:np_, 0:256]), reads=[Z0], writes=[zq_b])

        def f(e):
            ins = None
            for c in range(2):
                ins = e.transpose(out=pbf(7)[:, c * 128:c * 128 + np_], in_=zq_b[:np_, c * 128:(c + 1) * 128], identity=ident_b[:np_, :np_])
            return ins
        S.task("pe", f, reads=[zq_b, ident_b], writes=[PB[7]])
        S.task("dve", lambda e: e.tensor_tensor(
            out=qlT[:, :, :np_], in0=pbf(7)[:, 0:256].rearrange("p (c t) -> p c t", c=2)[:, :, :np_],
            in1=cols[:, C_QNW:C_QNW + 2].unsqueeze(2).broadcast_to([128, 2, np_]), op=ALU.mult), reads=[PB[7], cols], writes=[qlT])

        def f(e):
            ins = None
            for nb, (c0, c1) in enumerate(((0, 512), (512, 768))):
                for c in range(2):
                    ins = e.matmul(PB[5 + nb].ap[:np_, 0:c1 - c0], lhsT=qlT[:, c, :np_], rhs=w_uq[:, c, c0:c1], start=(c == 0), stop=(c == 1))
            return ins
        S.task("pe", f, reads=[qlT, w_uq], writes=[PB[5], PB[6]])
        qfl = q_f[:np_, :, :].rearrange("p h d -> p (h d)")
        S.task("act", lambda e: e.activation(out=qfl[:, 0:512], in_=PB[5].ap[:np_, :], func=AF.Identity, scale=s_q),
               reads=[PB[5], stt], writes=[q_f])
        S.task("act", lambda e: e.activation(out=qfl[:, 512:768], in_=PB[6].ap[:np_, 0:256], func=AF.Identity, scale=s_q),
               reads=[PB[6], stt], writes=[q_f])
        S.task("act", lambda e: e.activation(out=q_sq[:np_, :], in_=qfl, func=AF.Square), reads=[q_f], writes=[q_sq])

        def f(e):
            qv = q_sq[:np_, :].rearrange("p (h d) -> p h d", h=8)
            e.tensor_reduce(out=stt[:np_, 20:28], in_=qv[:, :, 0:64], axis=AX.X, op=ALU.add)
            return e.tensor_reduce(out=stt[:np_, 28:36], in_=qv[:, :, 64:96], axis=AX.X, op=ALU.add)
        S.chain("dve", [f, lambda e: e.tensor_tensor(out=stt[:np_, 20:36], in0=stt[:np_, 20:36], in1=rows[:np_, R_INVH:R_INVH + 16], op=ALU.mult)],
                reads=[q_sq, rows], writes=[stt])
        S.task("act", lambda e: e.activation(out=q_sq[:np_, 0:16], in_=stt[:np_, 20:36], func=AF.Sqrt, bias=cols[:np_, C_EPS:C_EPS + 1], scale=1.0),
               reads=[stt, cols], writes=[q_sq])
        S.task("dve", lambda e: e.reciprocal(out=stt[:np_, 36:52], in_=q_sq[:np_, 0:16]), reads=[q_sq], writes=[stt])

        def f(e):
            e.tensor_tensor(out=q_f[:np_, :, 0:64], in0=q_f[:np_, :, 0:64], in1=stt[:np_, 36:44].unsqueeze(2).broadcast_to([np_, 8, 64]), op=ALU.mult)
            return e.tensor_tensor(out=q_f[:np_, :, 64:96], in0=q_f[:np_, :, 64:96], in1=stt[:np_, 44:52].unsqueeze(2).broadcast_to([np_, 8, 32]), op=ALU.mult)
        S.chain("dve", [f, lambda e: e.tensor_tensor(out=q_f[:np_, :, :], in0=q_f[:np_, :, :], in1=rows[:np_, R_QHW:R_QHW + 768].rearrange("p (h d) -> p h d", h=8), op=ALU.mult)],
                reads=[stt, rows, q_f], writes=[q_f])
        qdst = Qcat_s if sample else Qcat_b
        S.task("pool", lambda e: e.tensor_copy(out=qdst[:np_, :, 0:64], in_=q_f[:np_, :, 0:64]), reads=[q_f], writes=[qdst])
        if sample:
            cos_ap = coss[:, :].unsqueeze(1).broadcast_to([np_, 8, 16]); sin_ap = sins[:, :].unsqueeze(1).broadcast_to([np_, 8, 16])
            cst = [coss, sins]
        else:
            cos_ap = cosp[:, i, :].unsqueeze(1).broadcast_to([np_, 8, 16]); sin_ap = sinp[:, i, :].unsqueeze(1).broadcast_to([np_, 8, 16])
            cst = [cosp, sinp]
        rope_apply(qdst, qdst[:np_, :, 64:80], qdst[:np_, :, 80:96], q_f, q_f[:np_, :, 64:80], q_f[:np_, :, 80:96],
                   cos_ap, sin_ap, cst, ropet, [ropet[:np_, k, :, :] for k in range(4)])
        if sample:
            S.task("pool", lambda e: e.tensor_copy(out=Qcat_b[:np_, :, :], in_=Qcat_s[:np_, :, :]), reads=[Qcat_s], writes=[Qcat_b])

        def f(e):
            ins = None
            for h in range(8):
                ins = e.transpose(out=pbf(7)[:96, h * 128:h * 128 + np_], in_=Qcat_b[:np_, h, :], identity=ident_b[:np_, :np_])
            return ins
        S.task("pe", f, reads=[Qcat_b, ident_b], writes=[PB[7]])
        if sample:
            S.task("act", lambda e: e.activation(out=QT_s[:, :, :], in_=pbf(7)[:96, :].rearrange("p (h t) -> p h t", h=8)[:, :, :np_], func=AF.Copy),
                   reads=[PB[7]], writes=[QT_s])
        else:
            S.task("act", lambda e: e.activation(out=QT[:, :, tl * 128:(tl + 1) * 128], in_=pbf(7)[:96, :].rearrange("p (h t) -> p h t", h=8), func=AF.Copy),
                   reads=[PB[7]], writes=[QT])

        cf = c_f[par]
        S.task("dve", lambda e: e.scalar_tensor_tensor(out=cf[:np_, :], in0=Z0.ap[:np_, 256:384], scalar=s_c, in1=rows[:np_, R_KVW:R_KVW + 128],
                                                       op0=ALU.mult, op1=ALU.mult), reads=[Z0, stt, rows], writes=[cf])
        dst_c = D["ckv_s"][:, :] if sample else D["ckv_p"][i * 128:(i + 1) * 128, :]
        S.dma("sp", lambda e: e.dma_start(out=dst_c, in_=cf[:np_, :]), cf, reads=[cf], final=True)
        if sample:
            S.task("pool", lambda e: e.tensor_copy(out=cown_b[:, :], in_=cf[:np_, :]), reads=[cf], writes=[cown_b])
        S.task("pe", lambda e: e.transpose(out=PB[0].ap[:, 0:np_], in_=cf[:np_, :], identity=ident_f[:np_, :np_]), reads=[cf, ident_f], writes=[PB[0]])
        S.task("act", lambda e: e.activation(out=ckvT[:, :np_], in_=PB[0].ap[:, 0:np_], func=AF.Copy), reads=[PB[0]], writes=[ckvT])

        def f(e):
            ins = e.matmul(PB[5].ap[:np_, :], lhsT=ckvT[:, :np_], rhs=w_uk[:, :], start=True, stop=True)
            if not sample:
                ins = e.matmul(PB[6].ap[:np_, :], lhsT=ckvT[:, :np_], rhs=w_uv[:, :], start=True, stop=True)
            return ins
        S.task("pe", f, reads=[ckvT, w_uk, w_uv], writes=[PB[5], PB[6]])
        if not sample:
            S.task("act", lambda e: e.activation(out=Vaug[qb][:, tl, :, 0:64], in_=PB[6].ap[:, :].rearrange("p (h d) -> p h d", h=8), func=AF.Copy),
                   reads=[PB[6]], writes=[Vaug[qb]])
        S.task("act", lambda e: e.activation(out=q_sq[:np_, 0:512], in_=PB[5].ap[:np_, :], func=AF.Square), reads=[PB[5]], writes=[q_sq])

        S.chain("dve", [
            lambda e: e.tensor_reduce(out=stt[:np_, 20:28], in_=q_sq[:np_, 0:512].rearrange("p (h d) -> p h d", h=8), axis=AX.X, op=ALU.add),
            lambda e: e.tensor_scalar(out=stt[:np_, 20:28], in0=stt[:np_, 20:28], scalar1=1.0 / 64, scalar2=None, op0=ALU.mult),
        ], reads=[q_sq], writes=[stt])
        S.task("act", lambda e: e.activation(out=q_sq[:np_, 0:8], in_=stt[:np_, 20:28], func=AF.Sqrt, bias=cols[:np_, C_EPS:C_EPS + 1], scale=1.0),
               reads=[stt, cols], writes=[q_sq])
        S.task("dve", lambda e: e.reciprocal(out=stt[:np_, 28:36], in_=q_sq[:np_, 0:8]), reads=[q_sq], writes=[stt])

        S.chain("dve", [
            lambda e: e.tensor_tensor(out=kn_f[:np_, :].rearrange("p (h d) -> p h d", h=8), in0=PB[5].ap[:np_, :].rearrange("p (h d) -> p h d", h=8),
                                      in1=stt[:np_, 28:36].unsqueeze(2).broadcast_to([np_, 8, 64]), op=ALU.mult),
            lambda e: e.tensor_tensor(out=kn_f[:np_, :], in0=kn_f[:np_, :], in1=rows[:np_, R_KNW:R_KNW + 512], op=ALU.mult),
        ], reads=[PB[5], stt, rows], writes=[kn_f], nosync=True)
        kdst = Kcat_s if sample else Kcat_b
        S.task("pool", lambda e: e.tensor_copy(out=kdst[:np_, :, 0:64], in_=kn_f[:np_, :].rearrange("p (h d) -> p h d", h=8)), reads=[kn_f], writes=[kdst])
        krf = kr_f[par]
        S.task("dve", lambda e: e.scalar_tensor_tensor(out=krt[:np_, 0:2, :].rearrange("p a b -> p (a b)"), in0=Z0.ap[:np_, 384:416], scalar=s_kr,
                                                       in1=rows[:np_, R_KRW:R_KRW + 32], op0=ALU.mult, op1=ALU.mult), reads=[Z0, stt, rows], writes=[krt])
        if sample:
            c1, s1, cst1 = coss[:, :], sins[:, :], [coss, sins]
        else:
            c1, s1, cst1 = cosp[:, i, :], sinp[:, i, :], [cosp, sinp]

        def f(e):
            e.tensor_tensor(out=krt[:np_, 2, :], in0=krt[:np_, 0, :], in1=c1, op=ALU.mult)
            e.tensor_tensor(out=krt[:np_, 3, :], in0=krt[:np_, 1, :], in1=s1, op=ALU.mult)
            e.tensor_tensor(out=krt[:np_, 4, :], in0=krt[:np_, 0, :], in1=s1, op=ALU.mult)
            return e.tensor_tensor(out=krt[:np_, 5, :], in0=krt[:np_, 1, :], in1=c1, op=ALU.mult)

        def g(e):
            e.tensor_tensor(out=krf[:np_, 0:16], in0=krt[:np_, 2, :], in1=krt[:np_, 3, :], op=ALU.subtract)
            return e.tensor_tensor(out=krf[:np_, 16:32], in0=krt[:np_, 4, :], in1=krt[:np_, 5, :], op=ALU.add)
        S.chain("dve", [f, g], reads=[krt] + cst1, writes=[krf, krt])
        dst_k = D["kr_s"][:, :] if sample else D["kr_p"][i * 128:(i + 1) * 128, :]
        S.dma("sp", lambda e: e.dma_start(out=dst_k, in_=krf[:np_, :]), krf, reads=[krf], final=True)
        S.task("pool", lambda e: e.tensor_copy(out=kdst[:np_, :, 64:96], in_=krf[:np_, :].unsqueeze(1).broadcast_to([np_, 8, 32])), reads=[krf], writes=[kdst])
        if not sample:
            def f(e):
                ins = None
                for h in range(8):
                    ins = e.transpose(out=pbf(7)[:96, h * 128:(h + 1) * 128], in_=Kcat_b[:, h, :], identity=ident_b[:, :])
                return ins
            S.task("pe", f, reads=[Kcat_b, ident_b], writes=[PB[7]])
            S.task("act", lambda e: e.activation(out=Kop[qb][:, :, tl * 128:(tl + 1) * 128], in_=pbf(7)[:96, :].rearrange("p (h t) -> p h t", h=8), func=AF.Copy),
                   reads=[PB[7]], writes=[Kop[qb]])

        if sample:
            S.chain("dve", [
                lambda e: e.tensor_tensor(out=mixt[:np_, :].rearrange("p (g d) -> p g d", g=8), in0=vf[:np_, :].rearrange("p (g d) -> p g d", g=8),
                                          in1=rows[:np_, R_WS00:R_WS00 + 8].unsqueeze(2).broadcast_to([np_, 8, 64]), op=ALU.mult),
                lambda e: e.tensor_tensor(out=mixt[:np_, :].rearrange("p (g d) -> p g d", g=8), in0=mixt[:np_, :].rearrange("p (g d) -> p g d", g=8),
                                          in1=rows[:np_, R_BS0:R_BS0 + 8].unsqueeze(2).broadcast_to([np_, 8, 64]), op=ALU.add),
                lambda e: e.tensor_tensor(out=sg_b[:np_, :], in0=mixt[:np_, :], in1=u_f[:np_, :], op=ALU.mult),
            ], reads=[vf, rows, u_f], writes=[mixt, sg_b])
        else:
            S.task("pool", lambda e: e.tensor_copy(out=v_b[:, :], in_=vf[:, :]), reads=[vf], writes=[v_b])

            def f(e):
                ins = None
                for g in range(8):
                    ins = e.matmul(PB[6].ap[:, g * 64:(g + 1) * 64], lhsT=wsT[:, g, :], rhs=v_b[:, g * 64:(g + 1) * 64], start=True, stop=True)
                return ins
            S.task("pe", f, reads=[wsT, v_b], writes=[PB[6]])

            S.chain("dve", [
                lambda e: e.tensor_tensor(out=mixt[:, :].rearrange("p (g d) -> p g d", g=8), in0=PB[6].ap[:, :].rearrange("p (g d) -> p g d", g=8),
                                          in1=cols[:, C_BST:C_BST + 8].unsqueeze(2).broadcast_to([128, 8, 64]), op=ALU.add),
                lambda e: e.tensor_tensor(out=sg_b[:, :], in0=mixt[:, :], in1=u_f[:, :], op=ALU.mult),
            ], reads=[PB[6], cols, u_f], writes=[mixt, sg_b], nosync=True)

        def f(e):
            ins = None
            for c in range(4):
                ins = e.transpose(out=pbf(7)[:, c * 128:c * 128 + np_], in_=sg_b[:np_, c * 128:(c + 1) * 128], identity=ident_b[:np_, :np_])
            return ins
        S.task("pe", f, reads=[sg_b, ident_b], writes=[PB[7]])
        if sample:
            S.task("act", lambda e: e.activation(out=sgT_s[:, :, :], in_=pbf(7)[:, 0:512].rearrange("p (c t) -> p c t", c=4)[:, :, :np_], func=AF.Copy),
                   reads=[PB[7]], writes=[sgT_s])
        else:
            S.task("act", lambda e: e.activation(out=sgT[:, :, tl * 128:(tl + 1) * 128], in_=pbf(7)[:, 0:512].rearrange("p (c t) -> p c t", c=4), func=AF.Copy),
                   reads=[PB[7]], writes=[sgT])

    sc_banks = [2, 3, 4, 5]

    def attention_block(Q):
        nk = (Q + 1) * 4
        units = [(h, kj) for h in range(8) for kj in range(nk)]
        NU = len(units)

        def geo(u):
            h, kj = units[u]
            a = kj - 4 * Q
            c0 = 128 * a if a > 0 else 0
            return h, kj, kj // 4, kj % 4, a, c0, PB[sc_banks[u % 4]], pT[u % 3], PB[h % 2]

        def st_S(u):
            h, kj, kq, kt, a, c0, sbk, pt_, ob = geo(u)
            S.task("pe", lambda e: e.matmul(sbk.ap[:, c0:512], lhsT=Kop[kq][:, h, kt * 128:(kt + 1) * 128], rhs=QT[:, h, c0:512], start=True, stop=True),
                   reads=[Kop[kq], QT], writes=[sbk])

        def st_E(u):
            h, kj, kq, kt, a, c0, sbk, pt_, ob = geo(u)
            S.task("act", lambda e: e.activation(out=pt_[:, c0:512], in_=sbk.ap[:, c0:512], func=AF.Exp, scale=SCALE), reads=[sbk], writes=[pt_])
            if a >= 0:
                S.task("pool", lambda e: e.tensor_tensor(out=pt_[:, c0:c0 + 128], in0=pt_[:, c0:c0 + 128], in1=tri_b[:, :], op=ALU.mult),
                       reads=[pt_, tri_b], writes=[pt_])

        def st_V(u):
            h, kj, kq, kt, a, c0, sbk, pt_, ob = geo(u)
            S.task("pe", lambda e: e.matmul(ob.ap[:65, c0:512], lhsT=Vaug[kq][:, kt, h, :], rhs=pt_[:, c0:512], start=(kj == 0), stop=(kj == nk - 1)),
                   reads=[Vaug[kq], pt_], writes=[ob])
            if kj == nk - 1:
                S.task("dve", lambda e: e.reciprocal(out=rden[64:65, :], in_=ob.ap[64:65, :]), reads=[ob], writes=[rden])
                S.task("pe", lambda e: e.matmul(PB[6].ap[:64, :], lhsT=ones_f[64:65, 0:64], rhs=rden[64:65, :], start=True, stop=True),
                       reads=[ones_f, rden], writes=[PB[6]])
                S.task("act", lambda e: e.activation(out=rden_bc[:64, :], in_=PB[6].ap[:64, :], func=AF.Copy), reads=[PB[6]], writes=[rden_bc])
                S.task("dve", lambda e: e.tensor_tensor(out=attnT[:, h, :], in0=ob.ap[:64, :], in1=rden_bc[:64, :], op=ALU.mult),
                       reads=[ob, rden_bc], writes=[attnT])

        for s in range(NU + 2):
            if s < NU:
                st_S(s)
            if 0 <= s - 2 < NU:
                st_V(s - 2)
            if 0 <= s - 1 < NU:
                st_E(s - 1)

    def wo_tile(Q, ti):
        i = Q * 4 + ti
        xt = x_t[ti % 2]
        ht = h_t[ti % 2]
        ld("sp", xt, xt[:, :], D["xp"][i * 128:(i + 1) * 128, :])

        def f(e):
            ins = None
            for nb in range(2):
                chunks = [(attnT[:, h, ti * 128:(ti + 1) * 128], w_oH[:, h, nb * 512:(nb + 1) * 512]) for h in range(8)]
                chunks += [(sgT[:, c, ti * 128:(ti + 1) * 128], w_oS[:, c, nb * 512:(nb + 1) * 512]) for c in range(4)]
                for k, (l, r) in enumerate(chunks):
                    ins = e.matmul(PB[6 + nb].ap[:, :], lhsT=l, rhs=r, start=(k == 0), stop=(k == len(chunks) - 1))
            return ins
        S.task("pe", f, reads=[attnT, sgT, w_oH, w_oS], writes=[PB[6], PB[7]])
        for nb in range(2):
            S.task("dve", lambda e, nb=nb: e.tensor_tensor(out=ht[:, nb * 512:(nb + 1) * 512], in0=PB[6 + nb].ap[:, :], in1=xt[:, nb * 512:(nb + 1) * 512], op=ALU.add),
                   reads=[PB[6 + nb], xt], writes=[ht])
        S.dma("sp", lambda e: e.dma_start(out=h_scr[i * 128:(i + 1) * 128, :], in_=ht[:, :]), ht, reads=[ht], writes=[h_dr[i]])

    phase_a_tile(0, sample=True)
    for Q in range(NQB):
        for tl in range(4):
            phase_a_tile(Q * 4 + tl)
        attention_block(Q)
        for ti in range(4):
            wo_tile(Q, ti)
    S.barrier()

    phase("U")
    c_nat = [sb("c_nat%d" % i, [128, 128 * 128], BF16) for i in range(2)]
    r_nat = [sb("r_nat%d" % i, [128, 128 * 32], BF16) for i in range(2)]
    cT = [sb("cT%d" % i, [128, 128], BF16) for i in range(4)]
    krT = [sb("krT%d" % i, [128, 128], BF16) for i in range(2)]
    ysq = [sb("ysq%d" % i, [128, 512], BF16) for i in range(3)]
    ssq = [sb("ssq%d" % i, [128, 256]) for i in range(2)]
    rsq = sb("rsq", [128, 256]); sq_t = sb("sq_t", [128, 256])
    pTs = [sb("pTs%d" % i, [128, 256], BF16) for i in range(2)]
    qaT = sb("qaT", [128, 8, NS], BF16)
    gT_s = sb("gT_s", [64, 8, NS], BF16)
    QR4 = sb("QR4", [128, 4, 8, NS], BF16)
    qr_rep = sb("qr_rep", [NS, 8, 128], BF16)
    qr_repT = sb("qr_repT", [128, 8, NS])
    s_own = sb("s_own", [NS, 8, 96]); p_own = sb("p_own", [NS, 16])
    Pmask = sb("Pmask", [NS, NS, 8], BF16)
    OL = sb("OL", [8, NS, 128], BF16)
    OLT = sb("OLT", [128, NS, 8], BF16)
    rd_s = sb("rd_s", [8, 2])
    attn_s = sb("attn_s", [NS, 512], BF16)

    def sample_attention():
        S.task("dve", lambda e: e.tensor_scalar(out=gT_s[:, :, :], in0=QT_s[0:64, :, :], scalar1=cols[0:64, C_KNW:C_KNW + 1], scalar2=None, op0=ALU.mult),
               reads=[QT_s, cols], writes=[gT_s])

        def f(e):
            ins = None
            for h in range(8):
                ins = e.matmul(PB[7].ap[:, h * NS:(h + 1) * NS], lhsT=w_ukT[:, h, :], rhs=gT_s[:, h, :], start=True, stop=True)
            return ins
        S.task("pe", f, reads=[w_ukT, gT_s], writes=[PB[7]])
        S.task("act", lambda e: e.activation(out=qaT[:, :, :].rearrange("p h b -> p (h b)"), in_=PB[7].ap[:, 0:8 * NS], func=AF.Copy), reads=[PB[7]], writes=[qaT])
        S.task("dve", lambda e: e.tensor_copy(out=qr_rep[:, :, :].rearrange("p h (a r) -> p h a r", a=4),
                                              in_=Qcat_s[:, :, 64:96].unsqueeze(2).broadcast_to([NS, 8, 4, 32])), reads=[Qcat_s], writes=[qr_rep])

        def f(e):
            ins = None
            for h in range(8):
                ins = e.transpose(out=pbf(6)[:, h * NS:(h + 1) * NS], in_=qr_rep[:, h, :], identity=ident_b[:NS, :NS])
            return ins
        S.task("pe", f, reads=[qr_rep, ident_b], writes=[PB[6]])
        S.task("act", lambda e: e.activation(out=qr_repT[:, :, :].rearrange("p h b -> p (h b)"), in_=pbf(6)[:, 0:8 * NS], func=AF.Copy),
               reads=[PB[6]], writes=[qr_repT])

        def f(e):
            ins = None
            for a in range(4):
                ins = e.tensor_scalar(out=QR4[:, a, :, :], in0=qr_repT[:, :, :], scalar1=cols[:, C_BLK + a:C_BLK + a + 1], scalar2=None, op0=ALU.mult)
            return ins
        S.task("dve", f, reads=[qr_repT, cols], writes=[QR4])

        S.chain("dve", [
            lambda e: e.tensor_tensor(out=s_own[:, :, :], in0=Qcat_s[:, :, :], in1=Kcat_s[:, :, :], op=ALU.mult),
            lambda e: e.tensor_reduce(out=p_own[:, 0:8], in_=s_own[:, :, :], axis=AX.X, op=ALU.add),
        ], reads=[Qcat_s, Kcat_s], writes=[s_own, p_own])
        S.task("act", lambda e: e.activation(out=p_own[:, 8:16], in_=p_own[:, 0:8], func=AF.Exp, scale=SCALE), reads=[p_own], writes=[p_own])
        S.task("dve", lambda e: e.tensor_tensor(out=Pmask[:, :, :], in0=p_own[:, 8:16].unsqueeze(1).broadcast_to([NS, NS, 8]),
                                                in1=eye16[:, :].unsqueeze(2).broadcast_to([NS, NS, 8]), op=ALU.mult), reads=[p_own, eye16], writes=[Pmask])

        OB, DB = PB[6], PB[7]
        cslot = [Tile(pbf(0)[:, k * 128:(k + 1) * 128], "cslot%d" % k) for k in range(8)]
        rslot = [Tile(pbf(1)[:, k * 128:(k + 1) * 128], "rslot%d" % k) for k in range(4)]
        SNR = [PB[4], PB[5]]
        NTOT = NS * 128
        deferred = {}

        def defer(s, fn):
            deferred.setdefault(s, []).append(fn)

        def gather(b):
            cn, rn = c_nat[b % 2], r_nat[b % 2]
            S.dma("pool", lambda e: e.indirect_dma_start(out=cn[:, :], out_offset=None, in_=D["cache_c"][:, :],
                  in_offset=bass.IndirectOffsetOnAxis(ap=idx[:, b:b + 1], axis=0)), cn, reads=[idx], writes=[cn])
            S.dma("pool", lambda e: e.indirect_dma_start(out=rn[:, :], out_offset=None, in_=D["cache_r"][:, :],
                  in_offset=bass.IndirectOffsetOnAxis(ap=idx[:, b:b + 1], axis=0)), rn, reads=[idx], writes=[rn])

        def info(n):
            b, t = n // 128, n % 128
            qg = n // 32
            return b, t, t % 32, qg, n // 4

        def st_T(n):
            b, t, tl, qg, g = info(n)
            cn, rn = c_nat[b % 2], r_nat[b % 2]
            bank = PB[n % 2]
            bv = pbf(n % 2)

            def f(e):
                if n % 4 == 0:
                    e.transpose(out=bv[:, 128:256], in_=rn[:, t * 32:(t + 4) * 32], identity=ident_b[:, :])
                return e.transpose(out=bv[:, 0:128], in_=cn[:, t * 128:(t + 1) * 128], identity=ident_b[:, :])
            S.task("pe", f, reads=[cn, rn, ident_b], writes=[bank])

        def st_C(n):
            b, t, tl, qg, g = info(n)
            bank = PB[n % 2]
            bv = pbf(n % 2)
            if n % 4 == 0:
                krT_ = krT[g % 2]
                S.task("act", lambda e: e.activation(out=krT_[:, :], in_=bv[:, 128:256], func=AF.Copy), reads=[bank], writes=[krT_])
            cT_ = cT[n % 4]
            S.task("act", lambda e: e.activation(out=cT_[:, :], in_=bv[:, 0:128], func=AF.Copy), reads=[bank], writes=[cT_])

        def st_Y(n):
            b, t, tl, qg, g = info(n)
            cT_, krT_, yb, snr = cT[n % 4], krT[g % 2], PB[2 + n % 2], SNR[qg % 2]
            tt = n % 4

            def f(e):
                e.matmul(yb.ap[:, :], lhsT=cT_[:, :], rhs=w_uk[:, :], start=True, stop=True)
                e.matmul(snr.ap[:, tl * 8:(tl + 1) * 8], lhsT=cT_[:, :], rhs=qaT[:, :, b], start=True, stop=True)
                return e.matmul(snr.ap[:, 256 + tl * 8:256 + (tl + 1) * 8], lhsT=krT_[:, :], rhs=QR4[:, tt, :, b], start=True, stop=True)
            S.task("pe", f, reads=[cT_, w_uk, qaT, krT_, QR4], writes=[yb, snr])

        def st_Q(n):
            yb, ys_ = PB[2 + n % 2], ysq[n % 3]
            S.task("act", lambda e: e.activation(out=ys_[:, :], in_=yb.ap[:, :], func=AF.Square), reads=[yb], writes=[ys_])

        def st_R(n):
            b, t, tl, qg, g = info(n)
            ys_, ss_ = ysq[n % 3], ssq[qg % 2]
            S.task("dve", lambda e: e.tensor_reduce(out=ss_[:, tl * 8:(tl + 1) * 8], in_=ys_[:, :].rearrange("p (h d) -> p h d", h=8),
                                                    axis=AX.X, op=ALU.add), reads=[ys_], writes=[ss_])
            if tl == 31:
                quarter_end(qg, n + SK[3] + 1)

        def quarter_end(qg, s0):
            b, qtr = qg // 4, qg % 4
            snr, ss_, pts, cn = SNR[qg % 2], ssq[qg % 2], pTs[qg % 2], c_nat[b % 2]
            defer(s0 + 1, lambda: S.task("act", lambda e: e.activation(out=sq_t[:, :], in_=ss_[:, :], func=AF.Sqrt, bias=cols[:, C_EPS:C_EPS + 1], scale=1.0 / 64),
                                         reads=[ss_, cols], writes=[sq_t]))
            defer(s0 + 2, lambda: S.task("dve", lambda e: e.reciprocal(out=rsq[:, :], in_=sq_t[:, :]), reads=[sq_t], writes=[rsq]))
            defer(s0 + 3, lambda: S.task("dve", lambda e: e.tensor_tensor(out=rsq[:, :], in0=snr.ap[:, 0:256], in1=rsq[:, :], op=ALU.mult), reads=[snr, rsq], writes=[rsq]))
            defer(s0 + 4, lambda: S.task("dve", lambda e: e.tensor_tensor(out=rsq[:, :], in0=snr.ap[:, 256:512], in1=rsq[:, :], op=ALU.add), reads=[snr, rsq], writes=[rsq]))
            defer(s0 + 5, lambda: S.task("act", lambda e: e.activation(out=pts[:, :], in_=rsq[:, :], func=AF.Exp, scale=SCALE), reads=[rsq], writes=[pts]))

            def pv():
                def f(e):
                    ins = None
                    for tl in range(32):
                        t = qtr * 32 + tl
                        first = (qtr == 0 and tl == 0)
                        e.matmul(OB.ap[:8, 0:128], lhsT=pts[:, tl * 8:(tl + 1) * 8], rhs=cn[:, t * 128:(t + 1) * 128], start=first, stop=False)
                        ins = e.matmul(DB.ap[:8, 0:1], lhsT=pts[:, tl * 8:(tl + 1) * 8], rhs=ones_b[:, 0:1], start=first, stop=False)
                    return ins
                S.task("pe", f, reads=[pts, cn, ones_b], writes=[OB, DB])
            defer(s0 + 7, pv)
            if qtr == 3:
                def fin():
                    def f(e):
                        e.matmul(OB.ap[:8, 0:128], lhsT=Pmask[:, b, :], rhs=cown_b[:, :], start=False, stop=True)
                        return e.matmul(DB.ap[:8, 0:1], lhsT=Pmask[:, b, :], rhs=ones_b[:NS, 0:1], start=False, stop=True)
                    S.task("pe", f, reads=[Pmask, cown_b, ones_b], writes=[OB, DB])
                defer(s0 + 7, fin)
                defer(s0 + 9, lambda: S.task("dve", lambda e: e.reciprocal(out=rd_s[:, 0:1], in_=DB.ap[:8, 0:1]), reads=[DB], writes=[rd_s]))
                defer(s0 + 10, lambda: S.task("dve", lambda e: e.tensor_scalar(out=OL[:, b, :], in0=OB.ap[:8, 0:128], scalar1=rd_s[:, 0:1], scalar2=None, op0=ALU.mult),
                                              reads=[OB, rd_s], writes=[OL]))

        SK = [int(v) for v in os.environ.get('KSKEW', '1,2,3,4').split(',')]
        gather(0)
        for s in range(NTOT + 24):
            if s < NTOT:
                st_T(s)
            if SK[1] > SK[0] and 0 <= s - SK[1] < NTOT:
                st_Y(s - SK[1])
            if 0 <= s - SK[0] < NTOT:
                st_C(s - SK[0])
            if SK[1] <= SK[0] and 0 <= s - SK[1] < NTOT:
                st_Y(s - SK[1])
            if 0 <= s - SK[2] < NTOT:
                st_Q(s - SK[2])
            if 0 <= s - SK[3] < NTOT:
                st_R(s - SK[3])
            for fn in deferred.pop(s, []):
                fn()
            if s < NTOT and s % 128 == 12 and s // 128 + 1 < NS:
                gather(s // 128 + 1)
        assert not deferred, sorted(deferred)

        def f(e):
            ins = None
            for b in range(NS):
                ins = e.transpose(out=pbf(2)[:, b * 8:(b + 1) * 8], in_=OL[:, b, :], identity=ident_b[:8, :8])
            return ins
        S.task("pe", f, reads=[OL, ident_b], writes=[PB[2]])
        S.task("act", lambda e: e.activation(out=OLT[:, :, :].rearrange("p b h -> p (b h)"), in_=pbf(2)[:, 0:8 * NS], func=AF.Copy), reads=[PB[2]], writes=[OLT])

        def f(e):
            ins = None
            for h in range(8):
                ins = e.matmul(PB[3].ap[:NS, h * 64:(h + 1) * 64], lhsT=OLT[:, :, h], rhs=w_uv[:, h * 64:(h + 1) * 64], start=True, stop=True)
            return ins
        S.task("pe", f, reads=[OLT, w_uv], writes=[PB[3]])
        S.task("act", lambda e: e.activation(out=attn_s[:, :], in_=PB[3].ap[:NS, :], func=AF.Copy), reads=[PB[3]], writes=[attn_s])

        def f(e):
            ins = None
            for c in range(4):
                ins = e.transpose(out=pbf(2)[:, 512 + c * NS:512 + (c + 1) * NS], in_=attn_s[:, c * 128:(c + 1) * 128], identity=ident_b[:NS, :NS])
            return ins
        S.task("pe", f, reads=[attn_s, ident_b], writes=[PB[2]])
        S.task("act", lambda e: e.activation(out=attnT_s[:, :, :].rearrange("p c b -> p (c b)"), in_=pbf(2)[:, 512:512 + 4 * NS], func=AF.Copy),
               reads=[PB[2]], writes=[attnT_s])

    sample_attention()
    S.barrier()

    phase("U")
    wog = sb("wog", [128, 8, DM], BF16)
    w_proj = sb("w_proj", [128, 2, DM], BF16)
    w_fo = sb("w_fo", [128, 22, 512], BF16)
    GW = 256
    NG = DFF // GW
    wfi = [sb("wfi%d" % i, [128, 8, 2, GW], BF16) for i in range(2)]
    h_sb = [sb("h_sb%d" % i, [128, DM]) for i in range(4)] + [h_s]
    hn_b = sb("hn_b", [128, DM], BF16)
    NBX = 512 + NS
    hn2T = sb("hn2T", [128, 8, NBX], BF16)
    gTt = sb("gTt", [128, 22, NBX], BF16)
    a_sb = [sb("a_sb%d" % i, [128, 2, 2 + 512]) for i in range(2)]
    a_s = [sb("a_s%d" % i, [128, 2, NS]) for i in range(2)]
    cv = [sb("cv%d" % i, [128, 2, NBX]) for i in range(2)]
    sil = [sb("sil%d" % i, [128, NBX]) for i in range(2)]
    carry = sb("carry", [128, 44, 2])
    histT = sb("histT", [128, 44, 2 * NS])
    hist_pc = sb("hist_pc", [2 * NS, 1408])
    a_tok = [sb("a_tok%d" % i, [NS, 2, GW]) for i in range(2)]
    a_tok2 = [sb("a_tok2%d" % i, [2, 2, GW]) for i in range(2)]
    p_t = [sb("p_t%d" % i, [128, 256]) for i in range(2)]
    pTt = sb("pTt", [128, 2, 128], BF16)
    h3T = sb("h3T", [128, 8, 128], BF16)
    gate_f = sb("gate_f", [128, DM])
    e_f = [sb("e_f%d" % i, [128, DM]) for i in range(2)]

    def norm_transpose(hs, np_, stt, gcol, dstT, c0):
        S.task("act", lambda e: e.activation(out=junk[:np_, :], in_=hs[:np_, :], func=AF.Square, accum_out=stt[:np_, 0:1]), reads=[hs], writes=[junk, stt])
        S.task("dve", lambda e: e.tensor_scalar(out=stt[:np_, 1:2], in0=stt[:np_, 0:1], scalar1=1.0 / DM, scalar2=None, op0=ALU.mult), reads=[stt], writes=[stt])
        rstd_chain(stt, stt[:np_, 1:2], stt, stt[:np_, 2:3], stt, stt[:np_, 3:4], np_)
        S.task("act", lambda e: e.activation(out=hn_b[:np_, :], in_=hs[:np_, :], func=AF.Identity, scale=stt[:np_, 2:3]), reads=[hs, stt], writes=[hn_b])

        def f(e):
            ins = None
            for c in range(8):
                ins = e.transpose(out=pbf(7)[:, c * 128:c * 128 + np_], in_=hn_b[:np_, c * 128:(c + 1) * 128], identity=ident_b[:np_, :np_])
            return ins
        S.task("pe", f, reads=[hn_b, ident_b], writes=[PB[7]])
        S.task("dve", lambda e: e.tensor_tensor(out=dstT[:, :, c0:c0 + np_], in0=pbf(7)[:, :].rearrange("p (c t) -> p c t", c=8)[:, :, :np_],
                                                in1=cols[:, gcol:gcol + 8].unsqueeze(2).broadcast_to([128, 8, np_]), op=ALU.mult),
               reads=[PB[7], cols], writes=[dstT])

    def post_init():
        ld("pool", wog, wog[:, :, :], D["w_o"].rearrange("(c p) n -> p c n", p=128))
        ld("pool", w_proj, w_proj[:, :, :], D["w_proj"].rearrange("(c p) n -> p c n", p=128))
        S.task("pool", lambda e: e.memset(carry[:, :, :], 0.0), writes=[carry])
        S.dma("sp", lambda e: e.dma_start(out=D["conv_s"][:, 0, :], in_=D["hist"].rearrange("(b k) f -> b k f", k=2)[:, 1, :]), hist_pc, final=True)
        for pc in range(4):
            ld("sp", hist_pc, hist_pc[:, :], D["hist"][:, pc * 1408:(pc + 1) * 1408])

            def f(e, pc=pc):
                ins = None
                for k in range(11):
                    ins = e.transpose(out=PB[7].ap[:, k * 32:(k + 1) * 32], in_=hist_pc[:, k * 128:(k + 1) * 128], identity=ident_f[:2 * NS, :2 * NS])
                return ins
            S.task("pe", f, reads=[hist_pc, ident_f], writes=[PB[7]])
            S.task("act", lambda e, pc=pc: e.activation(out=histT[:, pc * 11:(pc + 1) * 11, :].rearrange("p c k -> p (c k)"), in_=PB[7].ap[:, 0:11 * 32], func=AF.Copy),
                   reads=[PB[7]], writes=[histT])
        xt = x_t[0]
        ld("sp", xt, xt[:NS, :], D["xs"][:, :])

        def f(e):
            ins = None
            for nb in range(2):
                chunks = [(attnT_s[:, c, :], wog[:, c, nb * 512:(nb + 1) * 512]) for c in range(4)]
                chunks += [(sgT_s[:, c, :], wog[:, 4 + c, nb * 512:(nb + 1) * 512]) for c in range(4)]
                for k, (l, r) in enumerate(chunks):
                    ins = e.matmul(PB[nb].ap[:NS, :], lhsT=l, rhs=r, start=(k == 0), stop=(k == 7))
            return ins
        S.task("pe", f, reads=[attnT_s, sgT_s, wog], writes=[PB[0], PB[1]])
        for nb in range(2):
            S.task("dve", lambda e, nb=nb: e.tensor_tensor(out=h_s[:, nb * 512:(nb + 1) * 512], in0=PB[nb].ap[:NS, :], in1=xt[:NS, nb * 512:(nb + 1) * 512], op=ALU.add),
                   reads=[PB[nb], xt], writes=[h_s])
        ld("pool", wog, wog[:, :, :], D["w_gate"].rearrange("(c p) n -> p c n", p=128))

    def post_tile_in(blk, ti):
        i = blk * 4 + ti
        hs = h_sb[ti]
        ld("sp", hs, hs[:, :], h_scr[i * 128:(i + 1) * 128, :], reads=[h_dr[i]])
        norm_transpose(hs, 128, stp[ti % 2], C_FNW, hn2T, ti * 128)

    def ffn_block(blk):
        last = (blk == NQB - 1)
        UPG = GW // 128
        NU = 22
        nb_ = NBX if last else 512

        def geo(n):
            g, j = n // UPG, n % UPG
            par = n % 2
            ab = (PB[4], PB[5]) if par == 0 else (PB[6], PB[7])
            return g, j, wfi[g % 2], ab, a_sb[par], a_s[par], cv[par], sil[par]

        def st_M(n):
            g, j, wt, ab, asb, ass, cvt, sl = geo(n)
            if j == 0:
                for half in range(2):
                    ld("pool", wt, wt[:, :, half, :], D["w_ff_in"][:, half * DFF + g * GW:half * DFF + (g + 1) * GW].rearrange("(c p) n -> p c n", p=128))
                if last:
                    at, at2 = a_tok[g % 2], a_tok2[g % 2]

                    def f(e):
                        ins = None
                        for half in range(2):
                            for c in range(8):
                                ins = e.matmul(PB[2].ap[:NS, half * GW:(half + 1) * GW], lhsT=hn2T[:, c, 512:512 + NS], rhs=wt[:, c, half, :], start=(c == 0), stop=(c == 7))
                        for half in range(2):
                            for c in range(8):
                                ins = e.matmul(PB[3].ap[:2, half * GW:(half + 1) * GW], lhsT=hn2T[:, c, 510:512], rhs=wt[:, c, half, :], start=(c == 0), stop=(c == 7))
                        return ins
                    S.task("pe", f, reads=[hn2T, wt], writes=[PB[2], PB[3]])
                    S.task("act", lambda e: e.activation(out=at[:, :, :].rearrange("p a b -> p (a b)"), in_=PB[2].ap[:NS, 0:2 * GW], func=AF.Copy), reads=[PB[2]], writes=[at])
                    S.task("act", lambda e: e.activation(out=at2[:, :, :].rearrange("p a b -> p (a b)"), in_=PB[3].ap[:2, 0:2 * GW], func=AF.Copy), reads=[PB[3]], writes=[at2])
                    for half in range(2):
                        S.dma("sp", lambda e, half=half: e.dma_start(out=D["conv_s"][:, 1, half * DFF + g * GW:half * DFF + (g + 1) * GW], in_=at[:, half, :]),
                              at, reads=[at], final=True)
                        S.dma("sp", lambda e, half=half: e.dma_start(out=D["conv_p"][:, half * DFF + g * GW:half * DFF + (g + 1) * GW], in_=at2[:, half, :]),
                              at2, reads=[at2], final=True)

            def f(e):
                ins = None
                for half in range(2):
                    for c in range(8):
                        ins = e.matmul(ab[half].ap[:, :], lhsT=wt[:, c, half, j * 128:(j + 1) * 128], rhs=hn2T[:, c, 0:512], start=(c == 0), stop=(c == 7))
                return ins
            S.task("pe", f, reads=[wt, hn2T], writes=[ab[0], ab[1]])
            if last:
                def f2(e):
                    ins = None
                    for half in range(2):
                        for c in range(8):
                            ins = e.matmul(PB[n % 2].ap[:, half * NS:(half + 1) * NS], lhsT=wt[:, c, half, j * 128:(j + 1) * 128], rhs=hn2T[:, c, 512:512 + NS],
                                           start=(c == 0), stop=(c == 7))
                    return ins
                S.task("pe", f2, reads=[wt, hn2T], writes=[PB[n % 2]])

        def st_E(n):
            g, j, wt, ab, asb, ass, cvt, sl = geo(n)
            for half in range(2):
                ch = half * 22 + n
                S.task("act", lambda e, half=half: e.activation(out=asb[:, half, 2:514], in_=ab[half].ap[:, :], func=AF.Copy), reads=[ab[half]], writes=[asb])
                S.task("pool", lambda e, half=half, ch=ch: e.tensor_copy(out=asb[:, half, 0:2], in_=carry[:, ch, :]), reads=[carry], writes=[asb])
                S.task("pool", lambda e, half=half, ch=ch: e.tensor_copy(out=carry[:, ch, :], in_=asb[:, half, 512:514]), reads=[asb], writes=[carry])
            if last:
                S.task("act", lambda e: e.activation(out=ass[:, :, :].rearrange("p a b -> p (a b)"), in_=PB[n % 2].ap[:, 0:2 * NS], func=AF.Copy),
                       reads=[PB[n % 2]], writes=[ass])

        def st_V(n):
            g, j, wt, ab, asb, ass, cvt, sl = geo(n)
            W = []
            for half in range(2):
                ch = half * 22 + n
                W.append(tuple(cols[:, C_CW + k * 44 + ch:C_CW + k * 44 + ch + 1] for k in range(3)) + (cols[:, C_CB + ch:C_CB + ch + 1],))
            rd = [asb, cols]
            for step in range(3):
                for half in range(2):
                    w0, w1, w2, cb = W[half]
                    if step == 0:
                        fn = lambda e, half=half, w2=w2, cb=cb: e.tensor_scalar(out=cvt[:, half, 0:512], in0=asb[:, half, 2:514], scalar1=w2, scalar2=cb, op0=ALU.mult, op1=ALU.add)
                    elif step == 1:
                        fn = lambda e, half=half, w1=w1: e.scalar_tensor_tensor(out=cvt[:, half, 0:512], in0=asb[:, half, 1:513], scalar=w1, in1=cvt[:, half, 0:512], op0=ALU.mult, op1=ALU.add)
                    else:
                        fn = lambda e, half=half, w0=w0: e.scalar_tensor_tensor(out=cvt[:, half, 0:512], in0=asb[:, half, 0:512], scalar=w0, in1=cvt[:, half, 0:512], op0=ALU.mult, op1=ALU.add)
                    S.task("dve", fn, reads=rd + ([cvt] if step else []), writes=[cvt], nosync=(step > 0))
            if last:
                for half in range(2):
                    ch = half * 22 + n
                    w0, w1, w2, cb = W[half]
                    hv = histT[:, ch, :].rearrange("p (b k) -> p k b", k=2)
                    S.chain("dve", [
                        lambda e, half=half, w2=w2, cb=cb: e.tensor_scalar(out=cvt[:, half, 512:NBX], in0=ass[:, half, :], scalar1=w2, scalar2=cb, op0=ALU.mult, op1=ALU.add),
                        lambda e, half=half, w1=w1, hv=hv: e.scalar_tensor_tensor(out=cvt[:, half, 512:NBX], in0=hv[:, 1, :], scalar=w1, in1=cvt[:, half, 512:NBX], op0=ALU.mult, op1=ALU.add),
                        lambda e, half=half, w0=w0, hv=hv: e.scalar_tensor_tensor(out=cvt[:, half, 512:NBX], in0=hv[:, 0, :], scalar=w0, in1=cvt[:, half, 512:NBX], op0=ALU.mult, op1=ALU.add),
                    ], reads=[ass, cols, histT, cvt], writes=[cvt])

        def st_L(n):
            g, j, wt, ab, asb, ass, cvt, sl = geo(n)
            S.task("act", lambda e: e.activation(out=sl[:, 0:nb_], in_=cvt[:, 0, 0:nb_], func=AF.Silu), reads=[cvt], writes=[sl])

        def st_U(n):
            g, j, wt, ab, asb, ass, cvt, sl = geo(n)
            S.task("dve", lambda e: e.tensor_tensor(out=gTt[:, n, 0:nb_], in0=sl[:, 0:nb_], in1=cvt[:, 1, 0:nb_], op=ALU.mult), reads=[sl, cvt], writes=[gTt])

        for s in range(NU + 4):
            if s < NU:
                st_M(s)
            if 0 <= s - 1 < NU:
                st_E(s - 1)
            if 0 <= s - 4 < NU:
                st_U(s - 4)
            if 0 <= s - 2 < NU:
                st_V(s - 2)
            if 0 <= s - 3 < NU:
                st_L(s - 3)

    def ffn_out(blk):
        tl_list = [(ti, 128, ti * 128, h_sb[ti]) for ti in range(4)]
        if blk == NQB - 1:
            tl_list.append((4, NS, 512, h_s))
        for nb in range(2):
            for c4 in range(0, 22, 4):
                n = min(4, 22 - c4)
                ld("pool", w_fo, w_fo[:, c4:c4 + n, :], D["w_ff_out"][c4 * 128:(c4 + n) * 128, nb * 512:(nb + 1) * 512].rearrange("(c p) n -> p c n", p=128))
            for k, (ti, np_, c0, hs) in enumerate(tl_list):
                bank = PB[2 + k % 2]

                def f(e, bank=bank, np_=np_, c0=c0):
                    ins = None
                    for fc in range(22):
                        ins = e.matmul(bank.ap[:np_, :], lhsT=gTt[:, fc, c0:c0 + np_], rhs=w_fo[:, fc, :], start=(fc == 0), stop=(fc == 21))
                    return ins
                S.task("pe", f, reads=[gTt, w_fo], writes=[bank])
                S.task("dve", lambda e, bank=bank, np_=np_, hs=hs, nb=nb: e.tensor_tensor(out=hs[:np_, nb * 512:(nb + 1) * 512], in0=bank.ap[:np_, :],
                                                                                         in1=hs[:np_, nb * 512:(nb + 1) * 512], op=ALU.add),
                       reads=[bank], writes=[hs])

    def post_tile_out(blk, ti, sample=False):
        np_ = NS if sample else 128
        hs = h_s if sample else h_sb[ti]
        stt = stp[ti % 2]
        pt_ = p_t[ti % 2]
        yt = e_f[ti % 2]
        src_p = D["ps"][:, :] if sample else D["pp"][(blk * 4 + ti) * 128:(blk * 4 + ti + 1) * 128, :]
        ld("sp", pt_, pt_[:np_, :], src_p)
        norm_transpose(hs, np_, stt, C_PNW, h3T, 0)

        def f(e):
            ins = None
            for nb in range(2):
                for c in range(8):
                    ins = e.matmul(PB[nb].ap[:np_, :], lhsT=h3T[:, c, :np_], rhs=wog[:, c, nb * 512:(nb + 1) * 512], start=(c == 0), stop=(c == 7))
            return ins
        S.task("pe", f, reads=[h3T, wog], writes=[PB[0], PB[1]])
        for nb in range(2):
            S.task("act", lambda e, nb=nb: e.activation(out=gate_f[:np_, nb * 512:(nb + 1) * 512], in_=PB[nb].ap[:np_, :], func=AF.Sigmoid), reads=[PB[nb]], writes=[gate_f])

        def f(e):
            ins = None
            for c in range(2):
                ins = e.transpose(out=PB[6].ap[:, c * 128:c * 128 + np_], in_=pt_[:np_, c * 128:(c + 1) * 128], identity=ident_f[:np_, :np_])
            return ins
        S.task("pe", f, reads=[pt_, ident_f], writes=[PB[6]])
        S.task("act", lambda e: e.activation(out=pTt[:, :, :np_], in_=PB[6].ap[:, 0:256].rearrange("p (c t) -> p c t", c=2)[:, :, :np_], func=AF.Copy), reads=[PB[6]], writes=[pTt])

        def f(e):
            ins = None
            for nb in range(2):
                for c in range(2):
                    ins = e.matmul(PB[4 + nb].ap[:np_, :], lhsT=pTt[:, c, :np_], rhs=w_proj[:, c, nb * 512:(nb + 1) * 512], start=(c == 0), stop=(c == 1))
            return ins
        S.task("pe", f, reads=[pTt, w_proj], writes=[PB[4], PB[5]])
        for nb in range(2):
            S.task("act", lambda e, nb=nb: e.activation(out=junk[:np_, 0:512], in_=PB[4 + nb].ap[:np_, :], func=AF.Square, accum_out=stt[:np_, 4 + nb:5 + nb]),
                   reads=[PB[4 + nb]], writes=[junk, stt])

        S.chain("dve", [
            lambda e: e.tensor_tensor(out=stt[:np_, 6:7], in0=stt[:np_, 4:5], in1=stt[:np_, 5:6], op=ALU.add),
            lambda e: e.tensor_scalar(out=stt[:np_, 6:7], in0=stt[:np_, 6:7], scalar1=1.0 / DM, scalar2=None, op0=ALU.mult),
        ], reads=[stt], writes=[stt])
        rstd_chain(stt, stt[:np_, 6:7], stt, stt[:np_, 7:8], stt, stt[:np_, 8:9], np_)
        for nb in range(2):
            sl = slice(nb * 512, (nb + 1) * 512)
            S.chain("dve", [
                lambda e, nb=nb, sl=sl: e.scalar_tensor_tensor(out=yt[:np_, sl], in0=PB[4 + nb].ap[:np_, :], scalar=stt[:np_, 7:8],
                                                               in1=rows[:np_, R_PPW + nb * 512:R_PPW + (nb + 1) * 512], op0=ALU.mult, op1=ALU.mult),
                lambda e, sl=sl: e.tensor_tensor(out=yt[:np_, sl], in0=yt[:np_, sl], in1=gate_f[:np_, sl], op=ALU.mult),
                lambda e, sl=sl: e.tensor_tensor(out=yt[:np_, sl], in0=yt[:np_, sl], in1=hs[:np_, sl], op=ALU.add),
            ], reads=[PB[4 + nb], stt, rows, gate_f, hs], writes=[yt], nosync=True)
        dst = D["y_s"][:, :] if sample else D["y_p"][(blk * 4 + ti) * 128:(blk * 4 + ti + 1) * 128, :]
        S.dma("sp", lambda e: e.dma_start(out=dst, in_=yt[:np_, :]), yt, reads=[yt], final=True)

    post_init()
    for blk in range(NQB):
        for ti in range(4):
            post_tile_in(blk, ti)
        if blk == NQB - 1:
            norm_transpose(h_s, NS, stp[0], C_FNW, hn2T, 512)
        ffn_block(blk)
        ffn_out(blk)
        for ti in range(4):
            post_tile_out(blk, ti)
        if blk == NQB - 1:
            post_tile_out(blk, 0, sample=True)

    S.check_deadlock()
    with nc.Block() as block:
        @block.tensor
        def _(e):
            S.replay("pe", e)

        @block.scalar
        def _(e):
            S.replay("act", e)

        @block.vector
        def _(e):
            S.replay("dve", e)

        @block.gpsimd
        def _(e):
            S.replay("pool", e)

        @block.sync
        def _(e):
            S.replay("sp", e)
    print("SBUF peak bytes/partition: G=%d U=%d sems=%d" % (peak["G"], peak["U"], S.nsem))
    es.close()
    return nc


def _host_consts(inp):
    f32 = np.float32
    cols = np.zeros((128, NCOL), f32)
    cols[:, C_ANW:C_ANW + 8] = inp["attn_norm_w"][0].reshape(8, 128).T
    cols[:, C_QNW:C_QNW + 2] = inp["q_norm_w"][0].reshape(2, 128).T
    cols[:, C_FNW:C_FNW + 8] = inp["ffn_norm_w"][0].reshape(8, 128).T
    cols[:, C_PNW:C_PNW + 8] = inp["ple_norm_w"][0].reshape(8, 128).T
    cw = inp["conv_w"][0]
    for k in range(3):
        cols[:, C_CW + k * 44:C_CW + (k + 1) * 44] = cw[k].reshape(44, 128).T
    cols[:, C_CB:C_CB + 44] = inp["conv_b"][0].reshape(44, 128).T
    cols[:, C_BST:C_BST + 8] = inp["b_s"][0].T
    cols[0:64, C_KNW] = inp["k_nope_norm_w"][0]
    for a in range(4):
        cols[a * 32:(a + 1) * 32, C_BLK + a] = 1.0
    cols[:, C_EPS] = EPS
    rows = np.zeros((1, NROW), f32)
    rows[0, R_KVW:R_KVW + 128] = inp["kv_norm_w"][0]
    rows[0, R_KRW:R_KRW + 32] = inp["k_rope_norm_w"][0]
    rows[0, R_QHW:R_QHW + 768] = np.tile(np.concatenate([inp["q_nope_norm_w"][0], inp["q_rope_norm_w"][0]]), 8)
    rows[0, R_KNW:R_KNW + 512] = np.tile(inp["k_nope_norm_w"][0], 8)
    rows[0, R_PPW:R_PPW + 1024] = inp["ple_post_norm_w"][0]
    rows[0, R_INVD:R_INVD + 3] = [1.0 / 256, 1.0 / 128, 1.0 / 32]
    rows[0, R_WS00:R_WS00 + 8] = inp["w_s"][0][:, 0, 0]
    rows[0, R_BS0:R_BS0 + 8] = inp["b_s"][0][:, 0]
    rows[0, R_INVH:R_INVH + 8] = 1.0 / 64
    rows[0, R_INVH + 8:R_INVH + 16] = 1.0 / 32
    inv = (10000.0 ** (-np.arange(16, dtype=np.float32) * (2.0 / 32))).astype(f32)
    pos = np.arange(T, dtype=f32)
    ang = pos[:, None] * inv[None, :]
    angs = np.full((NS, 1), 16384.0, f32) * inv[None, :]
    consts = {
        "cols": cols, "rows": rows, "ident": np.eye(128, dtype=f32),
        "tri": np.triu(np.ones((128, 128), f32)),
        "cosp": np.cos(ang).astype(f32), "sinp": np.sin(ang).astype(f32),
        "coss": np.cos(angs).astype(f32), "sins": np.sin(angs).astype(f32),
        "eye16": np.eye(NS, dtype=f32),
    }
    return consts


_NC_CACHE = {}


def kernel(**inp):
    inp = {k: np.asarray(v) for k, v in inp.items()}
    if "nc" not in _NC_CACHE:
        _NC_CACHE["nc"] = build_program()
    nc = _NC_CACHE["nc"]
    consts = _host_consts(inp)
    shared = {
        "cache_c": np.ascontiguousarray(inp["cache_ckv"][0].reshape(NPOOL, 128 * 128)),
        "cache_r": np.ascontiguousarray(inp["cache_krope"][0].reshape(NPOOL, 128 * 32)),
        "w_in": inp["w_in"][0], "w_uq": inp["w_uq"][0],
        "w_uk": np.ascontiguousarray(inp["w_uk"][0].reshape(128, 512)),
        "w_uv": np.ascontiguousarray(inp["w_uv"][0].reshape(128, 512)),
        "w_s": inp["w_s"][0], "w_o": inp["w_o"][0], "w_ff_in": inp["w_ff_in"][0], "w_ff_out": inp["w_ff_out"][0],
        "w_gate": inp["w_ple_gate"][0], "w_proj": inp["w_ple_proj"][0],
    }
    shared.update(consts)
    in_maps = []
    for c in range(NCORES):
        m = dict(shared)
        m["xp"] = np.ascontiguousarray(inp["x_prompt"][c])
        m["pp"] = np.ascontiguousarray(inp["p_prompt"][0, c])
        m["xs"] = np.ascontiguousarray(inp["x_sample"][c * NS:(c + 1) * NS, 0])
        m["ps"] = np.ascontiguousarray(inp["p_sample"][0, c * NS:(c + 1) * NS, 0])
        m["pt"] = np.ascontiguousarray(inp["page_table"][c * NS:(c + 1) * NS].T.astype(np.int32))
        m["hist"] = np.ascontiguousarray(inp["state_conv"][0, c * NS:(c + 1) * NS].reshape(NS * 2, 2 * DFF))
        in_maps.append(m)
    res = run_bass_kernel_spmd(nc, in_maps, core_ids=list(range(NCORES)))
    R = res.results

    def cat(name):
        return np.concatenate([np.asarray(R[c][name]) for c in range(NCORES)], axis=0)

    y_p = cat("y_p").reshape(8, T, DM)
    y_s = cat("y_s").reshape(128, 1, DM)
    ckv_p = cat("ckv_p").reshape(1, 8, T, 128)
    kr_p = cat("kr_p").reshape(1, 8, T, 32)
    ckv_s = cat("ckv_s").reshape(1, 128, 1, 128)
    kr_s = cat("kr_s").reshape(1, 128, 1, 32)
    v_p = cat("v_p").reshape(1, 8, 128, 512)
    v_s = cat("v_s").reshape(1, 128, 1, 512)
    conv_p = cat("conv_p").reshape(1, 8, 2, 2 * DFF)
    conv_s = cat("conv_s").reshape(1, 128, 2, 2 * DFF)
    return tuple(np.ascontiguousarray(a, dtype=np.float32) for a in (y_p, y_s, ckv_p, kr_p, ckv_s, kr_s, v_p, v_s, conv_p, conv_s))
```

```python
import os
from contextlib import ExitStack
import numpy as np
import concourse.bass as bass
import concourse.mybir as mybir
from concourse.bass_utils import run_bass_kernel_spmd

F32 = mybir.dt.float32
BF16 = mybir.dt.bfloat16
I32 = mybir.dt.int32
AF = mybir.ActivationFunctionType
ALU = mybir.AluOpType
AX = mybir.AxisListType

NCORES = 8
T = 2048
DM = 1024
NS = 16
NPOOL = 20480
DFF = 2816
EPS = 1e-6
SCALE = 96 ** -0.5
NT = T // 128
NQB = 4

C_ANW, C_QNW, C_FNW, C_PNW, C_CW, C_CB, C_BST, C_KNW, C_BLK, C_EPS, NCOL = 0, 8, 10, 18, 26, 158, 202, 210, 211, 215, 216
R_KVW, R_KRW, R_QHW, R_KNW, R_PPW, R_INVD, R_WS00, R_BS0, R_INVH, NROW = 0, 128, 160, 928, 1440, 2464, 2467, 2475, 2483, 2499


class Tile:
    __slots__ = ("ap", "name", "w", "r", "sem", "cnt")

    def __init__(self, ap, name):
        self.ap = ap
        self.name = name
        self.w = None
        self.r = {}
        self.sem = None
        self.cnt = 0

    def __getitem__(self, k):
        return self.ap[k]


class Sched:
    ENGS = ("pe", "act", "dve", "pool", "sp")

    def __init__(self, nc, es):
        self.nc = nc
        self.es = es
        self.streams = {k: [] for k in self.ENGS}
        self.cnt = {k: 0 for k in self.ENGS}
        self.waited = {k: {} for k in self.ENGS}
        self.prog = {k: es.enter_context(nc.semaphore("prog_" + k)) for k in ("pe", "act", "dve", "pool")}
        self.final = {}
        self.nsem = 4
        self.dtiles = {}

    def _deps(self, eng, reads, writes, nosync=False):
        deps = {}

        def add(tok):
            if tok is None:
                return
            s, v = tok
            k = id(s)
            if k not in deps or deps[k][1] < v:
                deps[k] = (s, v)

        own = self.prog.get(eng)
        for t in reads:
            if nosync and t.w is not None and t.w[0] is own:
                continue
            add(t.w)
        for t in writes:
            if t.w is not None and t.w[0] is not own:
                add(t.w)
            for tok in t.r.values():
                if tok[0] is not own:
                    add(tok)
        out = []
        for k, (s, v) in deps.items():
            if eng == "pe" and s is own:
                continue
            if self.waited[eng].get(k, 0) >= v:
                continue
            self.waited[eng][k] = v
            out.append((s, v))
        return out

    def _post(self, tok, reads, writes):
        for t in reads:
            t.r[id(tok[0])] = tok
        for t in writes:
            t.w = tok
            t.r = {}

    def task(self, eng, fn, reads=(), writes=(), nosync=False):
        waits = self._deps(eng, reads, writes, nosync)
        self.cnt[eng] += 1
        tok = (self.prog[eng], self.cnt[eng])
        self.streams[eng].append((waits, fn, tok[0], 1))
        self._post(tok, reads, writes)
        return tok

    def chain(self, eng, fns, reads=(), writes=(), nosync=False):
        rd = list(reads)
        for k, fn in enumerate(fns):
            self.task(eng, fn, reads=rd if k == 0 else rd + list(writes), writes=writes, nosync=(nosync and k > 0))

    def dma(self, eng, fn, sb, reads=(), writes=(), final=False):
        waits = self._deps(eng, reads, writes)
        if sb.sem is None:
            sb.sem = self.es.enter_context(self.nc.semaphore("d_" + sb.name))
            self.nsem += 1
        sb.cnt += 16
        tok = (sb.sem, sb.cnt)
        self.dtiles[id(sb)] = sb
        self.streams[eng].append((waits, fn, tok[0], 16))
        self._post(tok, reads, writes)
        if final:
            self.final[id(sb.sem)] = tok
        return tok

    def barrier(self):
        toks = {}
        for k in ("pe", "act", "dve", "pool"):
            if self.cnt[k] > 0:
                toks[id(self.prog[k])] = (self.prog[k], self.cnt[k])
        for t in self.dtiles.values():
            toks[id(t.sem)] = (t.sem, t.cnt)
        for eng in self.ENGS:
            own = self.prog.get(eng)
            waits = []
            for k, (s, v) in toks.items():
                if s is own:
                    continue
                if self.waited[eng].get(k, 0) >= v:
                    continue
                self.waited[eng][k] = v
                waits.append((s, v))
            self.streams[eng].append((waits, None, None, 0))

    def check_deadlock(self):
        val = {}
        ptr = {k: 0 for k in self.ENGS}
        progress = True
        while progress:
            progress = False
            for k in self.ENGS:
                st = self.streams[k]
                while ptr[k] < len(st):
                    waits, fn, sem, inc = st[ptr[k]]
                    if any(val.get(id(s), 0) < v for s, v in waits):
                        break
                    if fn is not None:
                        val[id(sem)] = val.get(id(sem), 0) + inc
                    ptr[k] += 1
                    progress = True
        stuck = {k: (ptr[k], len(self.streams[k])) for k in self.ENGS if ptr[k] < len(self.streams[k])}
        assert not stuck, "semaphore deadlock: %s" % stuck

    def replay(self, eng, e):
        for waits, fn, sem, inc in self.streams[eng]:
            for s, v in waits:
                e.wait_ge(s, v)
            if fn is None:
                continue
            ins = fn(e)
            ins.then_inc(sem, inc)
        if eng == "sp":
            for s, v in self.final.values():
                e.wait_ge(s, v)


def build_program(npool=NPOOL):
    nc = bass.Bass("TRN2", target_bir_lowering=False)
    es = ExitStack()
    D = {}

    def din(name, shape, dt=F32):
        D[name] = nc.dram_tensor(name, list(shape), dt, kind="ExternalInput").ap()

    def dout(name, shape):
        D[name] = nc.dram_tensor(name, list(shape), F32, kind="ExternalOutput").ap()

    din("xp", [T, DM]); din("pp", [T, 256]); din("xs", [NS, DM]); din("ps", [NS, 256])
    din("cache_c", [npool, 128 * 128]); din("cache_r", [npool, 128 * 32])
    din("pt", [128, NS], I32); din("hist", [NS * 2, 2 * DFF])
    din("w_in", [DM, 1440]); din("w_uq", [256, 768]); din("w_uk", [128, 512]); din("w_uv", [128, 512])
    din("w_s", [8, 128, 128]); din("w_o", [DM, DM]); din("w_ffi_r", [11, 128, 4096]); din("w_ffo_r", [2, 128, 22 * 512])
    din("w_gate", [DM, DM]); din("w_proj", [256, DM])
    din("cols", [128, NCOL]); din("rows", [1, NROW]); din("ident", [128, 128]); din("tri", [128, 128])
    din("cosp", [T, 16]); din("sinp", [T, 16]); din("coss", [NS, 16]); din("sins", [NS, 16])
    din("eye16", [NS, NS])
    dout("y_p", [T, DM]); dout("y_s", [NS, DM]); dout("ckv_p", [T, 128]); dout("kr_p", [T, 32])
    dout("ckv_s", [NS, 128]); dout("kr_s", [NS, 32]); dout("v_p", [128, 512]); dout("v_s", [NS, 512])
    dout("conv_p", [2, 2 * DFF]); dout("conv_s", [NS, 2, 2 * DFF])

    S = Sched(nc, es)
    tiles = {}

    GBYTES = 43392
    UBYTES = 168576
    arenaG = es.enter_context(nc.sbuf_tensor("arenaG", [128, GBYTES // 2], BF16))
    arenaU = es.enter_context(nc.sbuf_tensor("arenaU", [128, UBYTES // 2], BF16))
    apos = {"G": 0, "U": 0}
    cur = ["G"]
    peak = {"G": 0, "U": 0}

    def sb(name, shape, dt=F32):
        esz = 4 if dt in (F32, I32) else 2
        n = 1
        for s_ in shape[1:]:
            n *= s_
        nbytes = (n * esz + 63) // 64 * 64
        a = cur[0]
        base = arenaG if a == "G" else arenaU
        off = apos[a]
        apos[a] += nbytes
        peak[a] = max(peak[a], apos[a])
        lim = GBYTES if a == "G" else UBYTES
        assert apos[a] <= lim, (name, a, apos[a], lim)
        ap = base[:shape[0], off // 2:(off + n * esz) // 2]
        if dt != BF16:
            ap = ap.bitcast(dt)
        if len(shape) == 3:
            ap = ap.rearrange("p (a b) -> p a b", a=shape[1])
        elif len(shape) == 4:
            ap = ap.rearrange("p (a b c) -> p a b c", a=shape[1], b=shape[2])
        t = Tile(ap, name)
        tiles[name] = t
        return t

    def phase(a):
        print("arena peaks so far", peak, "pos", apos)
        cur[0] = a
        if a == "U":
            apos["U"] = 0

    PB = [Tile(es.enter_context(nc.psum_tensor("pb%d" % i, [128, 512], F32)), "pb%d" % i) for i in range(8)]

    def pbf(i):
        return PB[i].ap[:, :].bitcast(BF16)

    phase("G")
    ident_f = sb("ident_f", [128, 128]); ident_b = sb("ident_b", [128, 128], BF16)
    tri_f = sb("tri_f", [128, 128]); tri_b = sb("tri_b", [128, 128], BF16)
    cols = sb("cols", [128, NCOL]); rows = sb("rows", [128, NROW])
    cosp = sb("cosp", [128, NT, 16]); sinp = sb("sinp", [128, NT, 16])
    coss = sb("coss", [NS, 16]); sins = sb("sins", [NS, 16])
    eye16 = sb("eye16", [NS, NS])
    idx = sb("idx", [128, NS], I32)
    ones_b = sb("ones_b", [128, 128], BF16)
    ones_f = sb("ones_f", [128, 64])

    def ld(eng, dst, dst_ap, src_ap, reads=(), **kw):
        S.dma(eng, lambda e: e.dma_start(out=dst_ap, in_=src_ap, **kw), dst, reads=list(reads), writes=[dst])

    ld("sp", ident_f, ident_f[:, :], D["ident"][:, :])
    ld("sp", tri_f, tri_f[:, :], D["tri"][:, :])
    ld("sp", cols, cols[:, :], D["cols"][:, :])
    ld("sp", rows, rows[:, :], D["rows"][0:1, :].partition_broadcast(128))
    ld("sp", cosp, cosp[:, :, :], D["cosp"].rearrange("(n p) j -> p n j", p=128))
    ld("sp", sinp, sinp[:, :, :], D["sinp"].rearrange("(n p) j -> p n j", p=128))
    ld("sp", coss, coss[:, :], D["coss"][:, :])
    ld("sp", sins, sins[:, :], D["sins"][:, :])
    ld("sp", eye16, eye16[:, :], D["eye16"][:, :])
    ld("sp", idx, idx[:, :], D["pt"][:, :])
    S.task("dve", lambda e: e.tensor_copy(out=ident_b[:, :], in_=ident_f[:, :]), reads=[ident_f], writes=[ident_b])
    S.task("dve", lambda e: e.tensor_copy(out=tri_b[:, :], in_=tri_f[:, :]), reads=[tri_f], writes=[tri_b])
    S.task("pool", lambda e: e.memset(ones_b[:, :], 1.0), writes=[ones_b])
    S.task("pool", lambda e: e.memset(ones_f[:, :], 1.0), writes=[ones_f])

    w_uk = sb("w_uk", [128, 512], BF16); w_uv = sb("w_uv", [128, 512], BF16)
    w_uk_f = sb("w_uk_f", [128, 512]); w_ukT = sb("w_ukT", [64, 8, 128], BF16)
    QT_s = sb("QT_s", [96, 8, NS], BF16)
    Qcat_s = sb("Qcat_s", [NS, 8, 96]); Kcat_s = sb("Kcat_s", [NS, 8, 96])
    cown_b = sb("cown_b", [NS, 128], BF16)
    sgT_s = sb("sgT_s", [128, 4, NS], BF16)
    attnT_s = sb("attnT_s", [128, 4, NS], BF16)
    h_s = sb("h_s", [NS, DM])
    x_t = [sb("x_t%d" % i, [128, DM]) for i in range(2)]
    junk = sb("junk", [128, DM], BF16)
    st = [sb("st%d" % i, [128, 64]) for i in range(2)]
    stp = [sb("stp%d" % i, [128, 16]) for i in range(2)]
    ld("pool", w_uk, w_uk[:, :], D["w_uk"][:, :])
    ld("pool", w_uv, w_uv[:, :], D["w_uv"][:, :])
    ld("sp", w_uk_f, w_uk_f[:, :], D["w_uk"][:, :])
    for g4 in range(2):
        def f(e, g4=g4):
            ins = None
            for gg in range(4):
                h = g4 * 4 + gg
                ins = e.transpose(out=PB[1].ap[:64, gg * 128:(gg + 1) * 128], in_=w_uk_f[:, h * 64:(h + 1) * 64], identity=ident_f[:, :])
            return ins
        S.task("pe", f, reads=[w_uk_f, ident_f], writes=[PB[1]])
        S.task("dve", lambda e, g4=g4: e.tensor_copy(
            out=w_ukT[:, g4 * 4:(g4 + 1) * 4, :], in_=PB[1].ap[:64, :].rearrange("p (g t) -> p g t", g=4)),
            reads=[PB[1]], writes=[w_ukT])
    h_scr = nc.dram_tensor("h_scr", [T, DM], F32).ap()
    h_dr = [Tile(None, "h_dr%d" % i) for i in range(NT)]

    def col(c0, n=1):
        return cols[:, c0:c0 + n]

    def rstd_chain(ss_tile, ss_ap, out_tile, out_ap, tmp_tile, tmp_ap, np_):
        S.task("act", lambda e: e.activation(out=tmp_ap, in_=ss_ap, func=AF.Sqrt, bias=cols[:np_, C_EPS:C_EPS + 1], scale=1.0),
               reads=[ss_tile, cols], writes=[tmp_tile])
        S.task("dve", lambda e: e.reciprocal(out=out_ap, in_=tmp_ap), reads=[tmp_tile], writes=[out_tile])

    def rope_apply(dst_tile, dst1, dst2, src_tile, x1, x2, cos_ap, sin_ap, cs_tiles, tmp_tile, tmpv):
        def f(e):
            e.tensor_tensor(out=tmpv[0], in0=x1, in1=cos_ap, op=ALU.mult)
            e.tensor_tensor(out=tmpv[1], in0=x2, in1=sin_ap, op=ALU.mult)
            e.tensor_tensor(out=tmpv[2], in0=x1, in1=sin_ap, op=ALU.mult)
            return e.tensor_tensor(out=tmpv[3], in0=x2, in1=cos_ap, op=ALU.mult)
        S.task("dve", f, reads=[src_tile] + cs_tiles, writes=[tmp_tile])

        def g(e):
            e.tensor_tensor(out=dst1, in0=tmpv[0], in1=tmpv[1], op=ALU.subtract)
            return e.tensor_tensor(out=dst2, in0=tmpv[2], in1=tmpv[3], op=ALU.add)
        S.task("dve", g, reads=[tmp_tile], writes=[dst_tile])

    phase("U")
    w_in = sb("w_in", [128, 8, 1440], BF16); w_uq = sb("w_uq", [128, 2, 768], BF16)
    w_s_f = sb("w_s_f", [128, 8, 128]); wsT = sb("wsT", [128, 8, 128], BF16)
    w_oH = sb("w_oH", [64, 8, DM], BF16)
    w_oS = sb("w_oS", [128, 4, DM], BF16)
    Kop = [sb("Kop%d" % q, [96, 8, 512], BF16) for q in range(NQB)]
    Vaug = [sb("Vaug%d" % q, [128, 4, 8, 65], BF16) for q in range(NQB)]
    attnT = sb("attnT", [64, 8, 512], BF16)
    sgT = sb("sgT", [128, 4, 512], BF16)
    xT = [sb("xT%d" % i, [128, 8, 128], BF16) for i in range(2)]
    zq_b = sb("zq_b", [128, 256], BF16)
    qlT = sb("qlT", [128, 2, 128], BF16)
    q_sq = sb("q_sq", [128, 768])
    q_f = sb("q_f", [128, 8, 96])
    ropet = sb("ropet", [128, 4, 8, 16])
    Qcat_b = sb("Qcat_b", [128, 8, 96], BF16)
    QT = sb("QT", [96, 8, 512], BF16)
    c_f = [sb("c_f%d" % i, [128, 128]) for i in range(2)]
    ckvT = sb("ckvT", [128, 128], BF16)
    kn_f = sb("kn_f", [128, 512])
    Kcat_b = sb("Kcat_b", [128, 8, 96], BF16)
    kr_f = [sb("kr_f%d" % i, [128, 32]) for i in range(2)]
    krt = sb("krt", [128, 6, 16])
    u_f = sb("u_f", [128, 512])
    v_f = [sb("v_f%d" % i, [128, 512]) for i in range(2)]
    v_b = sb("v_b", [128, 512], BF16)
    mixt = sb("mixt", [128, 512])
    sg_b = sb("sg_b", [128, 512], BF16)
    pT = [sb("pT%d" % i, [128, 512], BF16) for i in range(3)]
    rden = mixt; rden_bc = kn_f
    h_t = [sb("h_t0", [128, DM])] * 2

    ld("pool", w_in, w_in[:, :, :], D["w_in"].rearrange("(c p) n -> p c n", p=128))
    ld("pool", w_uq, w_uq[:, :, :], D["w_uq"].rearrange("(c p) n -> p c n", p=128))
    ld("sp", w_s_f, w_s_f[:, :, :], D["w_s"].rearrange("g t s -> t g s"))
    ld("pool", w_oH, w_oH[:, :, :], D["w_o"][0:512, :].rearrange("(h p) n -> p h n", p=64))
    ld("pool", w_oS, w_oS[:, :, :], D["w_o"][512:1024, :].rearrange("(c p) n -> p c n", p=128))
    for g4 in range(2):
        def f(e, g4=g4):
            ins = None
            for gg in range(4):
                g = g4 * 4 + gg
                ins = e.transpose(out=PB[0].ap[:, gg * 128:(gg + 1) * 128], in_=w_s_f[:, g, :], identity=ident_f[:, :])
            return ins
        S.task("pe", f, reads=[w_s_f, ident_f], writes=[PB[0]])
        S.task("dve", lambda e, g4=g4: e.tensor_tensor(
            out=wsT[:, g4 * 4:(g4 + 1) * 4, :], in0=PB[0].ap[:, :].rearrange("p (g t) -> p g t", g=4),
            in1=tri_f[:, :].unsqueeze(1).broadcast_to([128, 4, 128]), op=ALU.mult), reads=[PB[0], tri_f], writes=[wsT])
    for q in range(NQB):
        S.task("pool", lambda e, q=q: e.memset(Vaug[q][:, :, :, :], 1.0), writes=[Vaug[q]])

    def phase_a_tile(i, sample=False):
        np_ = NS if sample else 128
        par = i % 2
        xt, xTt, stt = x_t[par], xT[par], st[par]
        qb, tl = (0, 0) if sample else (i // 4, i % 4)
        src_x = D["xs"][:, :] if sample else D["xp"][i * 128:(i + 1) * 128, :]
        ld("sp", xt, xt[:np_, :], src_x)
        S.task("act", lambda e: e.activation(out=junk[:np_, :], in_=xt[:np_, :], func=AF.Square, accum_out=stt[:np_, 0:1]),
               reads=[xt], writes=[junk, stt])

        def f(e):
            ins = None
            for c in range(8):
                ins = e.transpose(out=PB[c // 4].ap[:, (c % 4) * 128:(c % 4) * 128 + np_], in_=xt[:np_, c * 128:(c + 1) * 128],
                                  identity=ident_f[:np_, :np_])
            return ins
        S.task("pe", f, reads=[xt, ident_f], writes=[PB[0], PB[1]])
        for hh in range(2):
            S.task("dve", lambda e, hh=hh: e.tensor_tensor(
                out=xTt[:, hh * 4:(hh + 1) * 4, :np_], in0=PB[hh].ap[:, :].rearrange("p (c t) -> p c t", c=4)[:, :, :np_],
                in1=cols[:, C_ANW + hh * 4:C_ANW + hh * 4 + 4].unsqueeze(2).broadcast_to([128, 4, np_]), op=ALU.mult),
                reads=[PB[hh], cols], writes=[xTt])

        def f(e):
            ins = None
            for nb, (c0, c1) in enumerate(((0, 512), (512, 1024), (1024, 1440))):
                for c in range(8):
                    ins = e.matmul(PB[2 + nb].ap[:np_, 0:c1 - c0], lhsT=xTt[:, c, :np_], rhs=w_in[:, c, c0:c1], start=(c == 0), stop=(c == 7))
            return ins
        S.task("pe", f, reads=[xTt, w_in], writes=[PB[2], PB[3], PB[4]])
        Z0, Z1, Z2 = PB[2], PB[3], PB[4]
        S.task("act", lambda e: e.activation(out=junk[:np_, 0:256], in_=Z0.ap[:np_, 0:256], func=AF.Square, accum_out=stt[:np_, 1:2]),
               reads=[Z0], writes=[junk, stt])
        S.task("act", lambda e: e.activation(out=junk[:np_, 0:128], in_=Z0.ap[:np_, 256:384], func=AF.Square, accum_out=stt[:np_, 2:3]),
               reads=[Z0], writes=[junk, stt])
        S.task("act", lambda e: e.activation(out=junk[:np_, 0:32], in_=Z0.ap[:np_, 384:416], func=AF.Square, accum_out=stt[:np_, 3:4]),
               reads=[Z0], writes=[junk, stt])
        S.task("dve", lambda e: e.tensor_scalar(out=stt[:np_, 14:15], in0=stt[:np_, 0:1], scalar1=1.0 / DM, scalar2=None, op0=ALU.mult),
               reads=[stt], writes=[stt])
        rstd_chain(stt, stt[:np_, 14:15], stt, stt[:np_, 4:5], stt, stt[:np_, 15:16], np_)
        rx = stt[:np_, 4:5]

        S.chain("dve", [
            lambda e: e.tensor_scalar(out=stt[:np_, 5:8], in0=stt[:np_, 1:4], scalar1=rx, scalar2=rx, op0=ALU.mult, op1=ALU.mult),
            lambda e: e.tensor_tensor(out=stt[:np_, 5:8], in0=stt[:np_, 5:8], in1=rows[:np_, R_INVD:R_INVD + 3], op=ALU.mult),
        ], reads=[stt, rows], writes=[stt])
        rstd_chain(stt, stt[:np_, 5:8], stt, stt[:np_, 8:11], stt, stt[:np_, 16:19], np_)
        S.task("dve", lambda e: e.tensor_scalar(out=stt[:np_, 11:14], in0=stt[:np_, 8:11], scalar1=rx, scalar2=None, op0=ALU.mult),
               reads=[stt], writes=[stt])
        s_q, s_c, s_kr = stt[:np_, 11:12], stt[:np_, 12:13], stt[:np_, 13:14]

        vf = v_f[par]
        S.task("act", lambda e: e.activation(out=u_f[:np_, 0:96], in_=Z0.ap[:np_, 416:512], func=AF.Gelu_apprx_tanh, scale=rx),
               reads=[Z0, stt], writes=[u_f])
        S.task("act", lambda e: e.activation(out=u_f[:np_, 96:512], in_=Z1.ap[:np_, 0:416], func=AF.Gelu_apprx_tanh, scale=rx),
               reads=[Z1, stt], writes=[u_f])
        S.task("act", lambda e: e.activation(out=vf[:np_, 0:96], in_=Z1.ap[:np_, 416:512], func=AF.Gelu_apprx_tanh, scale=rx),
               reads=[Z1, stt], writes=[vf])
        S.task("act", lambda e: e.activation(out=vf[:np_, 96:512], in_=Z2.ap[:np_, 0:416], func=AF.Gelu_apprx_tanh, scale=rx),
               reads=[Z2, stt], writes=[vf])
        if sample:
            S.dma("sp", lambda e: e.dma_start(out=D["v_s"][:, :], in_=vf[:np_, :]), vf, reads=[vf], final=True)
        elif i == NT - 1:
            S.dma("sp", lambda e: e.dma_start(out=D["v_p"][:, :], in_=vf[:, :]), vf, reads=[vf], final=True)

        S.task("dve", lambda e: e.tensor_copy(out=zq_b[:np_, :], in_=Z0.ap[:np_, 0:256]), reads=[Z0], writes=[zq_b])

        def f(e):
            ins = None
            for c in range(2):
                ins = e.transpose(out=pbf(7)[:, c * 128:c * 128 + np_], in_=zq_b[:np_, c * 128:(c + 1) * 128], identity=ident_b[:np_, :np_])
            return ins
        S.task("pe", f, reads=[zq_b, ident_b], writes=[PB[7]])
        S.task("dve", lambda e: e.tensor_tensor(
            out=qlT[:, :, :np_], in0=pbf(7)[:, 0:256].rearrange("p (c t) -> p c t", c=2)[:, :, :np_],
            in1=cols[:, C_QNW:C_QNW + 2].unsqueeze(2).broadcast_to([128, 2, np_]), op=ALU.mult), reads=[PB[7], cols], writes=[qlT])

        def f(e):
            ins = None
            for nb, (c0, c1) in enumerate(((0, 512), (512, 768))):
                for c in range(2):
                    ins = e.matmul(PB[5 + nb].ap[:np_, 0:c1 - c0], lhsT=qlT[:, c, :np_], rhs=w_uq[:, c, c0:c1], start=(c == 0), stop=(c == 1))
            return ins
        S.task("pe", f, reads=[qlT, w_uq], writes=[PB[5], PB[6]])
        qfl = q_f[:np_, :, :].rearrange("p h d -> p (h d)")
        S.task("act", lambda e: e.activation(out=qfl[:, 0:512], in_=PB[5].ap[:np_, :], func=AF.Identity, scale=s_q),
               reads=[PB[5], stt], writes=[q_f])
        S.task("act", lambda e: e.activation(out=qfl[:, 512:768], in_=PB[6].ap[:np_, 0:256], func=AF.Identity, scale=s_q),
               reads=[PB[6], stt], writes=[q_f])
        S.task("act", lambda e: e.activation(out=q_sq[:np_, :], in_=qfl, func=AF.Square), reads=[q_f], writes=[q_sq])

        def f(e):
            qv = q_sq[:np_, :].rearrange("p (h d) -> p h d", h=8)
            e.tensor_reduce(out=stt[:np_, 20:28], in_=qv[:, :, 0:64], axis=AX.X, op=ALU.add)
            return e.tensor_reduce(out=stt[:np_, 28:36], in_=qv[:, :, 64:96], axis=AX.X, op=ALU.add)
        S.chain("dve", [f, lambda e: e.tensor_tensor(out=stt[:np_, 20:36], in0=stt[:np_, 20:36], in1=rows[:np_, R_INVH:R_INVH + 16], op=ALU.mult)],
                reads=[q_sq, rows], writes=[stt])
        S.task("act", lambda e: e.activation(out=q_sq[:np_, 0:16], in_=stt[:np_, 20:36], func=AF.Sqrt, bias=cols[:np_, C_EPS:C_EPS + 1], scale=1.0),
               reads=[stt, cols], writes=[q_sq])
        S.task("dve", lambda e: e.reciprocal(out=stt[:np_, 36:52], in_=q_sq[:np_, 0:16]), reads=[q_sq], writes=[stt])

        def f(e):
            e.tensor_tensor(out=q_f[:np_, :, 0:64], in0=q_f[:np_, :, 0:64], in1=stt[:np_, 36:44].unsqueeze(2).broadcast_to([np_, 8, 64]), op=ALU.mult)
            return e.tensor_tensor(out=q_f[:np_, :, 64:96], in0=q_f[:np_, :, 64:96], in1=stt[:np_, 44:52].unsqueeze(2).broadcast_to([np_, 8, 32]), op=ALU.mult)
        S.chain("dve", [f, lambda e: e.tensor_tensor(out=q_f[:np_, :, :], in0=q_f[:np_, :, :], in1=rows[:np_, R_QHW:R_QHW + 768].rearrange("p (h d) -> p h d", h=8), op=ALU.mult)],
                reads=[stt, rows, q_f], writes=[q_f])
        qdst = Qcat_s if sample else Qcat_b
        S.task("pool", lambda e: e.tensor_copy(out=qdst[:np_, :, 0:64], in_=q_f[:np_, :, 0:64]), reads=[q_f], writes=[qdst])
        if sample:
            cos_ap = coss[:, :].unsqueeze(1).broadcast_to([np_, 8, 16]); sin_ap = sins[:, :].unsqueeze(1).broadcast_to([np_, 8, 16])
            cst = [coss, sins]
        else:
            cos_ap = cosp[:, i, :].unsqueeze(1).broadcast_to([np_, 8, 16]); sin_ap = sinp[:, i, :].unsqueeze(1).broadcast_to([np_, 8, 16])
            cst = [cosp, sinp]
        rope_apply(qdst, qdst[:np_, :, 64:80], qdst[:np_, :, 80:96], q_f, q_f[:np_, :, 64:80], q_f[:np_, :, 80:96],
                   cos_ap, sin_ap, cst, ropet, [ropet[:np_, k, :, :] for k in range(4)])
        if sample:
            S.task("pool", lambda e: e.tensor_copy(out=Qcat_b[:np_, :, :], in_=Qcat_s[:np_, :, :]), reads=[Qcat_s], writes=[Qcat_b])

        def f(e):
            ins = None
            for h in range(8):
                ins = e.transpose(out=pbf(7)[:96, h * 128:h * 128 + np_], in_=Qcat_b[:np_, h, :], identity=ident_b[:np_, :np_])
            return ins
        S.task("pe", f, reads=[Qcat_b, ident_b], writes=[PB[7]])
        if sample:
            S.task("act", lambda e: e.activation(out=QT_s[:, :, :], in_=pbf(7)[:96, :].rearrange("p (h t) -> p h t", h=8)[:, :, :np_], func=AF.Copy),
                   reads=[PB[7]], writes=[QT_s])
        else:
            S.task("act", lambda e: e.activation(out=QT[:, :, tl * 128:(tl + 1) * 128], in_=pbf(7)[:96, :].rearrange("p (h t) -> p h t", h=8), func=AF.Copy),
                   reads=[PB[7]], writes=[QT])

        cf = c_f[par]
        S.task("dve", lambda e: e.scalar_tensor_tensor(out=cf[:np_, :], in0=Z0.ap[:np_, 256:384], scalar=s_c, in1=rows[:np_, R_KVW:R_KVW + 128],
                                                       op0=ALU.mult, op1=ALU.mult), reads=[Z0, stt, rows], writes=[cf])
        dst_c = D["ckv_s"][:, :] if sample else D["ckv_p"][i * 128:(i + 1) * 128, :]
        S.dma("sp", lambda e: e.dma_start(out=dst_c, in_=cf[:np_, :]), cf, reads=[cf], final=True)
        if sample:
            S.task("pool", lambda e: e.tensor_copy(out=cown_b[:, :], in_=cf[:np_, :]), reads=[cf], writes=[cown_b])
        S.task("pe", lambda e: e.transpose(out=PB[0].ap[:, 0:np_], in_=cf[:np_, :], identity=ident_f[:np_, :np_]), reads=[cf, ident_f], writes=[PB[0]])
        S.task("act", lambda e: e.activation(out=ckvT[:, :np_], in_=PB[0].ap[:, 0:np_], func=AF.Copy), reads=[PB[0]], writes=[ckvT])

        def f(e):
            ins = e.matmul(PB[5].ap[:np_, :], lhsT=ckvT[:, :np_], rhs=w_uk[:, :], start=True, stop=True)
            if not sample:
                ins = e.matmul(PB[6].ap[:np_, :], lhsT=ckvT[:, :np_], rhs=w_uv[:, :], start=True, stop=True)
            return ins
        S.task("pe", f, reads=[ckvT, w_uk, w_uv], writes=[PB[5], PB[6]])
        if not sample:
            S.task("act", lambda e: e.activation(out=Vaug[qb][:, tl, :, 0:64], in_=PB[6].ap[:, :].rearrange("p (h d) -> p h d", h=8), func=AF.Copy),
                   reads=[PB[6]], writes=[Vaug[qb]])
        S.task("act", lambda e: e.activation(out=q_sq[:np_, 0:512], in_=PB[5].ap[:np_, :], func=AF.Square), reads=[PB[5]], writes=[q_sq])

        S.chain("dve", [
            lambda e: e.tensor_reduce(out=stt[:np_, 20:28], in_=q_sq[:np_, 0:512].rearrange("p (h d) -> p h d", h=8), axis=AX.X, op=ALU.add),
            lambda e: e.tensor_scalar(out=stt[:np_, 20:28], in0=stt[:np_, 20:28], scalar1=1.0 / 64, scalar2=None, op0=ALU.mult),
        ], reads=[q_sq], writes=[stt])
        S.task("act", lambda e: e.activation(out=q_sq[:np_, 0:8], in_=stt[:np_, 20:28], func=AF.Sqrt, bias=cols[:np_, C_EPS:C_EPS + 1], scale=1.0),
               reads=[stt, cols], writes=[q_sq])
        S.task("dve", lambda e: e.reciprocal(out=stt[:np_, 28:36], in_=q_sq[:np_, 0:8]), reads=[q_sq], writes=[stt])

        S.chain("dve", [
            lambda e: e.tensor_tensor(out=kn_f[:np_, :].rearrange("p (h d) -> p h d", h=8), in0=PB[5].ap[:np_, :].rearrange("p (h d) -> p h d", h=8),
                                      in1=stt[:np_, 28:36].unsqueeze(2).broadcast_to([np_, 8, 64]), op=ALU.mult),
            lambda e: e.tensor_tensor(out=kn_f[:np_, :], in0=kn_f[:np_, :], in1=rows[:np_, R_KNW:R_KNW + 512], op=ALU.mult),
        ], reads=[PB[5], stt, rows], writes=[kn_f], nosync=True)
        kdst = Kcat_s if sample else Kcat_b
        S.task("pool", lambda e: e.tensor_copy(out=kdst[:np_, :, 0:64], in_=kn_f[:np_, :].rearrange("p (h d) -> p h d", h=8)), reads=[kn_f], writes=[kdst])
        krf = kr_f[par]
        S.task("dve", lambda e: e.scalar_tensor_tensor(out=krt[:np_, 0:2, :].rearrange("p a b -> p (a b)"), in0=Z0.ap[:np_, 384:416], scalar=s_kr,
                                                       in1=rows[:np_, R_KRW:R_KRW + 32], op0=ALU.mult, op1=ALU.mult), reads=[Z0, stt, rows], writes=[krt])
        if sample:
            c1, s1, cst1 = coss[:, :], sins[:, :], [coss, sins]
        else:
            c1, s1, cst1 = cosp[:, i, :], sinp[:, i, :], [cosp, sinp]

        def f(e):
            e.tensor_tensor(out=krt[:np_, 2, :], in0=krt[:np_, 0, :], in1=c1, op=ALU.mult)
            e.tensor_tensor(out=krt[:np_, 3, :], in0=krt[:np_, 1, :], in1=s1, op=ALU.mult)
            e.tensor_tensor(out=krt[:np_, 4, :], in0=krt[:np_, 0, :], in1=s1, op=ALU.mult)
            return e.tensor_tensor(out=krt[:np_, 5, :], in0=krt[:np_, 1, :], in1=c1, op=ALU.mult)

        def g(e):
            e.tensor_tensor(out=krf[:np_, 0:16], in0=krt[:np_, 2, :], in1=krt[:np_, 3, :], op=ALU.subtract)
            return e.tensor_tensor(out=krf[:np_, 16:32], in0=krt[:np_, 4, :], in1=krt[:np_, 5, :], op=ALU.add)
        S.chain("dve", [f, g], reads=[krt] + cst1, writes=[krf, krt])
        dst_k = D["kr_s"][:, :] if sample else D["kr_p"][i * 128:(i + 1) * 128, :]
        S.dma("sp", lambda e: e.dma_start(out=dst_k, in_=krf[:np_, :]), krf, reads=[krf], final=True)
        S.task("pool", lambda e: e.tensor_copy(out=kdst[:np_, :, 64:96], in_=krf[:np_, :].unsqueeze(1).broadcast_to([np_, 8, 32])), reads=[krf], writes=[kdst])
        if not sample:
            def f(e):
                ins = None
                for h in range(8):
                    ins = e.transpose(out=pbf(7)[:96, h * 128:(h + 1) * 128], in_=Kcat_b[:, h, :], identity=ident_b[:, :])
                return ins
            S.task("pe", f, reads=[Kcat_b, ident_b], writes=[PB[7]])
            S.task("act", lambda e: e.activation(out=Kop[qb][:, :, tl * 128:(tl + 1) * 128], in_=pbf(7)[:96, :].rearrange("p (h t) -> p h t", h=8), func=AF.Copy),
                   reads=[PB[7]], writes=[Kop[qb]])

        if sample:
            S.chain("dve", [
                lambda e: e.tensor_tensor(out=mixt[:np_, :].rearrange("p (g d) -> p g d", g=8), in0=vf[:np_, :].rearrange("p (g d) -> p g d", g=8),
                                          in1=rows[:np_, R_WS00:R_WS00 + 8].unsqueeze(2).broadcast_to([np_, 8, 64]), op=ALU.mult),
                lambda e: e.tensor_tensor(out=mixt[:np_, :].rearrange("p (g d) -> p g d", g=8), in0=mixt[:np_, :].rearrange("p (g d) -> p g d", g=8),
                                          in1=rows[:np_, R_BS0:R_BS0 + 8].unsqueeze(2).broadcast_to([np_, 8, 64]), op=ALU.add),
                lambda e: e.tensor_tensor(out=sg_b[:np_, :], in0=mixt[:np_, :], in1=u_f[:np_, :], op=ALU.mult),
            ], reads=[vf, rows, u_f], writes=[mixt, sg_b])
        else:
            S.task("pool", lambda e: e.tensor_copy(out=v_b[:, :], in_=vf[:, :]), reads=[vf], writes=[v_b])

            def f(e):
                ins = None
                for g in range(8):
                    ins = e.matmul(PB[6].ap[:, g * 64:(g + 1) * 64], lhsT=wsT[:, g, :], rhs=v_b[:, g * 64:(g + 1) * 64], start=True, stop=True)
                return ins
            S.task("pe", f, reads=[wsT, v_b], writes=[PB[6]])

            S.chain("dve", [
                lambda e: e.tensor_tensor(out=mixt[:, :].rearrange("p (g d) -> p g d", g=8), in0=PB[6].ap[:, :].rearrange("p (g d) -> p g d", g=8),
                                          in1=cols[:, C_BST:C_BST + 8].unsqueeze(2).broadcast_to([128, 8, 64]), op=ALU.add),
                lambda e: e.tensor_tensor(out=sg_b[:, :], in0=mixt[:, :], in1=u_f[:, :], op=ALU.mult),
            ], reads=[PB[6], cols, u_f], writes=[mixt, sg_b], nosync=True)

        def f(e):
            ins = None
            for c in range(4):
                ins = e.transpose(out=pbf(7)[:, c * 128:c * 128 + np_], in_=sg_b[:np_, c * 128:(c + 1) * 128], identity=ident_b[:np_, :np_])
            return ins
        S.task("pe", f, reads=[sg_b, ident_b], writes=[PB[7]])
        if sample:
            S.task("act", lambda e: e.activation(out=sgT_s[:, :, :], in_=pbf(7)[:, 0:512].rearrange("p (c t) -> p c t", c=4)[:, :, :np_], func=AF.Copy),
                   reads=[PB[7]], writes=[sgT_s])
        else:
            S.task("act", lambda e: e.activation(out=sgT[:, :, tl * 128:(tl + 1) * 128], in_=pbf(7)[:, 0:512].rearrange("p (c t) -> p c t", c=4), func=AF.Copy),
                   reads=[PB[7]], writes=[sgT])

    sc_banks = [2, 3, 4, 5]

    def attention_block(Q):
        nk = (Q + 1) * 4
        units = [(h, kj) for h in range(8) for kj in range(nk)]
        NU = len(units)

        def geo(u):
            h, kj = units[u]
            a = kj - 4 * Q
            c0 = 128 * a if a > 0 else 0
            return h, kj, kj // 4, kj % 4, a, c0, PB[sc_banks[u % 4]], pT[u % 3], PB[h % 2]

        def st_S(u):
            h, kj, kq, kt, a, c0, sbk, pt_, ob = geo(u)
            S.task("pe", lambda e: e.matmul(sbk.ap[:, c0:512], lhsT=Kop[kq][:, h, kt * 128:(kt + 1) * 128], rhs=QT[:, h, c0:512], start=True, stop=True),
                   reads=[Kop[kq], QT], writes=[sbk])

        def st_E(u):
            h, kj, kq, kt, a, c0, sbk, pt_, ob = geo(u)
            S.task("act", lambda e: e.activation(out=pt_[:, c0:512], in_=sbk.ap[:, c0:512], func=AF.Exp, scale=SCALE), reads=[sbk], writes=[pt_])
            if a >= 0:
                S.task("pool", lambda e: e.tensor_tensor(out=pt_[:, c0:c0 + 128], in0=pt_[:, c0:c0 + 128], in1=tri_b[:, :], op=ALU.mult),
                       reads=[pt_, tri_b], writes=[pt_])

        def st_V(u):
            h, kj, kq, kt, a, c0, sbk, pt_, ob = geo(u)
            S.task("pe", lambda e: e.matmul(ob.ap[:65, c0:512], lhsT=Vaug[kq][:, kt, h, :], rhs=pt_[:, c0:512], start=(kj == 0), stop=(kj == nk - 1)),
                   reads=[Vaug[kq], pt_], writes=[ob])
            if kj == nk - 1:
                S.task("dve", lambda e: e.reciprocal(out=rden[64:65, :], in_=ob.ap[64:65, :]), reads=[ob], writes=[rden])
                S.task("pe", lambda e: e.matmul(PB[6].ap[:64, :], lhsT=ones_f[64:65, 0:64], rhs=rden[64:65, :], start=True, stop=True),
                       reads=[ones_f, rden], writes=[PB[6]])
                S.task("act", lambda e: e.activation(out=rden_bc[:64, :], in_=PB[6].ap[:64, :], func=AF.Copy), reads=[PB[6]], writes=[rden_bc])
                S.task("dve", lambda e: e.tensor_tensor(out=attnT[:, h, :], in0=ob.ap[:64, :], in1=rden_bc[:64, :], op=ALU.mult),
                       reads=[ob, rden_bc], writes=[attnT])

        for s in range(NU + 2):
            if s < NU:
                st_S(s)
            if 0 <= s - 2 < NU:
                st_V(s - 2)
            if 0 <= s - 1 < NU:
                st_E(s - 1)

    def wo_tile(Q, ti):
        i = Q * 4 + ti
        xt = x_t[ti % 2]
        ht = h_t[ti % 2]
        ld("sp", xt, xt[:, :], D["xp"][i * 128:(i + 1) * 128, :])

        def f(e):
            ins = None
            for nb in range(2):
                chunks = [(attnT[:, h, ti * 128:(ti + 1) * 128], w_oH[:, h, nb * 512:(nb + 1) * 512]) for h in range(8)]
                chunks += [(sgT[:, c, ti * 128:(ti + 1) * 128], w_oS[:, c, nb * 512:(nb + 1) * 512]) for c in range(4)]
                for k, (l, r) in enumerate(chunks):
                    ins = e.matmul(PB[6 + nb].ap[:, :], lhsT=l, rhs=r, start=(k == 0), stop=(k == len(chunks) - 1))
            return ins
        S.task("pe", f, reads=[attnT, sgT, w_oH, w_oS], writes=[PB[6], PB[7]])
        for nb in range(2):
            S.task("dve", lambda e, nb=nb: e.tensor_tensor(out=ht[:, nb * 512:(nb + 1) * 512], in0=PB[6 + nb].ap[:, :], in1=xt[:, nb * 512:(nb + 1) * 512], op=ALU.add),
                   reads=[PB[6 + nb], xt], writes=[ht])
        S.dma("sp", lambda e: e.dma_start(out=h_scr[i * 128:(i + 1) * 128, :], in_=ht[:, :]), ht, reads=[ht], writes=[h_dr[i]])

    phase_a_tile(0, sample=True)
    for Q in range(NQB):
        for tl in range(4):
            phase_a_tile(Q * 4 + tl)
        attention_block(Q)
        for ti in range(4):
            wo_tile(Q, ti)
    S.barrier()

    phase("U")
    c_nat = [sb("c_nat%d" % i, [128, 128 * 128], BF16) for i in range(2)]
    r_nat = [sb("r_nat%d" % i, [128, 128 * 32], BF16) for i in range(2)]
    cT = [sb("cT%d" % i, [128, 128], BF16) for i in range(4)]
    krT = [sb("krT%d" % i, [128, 128], BF16) for i in range(2)]
    ysq = [sb("ysq%d" % i, [128, 512], BF16) for i in range(3)]
    ssq = [sb("ssq%d" % i, [128, 256]) for i in range(2)]
    rsq = sb("rsq", [128, 256]); sq_t = sb("sq_t", [128, 256])
    pTs = [sb("pTs%d" % i, [128, 256], BF16) for i in range(2)]
    qaT = sb("qaT", [128, 8, NS], BF16)
    gT_s = sb("gT_s", [64, 8, NS], BF16)
    QR4 = sb("QR4", [128, 4, 8, NS], BF16)
    qr_rep = sb("qr_rep", [NS, 8, 128], BF16)
    qr_repT = sb("qr_repT", [128, 8, NS])
    s_own = sb("s_own", [NS, 8, 96]); p_own = sb("p_own", [NS, 16])
    Pmask = sb("Pmask", [NS, NS, 8], BF16)
    OL = sb("OL", [8, NS, 128], BF16)
    OLT = sb("OLT", [128, NS, 8], BF16)
    rd_s = sb("rd_s", [8, 2])
    attn_s = sb("attn_s", [NS, 512], BF16)

    def sample_attention():
        S.task("dve", lambda e: e.tensor_scalar(out=gT_s[:, :, :], in0=QT_s[0:64, :, :], scalar1=cols[0:64, C_KNW:C_KNW + 1], scalar2=None, op0=ALU.mult),
               reads=[QT_s, cols], writes=[gT_s])

        def f(e):
            ins = None
            for h in range(8):
                ins = e.matmul(PB[7].ap[:, h * NS:(h + 1) * NS], lhsT=w_ukT[:, h, :], rhs=gT_s[:, h, :], start=True, stop=True)
            return ins
        S.task("pe", f, reads=[w_ukT, gT_s], writes=[PB[7]])
        S.task("act", lambda e: e.activation(out=qaT[:, :, :].rearrange("p h b -> p (h b)"), in_=PB[7].ap[:, 0:8 * NS], func=AF.Copy), reads=[PB[7]], writes=[qaT])
        S.task("dve", lambda e: e.tensor_copy(out=qr_rep[:, :, :].rearrange("p h (a r) -> p h a r", a=4),
                                              in_=Qcat_s[:, :, 64:96].unsqueeze(2).broadcast_to([NS, 8, 4, 32])), reads=[Qcat_s], writes=[qr_rep])

        def f(e):
            ins = None
            for h in range(8):
                ins = e.transpose(out=pbf(6)[:, h * NS:(h + 1) * NS], in_=qr_rep[:, h, :], identity=ident_b[:NS, :NS])
            return ins
        S.task("pe", f, reads=[qr_rep, ident_b], writes=[PB[6]])
        S.task("act", lambda e: e.activation(out=qr_repT[:, :, :].rearrange("p h b -> p (h b)"), in_=pbf(6)[:, 0:8 * NS], func=AF.Copy),
               reads=[PB[6]], writes=[qr_repT])

        def f(e):
            ins = None
            for a in range(4):
                ins = e.tensor_scalar(out=QR4[:, a, :, :], in0=qr_repT[:, :, :], scalar1=cols[:, C_BLK + a:C_BLK + a + 1], scalar2=None, op0=ALU.mult)
            return ins
        S.task("dve", f, reads=[qr_repT, cols], writes=[QR4])

        S.chain("dve", [
            lambda e: e.tensor_tensor(out=s_own[:, :, :], in0=Qcat_s[:, :, :], in1=Kcat_s[:, :, :], op=ALU.mult),
            lambda e: e.tensor_reduce(out=p_own[:, 0:8], in_=s_own[:, :, :], axis=AX.X, op=ALU.add),
        ], reads=[Qcat_s, Kcat_s], writes=[s_own, p_own])
        S.task("act", lambda e: e.activation(out=p_own[:, 8:16], in_=p_own[:, 0:8], func=AF.Exp, scale=SCALE), reads=[p_own], writes=[p_own])
        S.task("dve", lambda e: e.tensor_tensor(out=Pmask[:, :, :], in0=p_own[:, 8:16].unsqueeze(1).broadcast_to([NS, NS, 8]),
                                                in1=eye16[:, :].unsqueeze(2).broadcast_to([NS, NS, 8]), op=ALU.mult), reads=[p_own, eye16], writes=[Pmask])

        OB, DB = PB[6], PB[7]
        cslot = [Tile(pbf(0)[:, k * 128:(k + 1) * 128], "cslot%d" % k) for k in range(8)]
        rslot = [Tile(pbf(1)[:, k * 128:(k + 1) * 128], "rslot%d" % k) for k in range(4)]
        SNR = [PB[4], PB[5]]
        NTOT = NS * 128
        deferred = {}

        def defer(s, fn):
            deferred.setdefault(s, []).append(fn)

        def gather(b):
            cn, rn = c_nat[b % 2], r_nat[b % 2]
            S.dma("pool", lambda e: e.indirect_dma_start(out=cn[:, :], out_offset=None, in_=D["cache_c"][:, :],
                  in_offset=bass.IndirectOffsetOnAxis(ap=idx[:, b:b + 1], axis=0)), cn, reads=[idx], writes=[cn])
            S.dma("pool", lambda e: e.indirect_dma_start(out=rn[:, :], out_offset=None, in_=D["cache_r"][:, :],
                  in_offset=bass.IndirectOffsetOnAxis(ap=idx[:, b:b + 1], axis=0)), rn, reads=[idx], writes=[rn])

        def info(n):
            b, t = n // 128, n % 128
            qg = n // 32
            return b, t, t % 32, qg, n // 4

        def st_T(n):
            b, t, tl, qg, g = info(n)
            cn, rn = c_nat[b % 2], r_nat[b % 2]
            bank = PB[n % 2]
            bv = pbf(n % 2)

            def f(e):
                if n % 4 == 0:
                    e.transpose(out=bv[:, 128:256], in_=rn[:, t * 32:(t + 4) * 32], identity=ident_b[:, :])
                return e.transpose(out=bv[:, 0:128], in_=cn[:, t * 128:(t + 1) * 128], identity=ident_b[:, :])
            S.task("pe", f, reads=[cn, rn, ident_b], writes=[bank])

        def st_C(n):
            b, t, tl, qg, g = info(n)
            bank = PB[n % 2]
            bv = pbf(n % 2)
            if n % 4 == 0:
                krT_ = krT[g % 2]
                S.task("act", lambda e: e.activation(out=krT_[:, :], in_=bv[:, 128:256], func=AF.Copy), reads=[bank], writes=[krT_])
            cT_ = cT[n % 4]
            S.task("act", lambda e: e.activation(out=cT_[:, :], in_=bv[:, 0:128], func=AF.Copy), reads=[bank], writes=[cT_])

        def st_Y(n):
            b, t, tl, qg, g = info(n)
            cT_, krT_, yb, snr = cT[n % 4], krT[g % 2], PB[2 + n % 2], SNR[qg % 2]
            tt = n % 4

            def f(e):
                e.matmul(yb.ap[:, :], lhsT=cT_[:, :], rhs=w_uk[:, :], start=True, stop=True)
                e.matmul(snr.ap[:, tl * 8:(tl + 1) * 8], lhsT=cT_[:, :], rhs=qaT[:, :, b], start=True, stop=True)
                return e.matmul(snr.ap[:, 256 + tl * 8:256 + (tl + 1) * 8], lhsT=krT_[:, :], rhs=QR4[:, tt, :, b], start=True, stop=True)
            S.task("pe", f, reads=[cT_, w_uk, qaT, krT_, QR4], writes=[yb, snr])

        def st_Q(n):
            yb, ys_ = PB[2 + n % 2], ysq[n % 3]
            S.task("act", lambda e: e.activation(out=ys_[:, :], in_=yb.ap[:, :], func=AF.Square), reads=[yb], writes=[ys_])

        def st_R(n):
            b, t, tl, qg, g = info(n)
            ys_, ss_ = ysq[n % 3], ssq[qg % 2]
            S.task("dve", lambda e: e.tensor_reduce(out=ss_[:, tl * 8:(tl + 1) * 8], in_=ys_[:, :].rearrange("p (h d) -> p h d", h=8),
                                                    axis=AX.X, op=ALU.add), reads=[ys_], writes=[ss_])
            if tl == 31:
                quarter_end(qg, n + SK[3] + 1)

        def quarter_end(qg, s0):
            b, qtr = qg // 4, qg % 4
            snr, ss_, pts, cn = SNR[qg % 2], ssq[qg % 2], pTs[qg % 2], c_nat[b % 2]
            defer(s0 + 1, lambda: S.task("act", lambda e: e.activation(out=sq_t[:, :], in_=ss_[:, :], func=AF.Ln, bias=cols[:, C_EPS:C_EPS + 1], scale=1.0 / 64),
                                         reads=[ss_, cols], writes=[sq_t]))
            defer(s0 + 2, lambda: S.task("act", lambda e: e.activation(out=rsq[:, :], in_=sq_t[:, :], func=AF.Exp, scale=-0.5), reads=[sq_t], writes=[rsq]))
            defer(s0 + 3, lambda: S.task("dve", lambda e: e.tensor_tensor(out=rsq[:, :], in0=snr.ap[:, 0:256], in1=rsq[:, :], op=ALU.mult), reads=[snr, rsq], writes=[rsq]))
            defer(s0 + 4, lambda: S.task("dve", lambda e: e.tensor_tensor(out=rsq[:, :], in0=snr.ap[:, 256:512], in1=rsq[:, :], op=ALU.add), reads=[snr, rsq], writes=[rsq]))
            defer(s0 + 5, lambda: S.task("act", lambda e: e.activation(out=pts[:, :], in_=rsq[:, :], func=AF.Exp, scale=SCALE), reads=[rsq], writes=[pts]))

            def pv():
                def f(e):
                    ins = None
                    for tl in range(32):
                        t = qtr * 32 + tl
                        first = (qtr == 0 and tl == 0)
                        e.matmul(OB.ap[:8, 0:128], lhsT=pts[:, tl * 8:(tl + 1) * 8], rhs=cn[:, t * 128:(t + 1) * 128], start=first, stop=False)
                        ins = e.matmul(DB.ap[:8, 0:1], lhsT=pts[:, tl * 8:(tl + 1) * 8], rhs=ones_b[:, 0:1], start=first, stop=False)
                    return ins
                S.task("pe", f, reads=[pts, cn, ones_b], writes=[OB, DB])
            defer(s0 + 7, pv)
            if qtr == 3:
                def fin():
                    def f(e):
                        e.matmul(OB.ap[:8, 0:128], lhsT=Pmask[:, b, :], rhs=cown_b[:, :], start=False, stop=True)
                        return e.matmul(DB.ap[:8, 0:1], lhsT=Pmask[:, b, :], rhs=ones_b[:NS, 0:1], start=False, stop=True)
                    S.task("pe", f, reads=[Pmask, cown_b, ones_b], writes=[OB, DB])
                defer(s0 + 7, fin)
                defer(s0 + 9, lambda: S.task("dve", lambda e: e.reciprocal(out=rd_s[:, 0:1], in_=DB.ap[:8, 0:1]), reads=[DB], writes=[rd_s]))
                defer(s0 + 10, lambda: S.task("dve", lambda e: e.tensor_scalar(out=OL[:, b, :], in0=OB.ap[:8, 0:128], scalar1=rd_s[:, 0:1], scalar2=None, op0=ALU.mult),
                                              reads=[OB, rd_s], writes=[OL]))

        SK = [int(v) for v in os.environ.get('KSKEW', '1,2,3,4').split(',')]
        gather(0)
        for s in range(NTOT + 24):
            if s < NTOT:
                st_T(s)
            if SK[1] > SK[0] and 0 <= s - SK[1] < NTOT:
                st_Y(s - SK[1])
            if 0 <= s - SK[0] < NTOT:
                st_C(s - SK[0])
            if SK[1] <= SK[0] and 0 <= s - SK[1] < NTOT:
                st_Y(s - SK[1])
            if 0 <= s - SK[2] < NTOT:
                st_Q(s - SK[2])
            if 0 <= s - SK[3] < NTOT:
                st_R(s - SK[3])
            for fn in deferred.pop(s, []):
                fn()
            if s < NTOT and s % 128 == 12 and s // 128 + 1 < NS:
                gather(s // 128 + 1)
        assert not deferred, sorted(deferred)

        def f(e):
            ins = None
            for b in range(NS):
                ins = e.transpose(out=pbf(2)[:, b * 8:(b + 1) * 8], in_=OL[:, b, :], identity=ident_b[:8, :8])
            return ins
        S.task("pe", f, reads=[OL, ident_b], writes=[PB[2]])
        S.task("act", lambda e: e.activation(out=OLT[:, :, :].rearrange("p b h -> p (b h)"), in_=pbf(2)[:, 0:8 * NS], func=AF.Copy), reads=[PB[2]], writes=[OLT])

        def f(e):
            ins = None
            for h in range(8):
                ins = e.matmul(PB[3].ap[:NS, h * 64:(h + 1) * 64], lhsT=OLT[:, :, h], rhs=w_uv[:, h * 64:(h + 1) * 64], start=True, stop=True)
            return ins
        S.task("pe", f, reads=[OLT, w_uv], writes=[PB[3]])
        S.task("act", lambda e: e.activation(out=attn_s[:, :], in_=PB[3].ap[:NS, :], func=AF.Copy), reads=[PB[3]], writes=[attn_s])

        def f(e):
            ins = None
            for c in range(4):
                ins = e.transpose(out=pbf(2)[:, 512 + c * NS:512 + (c + 1) * NS], in_=attn_s[:, c * 128:(c + 1) * 128], identity=ident_b[:NS, :NS])
            return ins
        S.task("pe", f, reads=[attn_s, ident_b], writes=[PB[2]])
        S.task("act", lambda e: e.activation(out=attnT_s[:, :, :].rearrange("p c b -> p (c b)"), in_=pbf(2)[:, 512:512 + 4 * NS], func=AF.Copy),
               reads=[PB[2]], writes=[attnT_s])

    sample_attention()
    S.barrier()

    phase("U")
    wog = sb("wog", [128, 8, DM], BF16)
    w_proj = sb("w_proj", [128, 2, DM], BF16)
    w_fo = sb("w_fo", [128, 22, 512], BF16)
    GW = 256
    NG = DFF // GW
    wfi = [sb("wfi%d" % i, [128, 8, 2, GW], BF16) for i in range(2)]
    h_sb = [sb("h_sb%d" % i, [128, DM]) for i in range(4)] + [h_s]
    hn_b = sb("hn_b", [128, DM], BF16)
    NBX = 512 + NS
    hn2T = sb("hn2T", [128, 8, NBX], BF16)
    gTt = sb("gTt", [128, 22, NBX], BF16)
    a_sb = [sb("a_sb%d" % i, [128, 2, 2 + 512]) for i in range(2)]
    a_s = [sb("a_s%d" % i, [128, 2, NS]) for i in range(2)]
    cv = [sb("cv%d" % i, [128, 2, NBX]) for i in range(2)]
    sil = [sb("sil%d" % i, [128, NBX]) for i in range(2)]
    carry = sb("carry", [128, 44, 2])
    histT = sb("histT", [128, 44, 2 * NS])
    hist_pc = sb("hist_pc", [2 * NS, 1408])
    a_tok = [sb("a_tok%d" % i, [NS, 2, GW]) for i in range(2)]
    a_tok2 = [sb("a_tok2%d" % i, [2, 2, GW]) for i in range(2)]
    p_t = [sb("p_t%d" % i, [128, 256]) for i in range(2)]
    pTt = sb("pTt", [128, 2, 128], BF16)
    h3T = sb("h3T", [128, 8, 128], BF16)
    gate_f = sb("gate_f", [128, DM])
    e_f = [sb("e_f%d" % i, [128, DM]) for i in range(2)]

    def norm_transpose(hs, np_, stt, gcol, dstT, c0):
        S.task("act", lambda e: e.activation(out=junk[:np_, :], in_=hs[:np_, :], func=AF.Square, accum_out=stt[:np_, 0:1]), reads=[hs], writes=[junk, stt])
        S.task("dve", lambda e: e.tensor_scalar(out=stt[:np_, 1:2], in0=stt[:np_, 0:1], scalar1=1.0 / DM, scalar2=None, op0=ALU.mult), reads=[stt], writes=[stt])
        rstd_chain(stt, stt[:np_, 1:2], stt, stt[:np_, 2:3], stt, stt[:np_, 3:4], np_)
        S.task("act", lambda e: e.activation(out=hn_b[:np_, :], in_=hs[:np_, :], func=AF.Identity, scale=stt[:np_, 2:3]), reads=[hs, stt], writes=[hn_b])

        def f(e):
            ins = None
            for c in range(8):
                ins = e.transpose(out=pbf(7)[:, c * 128:c * 128 + np_], in_=hn_b[:np_, c * 128:(c + 1) * 128], identity=ident_b[:np_, :np_])
            return ins
        S.task("pe", f, reads=[hn_b, ident_b], writes=[PB[7]])
        S.task("dve", lambda e: e.tensor_tensor(out=dstT[:, :, c0:c0 + np_], in0=pbf(7)[:, :].rearrange("p (c t) -> p c t", c=8)[:, :, :np_],
                                                in1=cols[:, gcol:gcol + 8].unsqueeze(2).broadcast_to([128, 8, np_]), op=ALU.mult),
               reads=[PB[7], cols], writes=[dstT])

    def post_init():
        ld("pool", wog, wog[:, :, :], D["w_o"].rearrange("(c p) n -> p c n", p=128))
        ld("pool", w_proj, w_proj[:, :, :], D["w_proj"].rearrange("(c p) n -> p c n", p=128))
        S.task("pool", lambda e: e.memset(carry[:, :, :], 0.0), writes=[carry])
        S.dma("sp", lambda e: e.dma_start(out=D["conv_s"][:, 0, :], in_=D["hist"].rearrange("(b k) f -> b k f", k=2)[:, 1, :]), hist_pc, final=True)
        for pc in range(4):
            ld("sp", hist_pc, hist_pc[:, :], D["hist"][:, pc * 1408:(pc + 1) * 1408])

            def f(e, pc=pc):
                ins = None
                for k in range(11):
                    ins = e.transpose(out=PB[7].ap[:, k * 32:(k + 1) * 32], in_=hist_pc[:, k * 128:(k + 1) * 128], identity=ident_f[:2 * NS, :2 * NS])
                return ins
            S.task("pe", f, reads=[hist_pc, ident_f], writes=[PB[7]])
            S.task("act", lambda e, pc=pc: e.activation(out=histT[:, pc * 11:(pc + 1) * 11, :].rearrange("p c k -> p (c k)"), in_=PB[7].ap[:, 0:11 * 32], func=AF.Copy),
                   reads=[PB[7]], writes=[histT])
        xt = x_t[0]
        ld("sp", xt, xt[:NS, :], D["xs"][:, :])

        def f(e):
            ins = None
            for nb in range(2):
                chunks = [(attnT_s[:, c, :], wog[:, c, nb * 512:(nb + 1) * 512]) for c in range(4)]
                chunks += [(sgT_s[:, c, :], wog[:, 4 + c, nb * 512:(nb + 1) * 512]) for c in range(4)]
                for k, (l, r) in enumerate(chunks):
                    ins = e.matmul(PB[nb].ap[:NS, :], lhsT=l, rhs=r, start=(k == 0), stop=(k == 7))
            return ins
        S.task("pe", f, reads=[attnT_s, sgT_s, wog], writes=[PB[0], PB[1]])
        for nb in range(2):
            S.task("dve", lambda e, nb=nb: e.tensor_tensor(out=h_s[:, nb * 512:(nb + 1) * 512], in0=PB[nb].ap[:NS, :], in1=xt[:NS, nb * 512:(nb + 1) * 512], op=ALU.add),
                   reads=[PB[nb], xt], writes=[h_s])
        ld("pool", wog, wog[:, :, :], D["w_gate"].rearrange("(c p) n -> p c n", p=128))

    def post_tile_in(blk, ti):
        i = blk * 4 + ti
        hs = h_sb[ti]
        ld("sp", hs, hs[:, :], h_scr[i * 128:(i + 1) * 128, :], reads=[h_dr[i]])
        norm_transpose(hs, 128, stp[ti % 2], C_FNW, hn2T, ti * 128)

    def ffn_block(blk):
        last = (blk == NQB - 1)
        UPG = GW // 128
        NU = 22
        nb_ = NBX if last else 512

        def geo(n):
            g, j = n // UPG, n % UPG
            par = n % 2
            ab = (PB[4], PB[5]) if par == 0 else (PB[6], PB[7])
            return g, j, wfi[g % 2], ab, a_sb[par], a_s[par], cv[par], sil[par]

        def st_M(n):
            g, j, wt, ab, asb, ass, cvt, sl = geo(n)
            if j == 0:
                ld("pool", wt, wt[:, :, :, :].rearrange("p c h n -> p (c h n)"), D["w_ffi_r"][g])
                if last:
                    at, at2 = a_tok[g % 2], a_tok2[g % 2]

                    def f(e):
                        ins = None
                        for half in range(2):
                            for c in range(8):
                                ins = e.matmul(PB[2].ap[:NS, half * GW:(half + 1) * GW], lhsT=hn2T[:, c, 512:512 + NS], rhs=wt[:, c, half, :], start=(c == 0), stop=(c == 7))
                        for half in range(2):
                            for c in range(8):
                                ins = e.matmul(PB[3].ap[:2, half * GW:(half + 1) * GW], lhsT=hn2T[:, c, 510:512], rhs=wt[:, c, half, :], start=(c == 0), stop=(c == 7))
                        return ins
                    S.task("pe", f, reads=[hn2T, wt], writes=[PB[2], PB[3]])
                    S.task("act", lambda e: e.activation(out=at[:, :, :].rearrange("p a b -> p (a b)"), in_=PB[2].ap[:NS, 0:2 * GW], func=AF.Copy), reads=[PB[2]], writes=[at])
                    S.task("act", lambda e: e.activation(out=at2[:, :, :].rearrange("p a b -> p (a b)"), in_=PB[3].ap[:2, 0:2 * GW], func=AF.Copy), reads=[PB[3]], writes=[at2])
                    for half in range(2):
                        S.dma("sp", lambda e, half=half: e.dma_start(out=D["conv_s"][:, 1, half * DFF + g * GW:half * DFF + (g + 1) * GW], in_=at[:, half, :]),
                              at, reads=[at], final=True)
                        S.dma("sp", lambda e, half=half: e.dma_start(out=D["conv_p"][:, half * DFF + g * GW:half * DFF + (g + 1) * GW], in_=at2[:, half, :]),
                              at2, reads=[at2], final=True)

            def f(e):
                ins = None
                for half in range(2):
                    for c in range(8):
                        ins = e.matmul(ab[half].ap[:, :], lhsT=wt[:, c, half, j * 128:(j + 1) * 128], rhs=hn2T[:, c, 0:512], start=(c == 0), stop=(c == 7))
                return ins
            S.task("pe", f, reads=[wt, hn2T], writes=[ab[0], ab[1]])
            if last:
                def f2(e):
                    ins = None
                    for half in range(2):
                        for c in range(8):
                            ins = e.matmul(PB[n % 2].ap[:, half * NS:(half + 1) * NS], lhsT=wt[:, c, half, j * 128:(j + 1) * 128], rhs=hn2T[:, c, 512:512 + NS],
                                           start=(c == 0), stop=(c == 7))
                    return ins
                S.task("pe", f2, reads=[wt, hn2T], writes=[PB[n % 2]])

        def st_E(n):
            g, j, wt, ab, asb, ass, cvt, sl = geo(n)
            for half in range(2):
                ch = half * 22 + n
                S.task("act", lambda e, half=half: e.activation(out=asb[:, half, 2:514], in_=ab[half].ap[:, :], func=AF.Copy), reads=[ab[half]], writes=[asb])
                S.task("pool", lambda e, half=half, ch=ch: e.tensor_copy(out=asb[:, half, 0:2], in_=carry[:, ch, :]), reads=[carry], writes=[asb])
                S.task("pool", lambda e, half=half, ch=ch: e.tensor_copy(out=carry[:, ch, :], in_=asb[:, half, 512:514]), reads=[asb], writes=[carry])
            if last:
                S.task("act", lambda e: e.activation(out=ass[:, :, :].rearrange("p a b -> p (a b)"), in_=PB[n % 2].ap[:, 0:2 * NS], func=AF.Copy),
                       reads=[PB[n % 2]], writes=[ass])

        def st_V(n):
            g, j, wt, ab, asb, ass, cvt, sl = geo(n)
            W = []
            for half in range(2):
                ch = half * 22 + n
                W.append(tuple(cols[:, C_CW + k * 44 + ch:C_CW + k * 44 + ch + 1] for k in range(3)) + (cols[:, C_CB + ch:C_CB + ch + 1],))
            rd = [asb, cols]
            for step in range(3):
                for half in range(2):
                    w0, w1, w2, cb = W[half]
                    if step == 0:
                        fn = lambda e, half=half, w2=w2, cb=cb: e.tensor_scalar(out=cvt[:, half, 0:512], in0=asb[:, half, 2:514], scalar1=w2, scalar2=cb, op0=ALU.mult, op1=ALU.add)
                    elif step == 1:
                        fn = lambda e, half=half, w1=w1: e.scalar_tensor_tensor(out=cvt[:, half, 0:512], in0=asb[:, half, 1:513], scalar=w1, in1=cvt[:, half, 0:512], op0=ALU.mult, op1=ALU.add)
                    else:
                        fn = lambda e, half=half, w0=w0: e.scalar_tensor_tensor(out=cvt[:, half, 0:512], in0=asb[:, half, 0:512], scalar=w0, in1=cvt[:, half, 0:512], op0=ALU.mult, op1=ALU.add)
                    S.task("dve", fn, reads=rd + ([cvt] if step else []), writes=[cvt], nosync=(step > 0))
            if last:
                for half in range(2):
                    ch = half * 22 + n
                    w0, w1, w2, cb = W[half]
                    hv = histT[:, ch, :].rearrange("p (b k) -> p k b", k=2)
                    S.chain("dve", [
                        lambda e, half=half, w2=w2, cb=cb: e.tensor_scalar(out=cvt[:, half, 512:NBX], in0=ass[:, half, :], scalar1=w2, scalar2=cb, op0=ALU.mult, op1=ALU.add),
                        lambda e, half=half, w1=w1, hv=hv: e.scalar_tensor_tensor(out=cvt[:, half, 512:NBX], in0=hv[:, 1, :], scalar=w1, in1=cvt[:, half, 512:NBX], op0=ALU.mult, op1=ALU.add),
                        lambda e, half=half, w0=w0, hv=hv: e.scalar_tensor_tensor(out=cvt[:, half, 512:NBX], in0=hv[:, 0, :], scalar=w0, in1=cvt[:, half, 512:NBX], op0=ALU.mult, op1=ALU.add),
                    ], reads=[ass, cols, histT, cvt], writes=[cvt])

        def st_L(n):
            g, j, wt, ab, asb, ass, cvt, sl = geo(n)
            S.task("act", lambda e: e.activation(out=sl[:, 0:nb_], in_=cvt[:, 0, 0:nb_], func=AF.Silu), reads=[cvt], writes=[sl])

        def st_U(n):
            g, j, wt, ab, asb, ass, cvt, sl = geo(n)
            S.task("dve", lambda e: e.tensor_tensor(out=gTt[:, n, 0:nb_], in0=sl[:, 0:nb_], in1=cvt[:, 1, 0:nb_], op=ALU.mult), reads=[sl, cvt], writes=[gTt])

        for s in range(NU + 4):
            if s < NU:
                st_M(s)
            if 0 <= s - 1 < NU:
                st_E(s - 1)
            if 0 <= s - 4 < NU:
                st_U(s - 4)
            if 0 <= s - 2 < NU:
                st_V(s - 2)
            if 0 <= s - 3 < NU:
                st_L(s - 3)

    def ffn_out(blk):
        tl_list = [(ti, 128, ti * 128, h_sb[ti]) for ti in range(4)]
        if blk == NQB - 1:
            tl_list.append((4, NS, 512, h_s))
        for nb in range(2):
            ld("pool", w_fo, w_fo[:, :, :].rearrange("p c n -> p (c n)"), D["w_ffo_r"][nb])
            for k, (ti, np_, c0, hs) in enumerate(tl_list):
                bank = PB[2 + k % 2]

                def f(e, bank=bank, np_=np_, c0=c0):
                    ins = None
                    for fc in range(22):
                        ins = e.matmul(bank.ap[:np_, :], lhsT=gTt[:, fc, c0:c0 + np_], rhs=w_fo[:, fc, :], start=(fc == 0), stop=(fc == 21))
                    return ins
                S.task("pe", f, reads=[gTt, w_fo], writes=[bank])
                S.task("dve", lambda e, bank=bank, np_=np_, hs=hs, nb=nb: e.tensor_tensor(out=hs[:np_, nb * 512:(nb + 1) * 512], in0=bank.ap[:np_, :],
                                                                                         in1=hs[:np_, nb * 512:(nb + 1) * 512], op=ALU.add),
                       reads=[bank], writes=[hs])

    def post_tile_out(blk, ti, sample=False):
        np_ = NS if sample else 128
        hs = h_s if sample else h_sb[ti]
        stt = stp[ti % 2]
        pt_ = p_t[ti % 2]
        yt = e_f[ti % 2]
        src_p = D["ps"][:, :] if sample else D["pp"][(blk * 4 + ti) * 128:(blk * 4 + ti + 1) * 128, :]
        ld("sp", pt_, pt_[:np_, :], src_p)
        norm_transpose(hs, np_, stt, C_PNW, h3T, 0)

        def f(e):
            ins = None
            for nb in range(2):
                for c in range(8):
                    ins = e.matmul(PB[nb].ap[:np_, :], lhsT=h3T[:, c, :np_], rhs=wog[:, c, nb * 512:(nb + 1) * 512], start=(c == 0), stop=(c == 7))
            return ins
        S.task("pe", f, reads=[h3T, wog], writes=[PB[0], PB[1]])
        for nb in range(2):
            S.task("act", lambda e, nb=nb: e.activation(out=gate_f[:np_, nb * 512:(nb + 1) * 512], in_=PB[nb].ap[:np_, :], func=AF.Sigmoid), reads=[PB[nb]], writes=[gate_f])

        def f(e):
            ins = None
            for c in range(2):
                ins = e.transpose(out=PB[6].ap[:, c * 128:c * 128 + np_], in_=pt_[:np_, c * 128:(c + 1) * 128], identity=ident_f[:np_, :np_])
            return ins
        S.task("pe", f, reads=[pt_, ident_f], writes=[PB[6]])
        S.task("act", lambda e: e.activation(out=pTt[:, :, :np_], in_=PB[6].ap[:, 0:256].rearrange("p (c t) -> p c t", c=2)[:, :, :np_], func=AF.Copy), reads=[PB[6]], writes=[pTt])

        def f(e):
            ins = None
            for nb in range(2):
                for c in range(2):
                    ins = e.matmul(PB[4 + nb].ap[:np_, :], lhsT=pTt[:, c, :np_], rhs=w_proj[:, c, nb * 512:(nb + 1) * 512], start=(c == 0), stop=(c == 1))
            return ins
        S.task("pe", f, reads=[pTt, w_proj], writes=[PB[4], PB[5]])
        for nb in range(2):
            S.task("act", lambda e, nb=nb: e.activation(out=junk[:np_, 0:512], in_=PB[4 + nb].ap[:np_, :], func=AF.Square, accum_out=stt[:np_, 4 + nb:5 + nb]),
                   reads=[PB[4 + nb]], writes=[junk, stt])

        S.chain("dve", [
            lambda e: e.tensor_tensor(out=stt[:np_, 6:7], in0=stt[:np_, 4:5], in1=stt[:np_, 5:6], op=ALU.add),
            lambda e: e.tensor_scalar(out=stt[:np_, 6:7], in0=stt[:np_, 6:7], scalar1=1.0 / DM, scalar2=None, op0=ALU.mult),
        ], reads=[stt], writes=[stt])
        rstd_chain(stt, stt[:np_, 6:7], stt, stt[:np_, 7:8], stt, stt[:np_, 8:9], np_)
        for nb in range(2):
            sl = slice(nb * 512, (nb + 1) * 512)
            S.chain("dve", [
                lambda e, nb=nb, sl=sl: e.scalar_tensor_tensor(out=yt[:np_, sl], in0=PB[4 + nb].ap[:np_, :], scalar=stt[:np_, 7:8],
                                                               in1=rows[:np_, R_PPW + nb * 512:R_PPW + (nb + 1) * 512], op0=ALU.mult, op1=ALU.mult),
                lambda e, sl=sl: e.tensor_tensor(out=yt[:np_, sl], in0=yt[:np_, sl], in1=gate_f[:np_, sl], op=ALU.mult),
                lambda e, sl=sl: e.tensor_tensor(out=yt[:np_, sl], in0=yt[:np_, sl], in1=hs[:np_, sl], op=ALU.add),
            ], reads=[PB[4 + nb], stt, rows, gate_f, hs], writes=[yt], nosync=True)
        dst = D["y_s"][:, :] if sample else D["y_p"][(blk * 4 + ti) * 128:(blk * 4 + ti + 1) * 128, :]
        S.dma("sp", lambda e: e.dma_start(out=dst, in_=yt[:np_, :]), yt, reads=[yt], final=True)

    post_init()
    for blk in range(NQB):
        for ti in range(4):
            post_tile_in(blk, ti)
        if blk == NQB - 1:
            norm_transpose(h_s, NS, stp[0], C_FNW, hn2T, 512)
        ffn_block(blk)
        ffn_out(blk)
        for ti in range(4):
            post_tile_out(blk, ti)
        if blk == NQB - 1:
            post_tile_out(blk, 0, sample=True)

    S.check_deadlock()
    with nc.Block() as block:
        @block.tensor
        def _(e):
            S.replay("pe", e)

        @block.scalar
        def _(e):
            S.replay("act", e)

        @block.vector
        def _(e):
            S.replay("dve", e)

        @block.gpsimd
        def _(e):
            S.replay("pool", e)

        @block.sync
        def _(e):
            S.replay("sp", e)
    print("SBUF peak bytes/partition: G=%d U=%d sems=%d" % (peak["G"], peak["U"], S.nsem))
    es.close()
    return nc


def _host_consts(inp):
    f32 = np.float32
    cols = np.zeros((128, NCOL), f32)
    cols[:, C_ANW:C_ANW + 8] = inp["attn_norm_w"][0].reshape(8, 128).T
    cols[:, C_QNW:C_QNW + 2] = inp["q_norm_w"][0].reshape(2, 128).T
    cols[:, C_FNW:C_FNW + 8] = inp["ffn_norm_w"][0].reshape(8, 128).T
    cols[:, C_PNW:C_PNW + 8] = inp["ple_norm_w"][0].reshape(8, 128).T
    cw = inp["conv_w"][0]
    for k in range(3):
        cols[:, C_CW + k * 44:C_CW + (k + 1) * 44] = cw[k].reshape(44, 128).T
    cols[:, C_CB:C_CB + 44] = inp["conv_b"][0].reshape(44, 128).T
    cols[:, C_BST:C_BST + 8] = inp["b_s"][0].T
    cols[0:64, C_KNW] = inp["k_nope_norm_w"][0]
    for a in range(4):
        cols[a * 32:(a + 1) * 32, C_BLK + a] = 1.0
    cols[:, C_EPS] = EPS
    rows = np.zeros((1, NROW), f32)
    rows[0, R_KVW:R_KVW + 128] = inp["kv_norm_w"][0]
    rows[0, R_KRW:R_KRW + 32] = inp["k_rope_norm_w"][0]
    rows[0, R_QHW:R_QHW + 768] = np.tile(np.concatenate([inp["q_nope_norm_w"][0], inp["q_rope_norm_w"][0]]), 8)
    rows[0, R_KNW:R_KNW + 512] = np.tile(inp["k_nope_norm_w"][0], 8)
    rows[0, R_PPW:R_PPW + 1024] = inp["ple_post_norm_w"][0]
    rows[0, R_INVD:R_INVD + 3] = [1.0 / 256, 1.0 / 128, 1.0 / 32]
    rows[0, R_WS00:R_WS00 + 8] = inp["w_s"][0][:, 0, 0]
    rows[0, R_BS0:R_BS0 + 8] = inp["b_s"][0][:, 0]
    rows[0, R_INVH:R_INVH + 8] = 1.0 / 64
    rows[0, R_INVH + 8:R_INVH + 16] = 1.0 / 32
    inv = (10000.0 ** (-np.arange(16, dtype=np.float32) * (2.0 / 32))).astype(f32)
    pos = np.arange(T, dtype=f32)
    ang = pos[:, None] * inv[None, :]
    angs = np.full((NS, 1), 16384.0, f32) * inv[None, :]
    consts = {
        "cols": cols, "rows": rows, "ident": np.eye(128, dtype=f32),
        "tri": np.triu(np.ones((128, 128), f32)),
        "cosp": np.cos(ang).astype(f32), "sinp": np.sin(ang).astype(f32),
        "coss": np.cos(angs).astype(f32), "sins": np.sin(angs).astype(f32),
        "eye16": np.eye(NS, dtype=f32),
    }
    return consts


_NC_CACHE = {}


def kernel(**inp):
    inp = {k: np.asarray(v) for k, v in inp.items()}
    if "nc" not in _NC_CACHE:
        _NC_CACHE["nc"] = build_program()
    nc = _NC_CACHE["nc"]
    consts = _host_consts(inp)
    shared = {
        "cache_c": np.ascontiguousarray(inp["cache_ckv"][0].reshape(NPOOL, 128 * 128)),
        "cache_r": np.ascontiguousarray(inp["cache_krope"][0].reshape(NPOOL, 128 * 32)),
        "w_in": inp["w_in"][0], "w_uq": inp["w_uq"][0],
        "w_uk": np.ascontiguousarray(inp["w_uk"][0].reshape(128, 512)),
        "w_uv": np.ascontiguousarray(inp["w_uv"][0].reshape(128, 512)),
        "w_s": inp["w_s"][0], "w_o": inp["w_o"][0],
        "w_ffi_r": np.ascontiguousarray(inp["w_ff_in"][0].reshape(8, 128, 2, 11, 256).transpose(3, 1, 0, 2, 4).reshape(11, 128, 4096)),
        "w_ffo_r": np.ascontiguousarray(inp["w_ff_out"][0].reshape(22, 128, 2, 512).transpose(2, 1, 0, 3).reshape(2, 128, 22 * 512)),
        "w_gate": inp["w_ple_gate"][0], "w_proj": inp["w_ple_proj"][0],
    }
    shared.update(consts)
    in_maps = []
    for c in range(NCORES):
        m = dict(shared)
        m["xp"] = np.ascontiguousarray(inp["x_prompt"][c])
        m["pp"] = np.ascontiguousarray(inp["p_prompt"][0, c])
        m["xs"] = np.ascontiguousarray(inp["x_sample"][c * NS:(c + 1) * NS, 0])
        m["ps"] = np.ascontiguousarray(inp["p_sample"][0, c * NS:(c + 1) * NS, 0])
        m["pt"] = np.ascontiguousarray(inp["page_table"][c * NS:(c + 1) * NS].T.astype(np.int32))
        m["hist"] = np.ascontiguousarray(inp["state_conv"][0, c * NS:(c + 1) * NS].reshape(NS * 2, 2 * DFF))
        in_maps.append(m)
    res = run_bass_kernel_spmd(nc, in_maps, core_ids=list(range(NCORES)))
    R = res.results

    def cat(name):
        return np.concatenate([np.asarray(R[c][name]) for c in range(NCORES)], axis=0)

    y_p = cat("y_p").reshape(8, T, DM)
    y_s = cat("y_s").reshape(128, 1, DM)
    ckv_p = cat("ckv_p").reshape(1, 8, T, 128)
    kr_p = cat("kr_p").reshape(1, 8, T, 32)
    ckv_s = cat("ckv_s").reshape(1, 128, 1, 128)
    kr_s = cat("kr_s").reshape(1, 128, 1, 32)
    v_p = cat("v_p").reshape(1, 8, 128, 512)
    v_s = cat("v_s").reshape(1, 128, 1, 512)
    conv_p = cat("conv_p").reshape(1, 8, 2, 2 * DFF)
    conv_s = cat("conv_s").reshape(1, 128, 2, 2 * DFF)
    return tuple(np.ascontiguousarray(a, dtype=np.float32) for a in (y_p, y_s, ckv_p, kr_p, ckv_s, kr_s, v_p, v_s, conv_p, conv_s))
```

```python
import os
from contextlib import ExitStack
import numpy as np
import concourse.bass as bass
import concourse.mybir as mybir
from concourse.bass_utils import run_bass_kernel_spmd

F32 = mybir.dt.float32
BF16 = mybir.dt.bfloat16
I32 = mybir.dt.int32
AF = mybir.ActivationFunctionType
ALU = mybir.AluOpType
AX = mybir.AxisListType

NCORES = 8
T = 2048
DM = 1024
NS = 16
NPOOL = 20480
DFF = 2816
EPS = 1e-6
SCALE = 96 ** -0.5
NT = T // 128
NQB = 4

C_ANW, C_QNW, C_FNW, C_PNW, C_CW, C_CB, C_BST, C_KNW, C_BLK, C_EPS, NCOL = 0, 8, 10, 18, 26, 158, 202, 210, 211, 215, 216
R_KVW, R_KRW, R_QHW, R_KNW, R_PPW, R_INVD, R_WS00, R_BS0, R_INVH, NROW = 0, 128, 160, 928, 1440, 2464, 2467, 2475, 2483, 2499


class Tile:
    __slots__ = ("ap", "name", "w", "r", "sem", "cnt")

    def __init__(self, ap, name):
        self.ap = ap
        self.name = name
        self.w = None
        self.r = {}
        self.sem = None
        self.cnt = 0

    def __getitem__(self, k):
        return self.ap[k]


class Sched:
    ENGS = ("pe", "act", "dve", "pool", "sp")

    def __init__(self, nc, es):
        self.nc = nc
        self.es = es
        self.streams = {k: [] for k in self.ENGS}
        self.cnt = {k: 0 for k in self.ENGS}
        self.waited = {k: {} for k in self.ENGS}
        self.prog = {k: es.enter_context(nc.semaphore("prog_" + k)) for k in ("pe", "act", "dve", "pool")}
        self.final = {}
        self.nsem = 4
        self.dtiles = {}

    def _deps(self, eng, reads, writes, nosync=False):
        deps = {}

        def add(tok):
            if tok is None:
                return
            s, v = tok
            k = id(s)
            if k not in deps or deps[k][1] < v:
                deps[k] = (s, v)

        own = self.prog.get(eng)
        for t in reads:
            if nosync and t.w is not None and t.w[0] is own:
                continue
            add(t.w)
        for t in writes:
            if t.w is not None and t.w[0] is not own:
                add(t.w)
            for tok in t.r.values():
                if tok[0] is not own:
                    add(tok)
        out = []
        for k, (s, v) in deps.items():
            if eng == "pe" and s is own:
                continue
            if self.waited[eng].get(k, 0) >= v:
                continue
            self.waited[eng][k] = v
            out.append((s, v))
        return out

    def _post(self, tok, reads, writes):
        for t in reads:
            t.r[id(tok[0])] = tok
        for t in writes:
            t.w = tok
            t.r = {}

    def task(self, eng, fn, reads=(), writes=(), nosync=False):
        waits = self._deps(eng, reads, writes, nosync)
        self.cnt[eng] += 1
        tok = (self.prog[eng], self.cnt[eng])
        self.streams[eng].append((waits, fn, tok[0], 1))
        self._post(tok, reads, writes)
        return tok

    def chain(self, eng, fns, reads=(), writes=(), nosync=False):
        rd = list(reads)
        for k, fn in enumerate(fns):
            self.task(eng, fn, reads=rd if k == 0 else rd + list(writes), writes=writes, nosync=(nosync and k > 0))

    def dma(self, eng, fn, sb, reads=(), writes=(), final=False):
        waits = self._deps(eng, reads, writes)
        if sb.sem is None:
            sb.sem = self.es.enter_context(self.nc.semaphore("d_" + sb.name))
            self.nsem += 1
        sb.cnt += 16
        tok = (sb.sem, sb.cnt)
        self.dtiles[id(sb)] = sb
        self.streams[eng].append((waits, fn, tok[0], 16))
        self._post(tok, reads, writes)
        if final:
            self.final[id(sb.sem)] = tok
        return tok

    def barrier(self):
        toks = {}
        for k in ("pe", "act", "dve", "pool"):
            if self.cnt[k] > 0:
                toks[id(self.prog[k])] = (self.prog[k], self.cnt[k])
        for t in self.dtiles.values():
            toks[id(t.sem)] = (t.sem, t.cnt)
        for eng in self.ENGS:
            own = self.prog.get(eng)
            waits = []
            for k, (s, v) in toks.items():
                if s is own:
                    continue
                if self.waited[eng].get(k, 0) >= v:
                    continue
                self.waited[eng][k] = v
                waits.append((s, v))
            self.streams[eng].append((waits, None, None, 0))

    def check_deadlock(self):
        val = {}
        ptr = {k: 0 for k in self.ENGS}
        progress = True
        while progress:
            progress = False
            for k in self.ENGS:
                st = self.streams[k]
                while ptr[k] < len(st):
                    waits, fn, sem, inc = st[ptr[k]]
                    if any(val.get(id(s), 0) < v for s, v in waits):
                        break
                    if fn is not None:
                        val[id(sem)] = val.get(id(sem), 0) + inc
                    ptr[k] += 1
                    progress = True
        stuck = {k: (ptr[k], len(self.streams[k])) for k in self.ENGS if ptr[k] < len(self.streams[k])}
        assert not stuck, "semaphore deadlock: %s" % stuck

    def replay(self, eng, e):
        for waits, fn, sem, inc in self.streams[eng]:
            for s, v in waits:
                e.wait_ge(s, v)
            if fn is None:
                continue
            ins = fn(e)
            ins.then_inc(sem, inc)
        if eng == "sp":
            for s, v in self.final.values():
                e.wait_ge(s, v)


def build_program(npool=NPOOL):
    nc = bass.Bass("TRN2", target_bir_lowering=False)
    es = ExitStack()
    D = {}

    def din(name, shape, dt=F32):
        D[name] = nc.dram_tensor(name, list(shape), dt, kind="ExternalInput").ap()

    def dout(name, shape):
        D[name] = nc.dram_tensor(name, list(shape), F32, kind="ExternalOutput").ap()

    din("xp", [T, DM]); din("pp", [T, 256]); din("xs", [NS, DM]); din("ps", [NS, 256])
    din("cache_c", [npool, 128 * 128]); din("cache_r", [npool, 128 * 32])
    din("pt", [128, NS], I32); din("hist", [NS * 2, 2 * DFF])
    din("w_in", [DM, 1440]); din("w_uq", [256, 768]); din("w_uk", [128, 512]); din("w_uv", [128, 512])
    din("w_s", [8, 128, 128]); din("w_o", [DM, DM]); din("w_ffi_r", [11, 128, 4096]); din("w_ffo_r", [2, 128, 22 * 512])
    din("w_gate", [DM, DM]); din("w_proj", [256, DM])
    din("cols", [128, NCOL]); din("rows", [1, NROW]); din("ident", [128, 128]); din("tri", [128, 128])
    din("cosp", [T, 16]); din("sinp", [T, 16]); din("coss", [NS, 16]); din("sins", [NS, 16])
    din("eye16", [NS, NS])
    dout("y_p", [T, DM]); dout("y_s", [NS, DM]); dout("ckv_p", [T, 128]); dout("kr_p", [T, 32])
    dout("ckv_s", [NS, 128]); dout("kr_s", [NS, 32]); dout("v_p", [128, 512]); dout("v_s", [NS, 512])
    dout("conv_p", [2, 2 * DFF]); dout("conv_s", [NS, 2, 2 * DFF])

    S = Sched(nc, es)
    tiles = {}

    GBYTES = 43392
    UBYTES = 168576
    arenaG = es.enter_context(nc.sbuf_tensor("arenaG", [128, GBYTES // 2], BF16))
    arenaU = es.enter_context(nc.sbuf_tensor("arenaU", [128, UBYTES // 2], BF16))
    apos = {"G": 0, "U": 0}
    cur = ["G"]
    peak = {"G": 0, "U": 0}

    def sb(name, shape, dt=F32):
        esz = 4 if dt in (F32, I32) else 2
        n = 1
        for s_ in shape[1:]:
            n *= s_
        nbytes = (n * esz + 63) // 64 * 64
        a = cur[0]
        base = arenaG if a == "G" else arenaU
        off = apos[a]
        apos[a] += nbytes
        peak[a] = max(peak[a], apos[a])
        lim = GBYTES if a == "G" else UBYTES
        assert apos[a] <= lim, (name, a, apos[a], lim)
        ap = base[:shape[0], off // 2:(off + n * esz) // 2]
        if dt != BF16:
            ap = ap.bitcast(dt)
        if len(shape) == 3:
            ap = ap.rearrange("p (a b) -> p a b", a=shape[1])
        elif len(shape) == 4:
            ap = ap.rearrange("p (a b c) -> p a b c", a=shape[1], b=shape[2])
        t = Tile(ap, name)
        tiles[name] = t
        return t

    def phase(a):
        print("arena peaks so far", peak, "pos", apos)
        cur[0] = a
        if a == "U":
            apos["U"] = 0

    PB = [Tile(es.enter_context(nc.psum_tensor("pb%d" % i, [128, 512], F32)), "pb%d" % i) for i in range(8)]

    def pbf(i):
        return PB[i].ap[:, :].bitcast(BF16)

    phase("G")
    ident_f = sb("ident_f", [128, 128]); ident_b = sb("ident_b", [128, 128], BF16)
    tri_f = sb("tri_f", [128, 128]); tri_b = sb("tri_b", [128, 128], BF16)
    cols = sb("cols", [128, NCOL]); rows = sb("rows", [128, NROW])
    cosp = sb("cosp", [128, NT, 16]); sinp = sb("sinp", [128, NT, 16])
    coss = sb("coss", [NS, 16]); sins = sb("sins", [NS, 16])
    eye16 = sb("eye16", [NS, NS])
    idx = sb("idx", [128, NS], I32)
    ones_b = sb("ones_b", [128, 128], BF16)
    ones_f = sb("ones_f", [128, 64])

    def ld(eng, dst, dst_ap, src_ap, reads=(), **kw):
        S.dma(eng, lambda e: e.dma_start(out=dst_ap, in_=src_ap, **kw), dst, reads=list(reads), writes=[dst])

    ld("sp", ident_f, ident_f[:, :], D["ident"][:, :])
    ld("sp", tri_f, tri_f[:, :], D["tri"][:, :])
    ld("sp", cols, cols[:, :], D["cols"][:, :])
    ld("sp", rows, rows[:, :], D["rows"][0:1, :].partition_broadcast(128))
    ld("sp", cosp, cosp[:, :, :], D["cosp"].rearrange("(n p) j -> p n j", p=128))
    ld("sp", sinp, sinp[:, :, :], D["sinp"].rearrange("(n p) j -> p n j", p=128))
    ld("sp", coss, coss[:, :], D["coss"][:, :])
    ld("sp", sins, sins[:, :], D["sins"][:, :])
    ld("sp", eye16, eye16[:, :], D["eye16"][:, :])
    ld("sp", idx, idx[:, :], D["pt"][:, :])
    S.task("dve", lambda e: e.tensor_copy(out=ident_b[:, :], in_=ident_f[:, :]), reads=[ident_f], writes=[ident_b])
    S.task("dve", lambda e: e.tensor_copy(out=tri_b[:, :], in_=tri_f[:, :]), reads=[tri_f], writes=[tri_b])
    S.task("pool", lambda e: e.memset(ones_b[:, :], 1.0), writes=[ones_b])
    S.task("pool", lambda e: e.memset(ones_f[:, :], 1.0), writes=[ones_f])

    w_uk = sb("w_uk", [128, 512], BF16); w_uv = sb("w_uv", [128, 512], BF16)
    w_uk_f = sb("w_uk_f", [128, 512]); w_ukT = sb("w_ukT", [64, 8, 128], BF16)
    QT_s = sb("QT_s", [96, 8, NS], BF16)
    Qcat_s = sb("Qcat_s", [NS, 8, 96]); Kcat_s = sb("Kcat_s", [NS, 8, 96])
    cown_b = sb("cown_b", [NS, 128], BF16)
    sgT_s = sb("sgT_s", [128, 4, NS], BF16)
    attnT_s = sb("attnT_s", [128, 4, NS], BF16)
    h_s = sb("h_s", [NS, DM])
    x_t = [sb("x_t%d" % i, [128, DM]) for i in range(2)]
    junk = sb("junk", [128, DM], BF16)
    st = [sb("st%d" % i, [128, 64]) for i in range(2)]
    stp = [sb("stp%d" % i, [128, 16]) for i in range(2)]
    ld("pool", w_uk, w_uk[:, :], D["w_uk"][:, :])
    ld("pool", w_uv, w_uv[:, :], D["w_uv"][:, :])
    ld("sp", w_uk_f, w_uk_f[:, :], D["w_uk"][:, :])
    for g4 in range(2):
        def f(e, g4=g4):
            ins = None
            for gg in range(4):
                h = g4 * 4 + gg
                ins = e.transpose(out=PB[1].ap[:64, gg * 128:(gg + 1) * 128], in_=w_uk_f[:, h * 64:(h + 1) * 64], identity=ident_f[:, :])
            return ins
        S.task("pe", f, reads=[w_uk_f, ident_f], writes=[PB[1]])
        S.task("dve", lambda e, g4=g4: e.tensor_copy(
            out=w_ukT[:, g4 * 4:(g4 + 1) * 4, :], in_=PB[1].ap[:64, :].rearrange("p (g t) -> p g t", g=4)),
            reads=[PB[1]], writes=[w_ukT])
    h_scr = nc.dram_tensor("h_scr", [T, DM], F32).ap()
    h_dr = [Tile(None, "h_dr%d" % i) for i in range(NT)]

    def col(c0, n=1):
        return cols[:, c0:c0 + n]

    def rstd_chain(ss_tile, ss_ap, out_tile, out_ap, tmp_tile, tmp_ap, np_):
        S.task("act", lambda e: e.activation(out=tmp_ap, in_=ss_ap, func=AF.Sqrt, bias=cols[:np_, C_EPS:C_EPS + 1], scale=1.0),
               reads=[ss_tile, cols], writes=[tmp_tile])
        S.task("dve", lambda e: e.reciprocal(out=out_ap, in_=tmp_ap), reads=[tmp_tile], writes=[out_tile])

    def rope_apply(dst_tile, dst1, dst2, src_tile, x1, x2, cos_ap, sin_ap, cs_tiles, tmp_tile, tmpv):
        def f(e):
            e.tensor_tensor(out=tmpv[0], in0=x1, in1=cos_ap, op=ALU.mult)
            e.tensor_tensor(out=tmpv[1], in0=x2, in1=sin_ap, op=ALU.mult)
            e.tensor_tensor(out=tmpv[2], in0=x1, in1=sin_ap, op=ALU.mult)
            return e.tensor_tensor(out=tmpv[3], in0=x2, in1=cos_ap, op=ALU.mult)
        S.task("dve", f, reads=[src_tile] + cs_tiles, writes=[tmp_tile])

        def g(e):
            e.tensor_tensor(out=dst1, in0=tmpv[0], in1=tmpv[1], op=ALU.subtract)
            return e.tensor_tensor(out=dst2, in0=tmpv[2], in1=tmpv[3], op=ALU.add)
        S.task("dve", g, reads=[tmp_tile], writes=[dst_tile])

    phase("U")
    w_in = sb("w_in", [128, 8, 1440], BF16); w_uq = sb("w_uq", [128, 2, 768], BF16)
    w_s_f = sb("w_s_f", [128, 8, 128]); wsT = sb("wsT", [128, 8, 128], BF16)
    w_oH = sb("w_oH", [64, 8, DM], BF16)
    w_oS = sb("w_oS", [128, 4, DM], BF16)
    Kop = [sb("Kop%d" % q, [96, 8, 512], BF16) for q in range(NQB)]
    Vaug = [sb("Vaug%d" % q, [128, 4, 8, 65], BF16) for q in range(NQB)]
    attnT = sb("attnT", [64, 8, 512], BF16)
    sgT = sb("sgT", [128, 4, 512], BF16)
    xT = [sb("xT%d" % i, [128, 8, 128], BF16) for i in range(2)]
    zq_b = sb("zq_b", [128, 256], BF16)
    qlT = sb("qlT", [128, 2, 128], BF16)
    q_sq = sb("q_sq", [128, 768])
    q_f = sb("q_f", [128, 8, 96])
    ropet = sb("ropet", [128, 4, 8, 16])
    Qcat_b = sb("Qcat_b", [128, 8, 96], BF16)
    QT = sb("QT", [96, 8, 512], BF16)
    c_f = [sb("c_f%d" % i, [128, 128]) for i in range(2)]
    ckvT = sb("ckvT", [128, 128], BF16)
    kn_f = sb("kn_f", [128, 512])
    Kcat_b = sb("Kcat_b", [128, 8, 96], BF16)
    kr_f = [sb("kr_f%d" % i, [128, 32]) for i in range(2)]
    krt = sb("krt", [128, 6, 16])
    u_f = sb("u_f", [128, 512])
    v_f = [sb("v_f%d" % i, [128, 512]) for i in range(2)]
    v_b = sb("v_b", [128, 512], BF16)
    mixt = sb("mixt", [128, 512])
    sg_b = sb("sg_b", [128, 512], BF16)
    pT = [sb("pT%d" % i, [128, 512], BF16) for i in range(3)]
    rden = mixt; rden_bc = kn_f
    h_t = [sb("h_t0", [128, DM])] * 2

    ld("pool", w_in, w_in[:, :, :], D["w_in"].rearrange("(c p) n -> p c n", p=128))
    ld("pool", w_uq, w_uq[:, :, :], D["w_uq"].rearrange("(c p) n -> p c n", p=128))
    ld("sp", w_s_f, w_s_f[:, :, :], D["w_s"].rearrange("g t s -> t g s"))
    ld("pool", w_oH, w_oH[:, :, :], D["w_o"][0:512, :].rearrange("(h p) n -> p h n", p=64))
    ld("pool", w_oS, w_oS[:, :, :], D["w_o"][512:1024, :].rearrange("(c p) n -> p c n", p=128))
    for g4 in range(2):
        def f(e, g4=g4):
            ins = None
            for gg in range(4):
                g = g4 * 4 + gg
                ins = e.transpose(out=PB[0].ap[:, gg * 128:(gg + 1) * 128], in_=w_s_f[:, g, :], identity=ident_f[:, :])
            return ins
        S.task("pe", f, reads=[w_s_f, ident_f], writes=[PB[0]])
        S.task("dve", lambda e, g4=g4: e.tensor_tensor(
            out=wsT[:, g4 * 4:(g4 + 1) * 4, :], in0=PB[0].ap[:, :].rearrange("p (g t) -> p g t", g=4),
            in1=tri_f[:, :].unsqueeze(1).broadcast_to([128, 4, 128]), op=ALU.mult), reads=[PB[0], tri_f], writes=[wsT])
    for q in range(NQB):
        S.task("pool", lambda e, q=q: e.memset(Vaug[q][:, :, :, :], 1.0), writes=[Vaug[q]])

    def phase_a_tile(i, sample=False):
        np_ = NS if sample else 128
        par = i % 2
        xt, xTt, stt = x_t[par], xT[par], st[par]
        qb, tl = (0, 0) if sample else (i // 4, i % 4)
        src_x = D["xs"][:, :] if sample else D["xp"][i * 128:(i + 1) * 128, :]
        ld("sp", xt, xt[:np_, :], src_x)
        S.task("act", lambda e: e.activation(out=junk[:np_, :], in_=xt[:np_, :], func=AF.Square, accum_out=stt[:np_, 0:1]),
               reads=[xt], writes=[junk, stt])

        def f(e):
            ins = None
            for c in range(8):
                ins = e.transpose(out=PB[c // 4].ap[:, (c % 4) * 128:(c % 4) * 128 + np_], in_=xt[:np_, c * 128:(c + 1) * 128],
                                  identity=ident_f[:np_, :np_])
            return ins
        S.task("pe", f, reads=[xt, ident_f], writes=[PB[0], PB[1]])
        for hh in range(2):
            S.task("dve", lambda e, hh=hh: e.tensor_tensor(
                out=xTt[:, hh * 4:(hh + 1) * 4, :np_], in0=PB[hh].ap[:, :].rearrange("p (c t) -> p c t", c=4)[:, :, :np_],
                in1=cols[:, C_ANW + hh * 4:C_ANW + hh * 4 + 4].unsqueeze(2).broadcast_to([128, 4, np_]), op=ALU.mult),
                reads=[PB[hh], cols], writes=[xTt])

        def f(e):
            ins = None
            for nb, (c0, c1) in enumerate(((0, 512), (512, 1024), (1024, 1440))):
                for c in range(8):
                    ins = e.matmul(PB[2 + nb].ap[:np_, 0:c1 - c0], lhsT=xTt[:, c, :np_], rhs=w_in[:, c, c0:c1], start=(c == 0), stop=(c == 7))
            return ins
        S.task("pe", f, reads=[xTt, w_in], writes=[PB[2], PB[3], PB[4]])
        Z0, Z1, Z2 = PB[2], PB[3], PB[4]
        S.task("act", lambda e: e.activation(out=junk[:np_, 0:256], in_=Z0.ap[:np_, 0:256], func=AF.Square, accum_out=stt[:np_, 1:2]),
               reads=[Z0], writes=[junk, stt])
        S.task("act", lambda e: e.activation(out=junk[:np_, 0:128], in_=Z0.ap[:np_, 256:384], func=AF.Square, accum_out=stt[:np_, 2:3]),
               reads=[Z0], writes=[junk, stt])
        S.task("act", lambda e: e.activation(out=junk[:np_, 0:32], in_=Z0.ap[:np_, 384:416], func=AF.Square, accum_out=stt[:np_, 3:4]),
               reads=[Z0], writes=[junk, stt])
        S.task("dve", lambda e: e.tensor_scalar(out=stt[:np_, 14:15], in0=stt[:np_, 0:1], scalar1=1.0 / DM, scalar2=None, op0=ALU.mult),
               reads=[stt], writes=[stt])
        rstd_chain(stt, stt[:np_, 14:15], stt, stt[:np_, 4:5], stt, stt[:np_, 15:16], np_)
        rx = stt[:np_, 4:5]

        S.chain("dve", [
            lambda e: e.tensor_scalar(out=stt[:np_, 5:8], in0=stt[:np_, 1:4], scalar1=rx, scalar2=rx, op0=ALU.mult, op1=ALU.mult),
            lambda e: e.tensor_tensor(out=stt[:np_, 5:8], in0=stt[:np_, 5:8], in1=rows[:np_, R_INVD:R_INVD + 3], op=ALU.mult),
        ], reads=[stt, rows], writes=[stt])
        rstd_chain(stt, stt[:np_, 5:8], stt, stt[:np_, 8:11], stt, stt[:np_, 16:19], np_)
        S.task("dve", lambda e: e.tensor_scalar(out=stt[:np_, 11:14], in0=stt[:np_, 8:11], scalar1=rx, scalar2=None, op0=ALU.mult),
               reads=[stt], writes=[stt])
        s_q, s_c, s_kr = stt[:np_, 11:12], stt[:np_, 12:13], stt[:np_, 13:14]

        vf = v_f[par]
        S.task("act", lambda e: e.activation(out=u_f[:np_, 0:96], in_=Z0.ap[:np_, 416:512], func=AF.Gelu_apprx_tanh, scale=rx),
               reads=[Z0, stt], writes=[u_f])
        S.task("act", lambda e: e.activation(out=u_f[:np_, 96:512], in_=Z1.ap[:np_, 0:416], func=AF.Gelu_apprx_tanh, scale=rx),
               reads=[Z1, stt], writes=[u_f])
        S.task("act", lambda e: e.activation(out=vf[:np_, 0:96], in_=Z1.ap[:np_, 416:512], func=AF.Gelu_apprx_tanh, scale=rx),
               reads=[Z1, stt], writes=[vf])
        S.task("act", lambda e: e.activation(out=vf[:np_, 96:512], in_=Z2.ap[:np_, 0:416], func=AF.Gelu_apprx_tanh, scale=rx),
               reads=[Z2, stt], writes=[vf])
        if sample:
            S.dma("sp", lambda e: e.dma_start(out=D["v_s"][:, :], in_=vf[:np_, :]), vf, reads=[vf], final=True)
        elif i == NT - 1:
            S.dma("sp", lambda e: e.dma_start(out=D["v_p"][:, :], in_=vf[:, :]), vf, reads=[vf], final=True)

        S.task("dve", lambda e: e.tensor_copy(out=zq_b[:np_, :], in_=Z0.ap[:np_, 0:256]), reads=[Z0], writes=[zq_b])

        def f(e):
            ins = None
            for c in range(2):
                ins = e.transpose(out=pbf(7)[:, c * 128:c * 128 + np_], in_=zq_b[:np_, c * 128:(c + 1) * 128], identity=ident_b[:np_, :np_])
            return ins
        S.task("pe", f, reads=[zq_b, ident_b], writes=[PB[7]])
        S.task("dve", lambda e: e.tensor_tensor(
            out=qlT[:, :, :np_], in0=pbf(7)[:, 0:256].rearrange("p (c t) -> p c t", c=2)[:, :, :np_],
            in1=cols[:, C_QNW:C_QNW + 2].unsqueeze(2).broadcast_to([128, 2, np_]), op=ALU.mult), reads=[PB[7], cols], writes=[qlT])

        def f(e):
            ins = None
            for nb, (c0, c1) in enumerate(((0, 512), (512, 768))):
                for c in range(2):
                    ins = e.matmul(PB[5 + nb].ap[:np_, 0:c1 - c0], lhsT=qlT[:, c, :np_], rhs=w_uq[:, c, c0:c1], start=(c == 0), stop=(c == 1))
            return ins
        S.task("pe", f, reads=[qlT, w_uq], writes=[PB[5], PB[6]])
        qfl = q_f[:np_, :, :].rearrange("p h d -> p (h d)")
        S.task("act", lambda e: e.activation(out=qfl[:, 0:512], in_=PB[5].ap[:np_, :], func=AF.Identity, scale=s_q),
               reads=[PB[5], stt], writes=[q_f])
        S.task("act", lambda e: e.activation(out=qfl[:, 512:768], in_=PB[6].ap[:np_, 0:256], func=AF.Identity, scale=s_q),
               reads=[PB[6], stt], writes=[q_f])
        S.task("act", lambda e: e.activation(out=q_sq[:np_, :], in_=qfl, func=AF.Square), reads=[q_f], writes=[q_sq])

        def f(e):
            qv = q_sq[:np_, :].rearrange("p (h d) -> p h d", h=8)
            e.tensor_reduce(out=stt[:np_, 20:28], in_=qv[:, :, 0:64], axis=AX.X, op=ALU.add)
            return e.tensor_reduce(out=stt[:np_, 28:36], in_=qv[:, :, 64:96], axis=AX.X, op=ALU.add)
        S.chain("dve", [f, lambda e: e.tensor_tensor(out=stt[:np_, 20:36], in0=stt[:np_, 20:36], in1=rows[:np_, R_INVH:R_INVH + 16], op=ALU.mult)],
                reads=[q_sq, rows], writes=[stt])
        S.task("act", lambda e: e.activation(out=q_sq[:np_, 0:16], in_=stt[:np_, 20:36], func=AF.Sqrt, bias=cols[:np_, C_EPS:C_EPS + 1], scale=1.0),
               reads=[stt, cols], writes=[q_sq])
        S.task("dve", lambda e: e.reciprocal(out=stt[:np_, 36:52], in_=q_sq[:np_, 0:16]), reads=[q_sq], writes=[stt])

        def f(e):
            e.tensor_tensor(out=q_f[:np_, :, 0:64], in0=q_f[:np_, :, 0:64], in1=stt[:np_, 36:44].unsqueeze(2).broadcast_to([np_, 8, 64]), op=ALU.mult)
            return e.tensor_tensor(out=q_f[:np_, :, 64:96], in0=q_f[:np_, :, 64:96], in1=stt[:np_, 44:52].unsqueeze(2).broadcast_to([np_, 8, 32]), op=ALU.mult)
        S.chain("dve", [f, lambda e: e.tensor_tensor(out=q_f[:np_, :, :], in0=q_f[:np_, :, :], in1=rows[:np_, R_QHW:R_QHW + 768].rearrange("p (h d) -> p h d", h=8), op=ALU.mult)],
                reads=[stt, rows, q_f], writes=[q_f])
        qdst = Qcat_s if sample else Qcat_b
        S.task("pool", lambda e: e.tensor_copy(out=qdst[:np_, :, 0:64], in_=q_f[:np_, :, 0:64]), reads=[q_f], writes=[qdst])
        if sample:
            cos_ap = coss[:, :].unsqueeze(1).broadcast_to([np_, 8, 16]); sin_ap = sins[:, :].unsqueeze(1).broadcast_to([np_, 8, 16])
            cst = [coss, sins]
        else:
            cos_ap = cosp[:, i, :].unsqueeze(1).broadcast_to([np_, 8, 16]); sin_ap = sinp[:, i, :].unsqueeze(1).broadcast_to([np_, 8, 16])
            cst = [cosp, sinp]
        rope_apply(qdst, qdst[:np_, :, 64:80], qdst[:np_, :, 80:96], q_f, q_f[:np_, :, 64:80], q_f[:np_, :, 80:96],
                   cos_ap, sin_ap, cst, ropet, [ropet[:np_, k, :, :] for k in range(4)])
        if sample:
            S.task("pool", lambda e: e.tensor_copy(out=Qcat_b[:np_, :, :], in_=Qcat_s[:np_, :, :]), reads=[Qcat_s], writes=[Qcat_b])

        def f(e):
            ins = None
            for h in range(8):
                ins = e.transpose(out=pbf(7)[:96, h * 128:h * 128 + np_], in_=Qcat_b[:np_, h, :], identity=ident_b[:np_, :np_])
            return ins
        S.task("pe", f, reads=[Qcat_b, ident_b], writes=[PB[7]])
        if sample:
            S.task("act", lambda e: e.activation(out=QT_s[:, :, :], in_=pbf(7)[:96, :].rearrange("p (h t) -> p h t", h=8)[:, :, :np_], func=AF.Copy),
                   reads=[PB[7]], writes=[QT_s])
        else:
            S.task("act", lambda e: e.activation(out=QT[:, :, tl * 128:(tl + 1) * 128], in_=pbf(7)[:96, :].rearrange("p (h t) -> p h t", h=8), func=AF.Copy),
                   reads=[PB[7]], writes=[QT])

        cf = c_f[par]
        S.task("dve", lambda e: e.scalar_tensor_tensor(out=cf[:np_, :], in0=Z0.ap[:np_, 256:384], scalar=s_c, in1=rows[:np_, R_KVW:R_KVW + 128],
                                                       op0=ALU.mult, op1=ALU.mult), reads=[Z0, stt, rows], writes=[cf])
        dst_c = D["ckv_s"][:, :] if sample else D["ckv_p"][i * 128:(i + 1) * 128, :]
        S.dma("sp", lambda e: e.dma_start(out=dst_c, in_=cf[:np_, :]), cf, reads=[cf], final=True)
        if sample:
            S.task("pool", lambda e: e.tensor_copy(out=cown_b[:, :], in_=cf[:np_, :]), reads=[cf], writes=[cown_b])
        S.task("pe", lambda e: e.transpose(out=PB[0].ap[:, 0:np_], in_=cf[:np_, :], identity=ident_f[:np_, :np_]), reads=[cf, ident_f], writes=[PB[0]])
        S.task("act", lambda e: e.activation(out=ckvT[:, :np_], in_=PB[0].ap[:, 0:np_], func=AF.Copy), reads=[PB[0]], writes=[ckvT])

        def f(e):
            ins = e.matmul(PB[5].ap[:np_, :], lhsT=ckvT[:, :np_], rhs=w_uk[:, :], start=True, stop=True)
            if not sample:
                ins = e.matmul(PB[6].ap[:np_, :], lhsT=ckvT[:, :np_], rhs=w_uv[:, :], start=True, stop=True)
            return ins
        S.task("pe", f, reads=[ckvT, w_uk, w_uv], writes=[PB[5], PB[6]])
        if not sample:
            S.task("act", lambda e: e.activation(out=Vaug[qb][:, tl, :, 0:64], in_=PB[6].ap[:, :].rearrange("p (h d) -> p h d", h=8), func=AF.Copy),
                   reads=[PB[6]], writes=[Vaug[qb]])
        S.task("act", lambda e: e.activation(out=q_sq[:np_, 0:512], in_=PB[5].ap[:np_, :], func=AF.Square), reads=[PB[5]], writes=[q_sq])

        S.chain("dve", [
            lambda e: e.tensor_reduce(out=stt[:np_, 20:28], in_=q_sq[:np_, 0:512].rearrange("p (h d) -> p h d", h=8), axis=AX.X, op=ALU.add),
            lambda e: e.tensor_scalar(out=stt[:np_, 20:28], in0=stt[:np_, 20:28], scalar1=1.0 / 64, scalar2=None, op0=ALU.mult),
        ], reads=[q_sq], writes=[stt])
        S.task("act", lambda e: e.activation(out=q_sq[:np_, 0:8], in_=stt[:np_, 20:28], func=AF.Sqrt, bias=cols[:np_, C_EPS:C_EPS + 1], scale=1.0),
               reads=[stt, cols], writes=[q_sq])
        S.task("dve", lambda e: e.reciprocal(out=stt[:np_, 28:36], in_=q_sq[:np_, 0:8]), reads=[q_sq], writes=[stt])

        S.chain("dve", [
            lambda e: e.tensor_tensor(out=kn_f[:np_, :].rearrange("p (h d) -> p h d", h=8), in0=PB[5].ap[:np_, :].rearrange("p (h d) -> p h d", h=8),
                                      in1=stt[:np_, 28:36].unsqueeze(2).broadcast_to([np_, 8, 64]), op=ALU.mult),
            lambda e: e.tensor_tensor(out=kn_f[:np_, :], in0=kn_f[:np_, :], in1=rows[:np_, R_KNW:R_KNW + 512], op=ALU.mult),
        ], reads=[PB[5], stt, rows], writes=[kn_f], nosync=True)
        kdst = Kcat_s if sample else Kcat_b
        S.task("pool", lambda e: e.tensor_copy(out=kdst[:np_, :, 0:64], in_=kn_f[:np_, :].rearrange("p (h d) -> p h d", h=8)), reads=[kn_f], writes=[kdst])
        krf = kr_f[par]
        S.task("dve", lambda e: e.scalar_tensor_tensor(out=krt[:np_, 0:2, :].rearrange("p a b -> p (a b)"), in0=Z0.ap[:np_, 384:416], scalar=s_kr,
                                                       in1=rows[:np_, R_KRW:R_KRW + 32], op0=ALU.mult, op1=ALU.mult), reads=[Z0, stt, rows], writes=[krt])
        if sample:
            c1, s1, cst1 = coss[:, :], sins[:, :], [coss, sins]
        else:
            c1, s1, cst1 = cosp[:, i, :], sinp[:, i, :], [cosp, sinp]

        def f(e):
            e.tensor_tensor(out=krt[:np_, 2, :], in0=krt[:np_, 0, :], in1=c1, op=ALU.mult)
            e.tensor_tensor(out=krt[:np_, 3, :], in0=krt[:np_, 1, :], in1=s1, op=ALU.mult)
            e.tensor_tensor(out=krt[:np_, 4, :], in0=krt[:np_, 0, :], in1=s1, op=ALU.mult)
            return e.tensor_tensor(out=krt[:np_, 5, :], in0=krt[:np_, 1, :], in1=c1, op=ALU.mult)

        def g(e):
            e.tensor_tensor(out=krf[:np_, 0:16], in0=krt[:np_, 2, :], in1=krt[:np_, 3, :], op=ALU.subtract)
            return e.tensor_tensor(out=krf[:np_, 16:32], in0=krt[:np_, 4, :], in1=krt[:np_, 5, :], op=ALU.add)
        S.chain("dve", [f, g], reads=[krt] + cst1, writes=[krf, krt])
        dst_k = D["kr_s"][:, :] if sample else D["kr_p"][i * 128:(i + 1) * 128, :]
        S.dma("sp", lambda e: e.dma_start(out=dst_k, in_=krf[:np_, :]), krf, reads=[krf], final=True)
        S.task("pool", lambda e: e.tensor_copy(out=kdst[:np_, :, 64:96], in_=krf[:np_, :].unsqueeze(1).broadcast_to([np_, 8, 32])), reads=[krf], writes=[kdst])
        if not sample:
            def f(e):
                ins = None
                for h in range(8):
                    ins = e.transpose(out=pbf(7)[:96, h * 128:(h + 1) * 128], in_=Kcat_b[:, h, :], identity=ident_b[:, :])
                return ins
            S.task("pe", f, reads=[Kcat_b, ident_b], writes=[PB[7]])
            S.task("act", lambda e: e.activation(out=Kop[qb][:, :, tl * 128:(tl + 1) * 128], in_=pbf(7)[:96, :].rearrange("p (h t) -> p h t", h=8), func=AF.Copy),
                   reads=[PB[7]], writes=[Kop[qb]])

        if sample:
            S.chain("dve", [
                lambda e: e.tensor_tensor(out=mixt[:np_, :].rearrange("p (g d) -> p g d", g=8), in0=vf[:np_, :].rearrange("p (g d) -> p g d", g=8),
                                          in1=rows[:np_, R_WS00:R_WS00 + 8].unsqueeze(2).broadcast_to([np_, 8, 64]), op=ALU.mult),
                lambda e: e.tensor_tensor(out=mixt[:np_, :].rearrange("p (g d) -> p g d", g=8), in0=mixt[:np_, :].rearrange("p (g d) -> p g d", g=8),
                                          in1=rows[:np_, R_BS0:R_BS0 + 8].unsqueeze(2).broadcast_to([np_, 8, 64]), op=ALU.add),
                lambda e: e.tensor_tensor(out=sg_b[:np_, :], in0=mixt[:np_, :], in1=u_f[:np_, :], op=ALU.mult),
            ], reads=[vf, rows, u_f], writes=[mixt, sg_b])
        else:
            S.task("pool", lambda e: e.tensor_copy(out=v_b[:, :], in_=vf[:, :]), reads=[vf], writes=[v_b])

            def f(e):
                ins = None
                for g in range(8):
                    ins = e.matmul(PB[6].ap[:, g * 64:(g + 1) * 64], lhsT=wsT[:, g, :], rhs=v_b[:, g * 64:(g + 1) * 64], start=True, stop=True)
                return ins
            S.task("pe", f, reads=[wsT, v_b], writes=[PB[6]])

            S.chain("dve", [
                lambda e: e.tensor_tensor(out=mixt[:, :].rearrange("p (g d) -> p g d", g=8), in0=PB[6].ap[:, :].rearrange("p (g d) -> p g d", g=8),
                                          in1=cols[:, C_BST:C_BST + 8].unsqueeze(2).broadcast_to([128, 8, 64]), op=ALU.add),
                lambda e: e.tensor_tensor(out=sg_b[:, :], in0=mixt[:, :], in1=u_f[:, :], op=ALU.mult),
            ], reads=[PB[6], cols, u_f], writes=[mixt, sg_b], nosync=True)

        def f(e):
            ins = None
            for c in range(4):
                ins = e.transpose(out=pbf(7)[:, c * 128:c * 128 + np_], in_=sg_b[:np_, c * 128:(c + 1) * 128], identity=ident_b[:np_, :np_])
            return ins
        S.task("pe", f, reads=[sg_b, ident_b], writes=[PB[7]])
        if sample:
            S.task("act", lambda e: e.activation(out=sgT_s[:, :, :], in_=pbf(7)[:, 0:512].rearrange("p (c t) -> p c t", c=4)[:, :, :np_], func=AF.Copy),
                   reads=[PB[7]], writes=[sgT_s])
        else:
            S.task("act", lambda e: e.activation(out=sgT[:, :, tl * 128:(tl + 1) * 128], in_=pbf(7)[:, 0:512].rearrange("p (c t) -> p c t", c=4), func=AF.Copy),
                   reads=[PB[7]], writes=[sgT])

    sc_banks = [2, 3, 4, 5]

    def attention_block(Q):
        nk = (Q + 1) * 4
        units = [(h, kj) for h in range(8) for kj in range(nk)]
        NU = len(units)

        def geo(u):
            h, kj = units[u]
            a = kj - 4 * Q
            c0 = 128 * a if a > 0 else 0
            return h, kj, kj // 4, kj % 4, a, c0, PB[sc_banks[u % 4]], pT[u % 3], PB[h % 2]

        def st_S(u):
            h, kj, kq, kt, a, c0, sbk, pt_, ob = geo(u)
            S.task("pe", lambda e: e.matmul(sbk.ap[:, c0:512], lhsT=Kop[kq][:, h, kt * 128:(kt + 1) * 128], rhs=QT[:, h, c0:512], start=True, stop=True),
                   reads=[Kop[kq], QT], writes=[sbk])

        def st_E(u):
            h, kj, kq, kt, a, c0, sbk, pt_, ob = geo(u)
            S.task("act", lambda e: e.activation(out=pt_[:, c0:512], in_=sbk.ap[:, c0:512], func=AF.Exp, scale=SCALE), reads=[sbk], writes=[pt_])
            if a >= 0:
                S.task("pool", lambda e: e.tensor_tensor(out=pt_[:, c0:c0 + 128], in0=pt_[:, c0:c0 + 128], in1=tri_b[:, :], op=ALU.mult),
                       reads=[pt_, tri_b], writes=[pt_])

        def st_V(u):
            h, kj, kq, kt, a, c0, sbk, pt_, ob = geo(u)
            S.task("pe", lambda e: e.matmul(ob.ap[:65, c0:512], lhsT=Vaug[kq][:, kt, h, :], rhs=pt_[:, c0:512], start=(kj == 0), stop=(kj == nk - 1)),
                   reads=[Vaug[kq], pt_], writes=[ob])
            if kj == nk - 1:
                S.task("dve", lambda e: e.reciprocal(out=rden[64:65, :], in_=ob.ap[64:65, :]), reads=[ob], writes=[rden])
                S.task("pe", lambda e: e.matmul(PB[6].ap[:64, :], lhsT=ones_f[64:65, 0:64], rhs=rden[64:65, :], start=True, stop=True),
                       reads=[ones_f, rden], writes=[PB[6]])
                S.task("act", lambda e: e.activation(out=rden_bc[:64, :], in_=PB[6].ap[:64, :], func=AF.Copy), reads=[PB[6]], writes=[rden_bc])
                S.task("dve", lambda e: e.tensor_tensor(out=attnT[:, h, :], in0=ob.ap[:64, :], in1=rden_bc[:64, :], op=ALU.mult),
                       reads=[ob, rden_bc], writes=[attnT])

        for s in range(NU + 2):
            if s < NU:
                st_S(s)
            if 0 <= s - 2 < NU:
                st_V(s - 2)
            if 0 <= s - 1 < NU:
                st_E(s - 1)

    def wo_tile(Q, ti):
        i = Q * 4 + ti
        xt = x_t[ti % 2]
        ht = h_t[ti % 2]
        ld("sp", xt, xt[:, :], D["xp"][i * 128:(i + 1) * 128, :])

        def f(e):
            ins = None
            for nb in range(2):
                chunks = [(attnT[:, h, ti * 128:(ti + 1) * 128], w_oH[:, h, nb * 512:(nb + 1) * 512]) for h in range(8)]
                chunks += [(sgT[:, c, ti * 128:(ti + 1) * 128], w_oS[:, c, nb * 512:(nb + 1) * 512]) for c in range(4)]
                for k, (l, r) in enumerate(chunks):
                    ins = e.matmul(PB[6 + nb].ap[:, :], lhsT=l, rhs=r, start=(k == 0), stop=(k == len(chunks) - 1))
            return ins
        S.task("pe", f, reads=[attnT, sgT, w_oH, w_oS], writes=[PB[6], PB[7]])
        for nb in range(2):
            S.task("dve", lambda e, nb=nb: e.tensor_tensor(out=ht[:, nb * 512:(nb + 1) * 512], in0=PB[6 + nb].ap[:, :], in1=xt[:, nb * 512:(nb + 1) * 512], op=ALU.add),
                   reads=[PB[6 + nb], xt], writes=[ht])
        S.dma("sp", lambda e: e.dma_start(out=h_scr[i * 128:(i + 1) * 128, :], in_=ht[:, :]), ht, reads=[ht], writes=[h_dr[i]])

    phase_a_tile(0, sample=True)
    for Q in range(NQB):
        for tl in range(4):
            phase_a_tile(Q * 4 + tl)
        attention_block(Q)
        for ti in range(4):
            wo_tile(Q, ti)
    S.barrier()

    phase("U")
    c_nat = [sb("c_nat%d" % i, [128, 128 * 128], BF16) for i in range(2)]
    r_nat = [sb("r_nat%d" % i, [128, 128 * 32], BF16) for i in range(2)]
    cT = [sb("cT%d" % i, [128, 128], BF16) for i in range(4)]
    krT = [sb("krT%d" % i, [128, 128], BF16) for i in range(2)]
    ysq = [sb("ysq%d" % i, [128, 512], BF16) for i in range(3)]
    ssq = [sb("ssq%d" % i, [128, 256]) for i in range(2)]
    rsq = sb("rsq", [128, 256]); sq_t = sb("sq_t", [128, 256])
    pTs = [sb("pTs%d" % i, [128, 256], BF16) for i in range(2)]
    qaT = sb("qaT", [128, 8, NS], BF16)
    gT_s = sb("gT_s", [64, 8, NS], BF16)
    QR4 = sb("QR4", [128, 4, 8, NS], BF16)
    qr_rep = sb("qr_rep", [NS, 8, 128], BF16)
    qr_repT = sb("qr_repT", [128, 8, NS])
    s_own = sb("s_own", [NS, 8, 96]); p_own = sb("p_own", [NS, 16])
    Pmask = sb("Pmask", [NS, NS, 8], BF16)
    OL = sb("OL", [8, NS, 128], BF16)
    OLT = sb("OLT", [128, NS, 8], BF16)
    rd_s = sb("rd_s", [8, 2])
    attn_s = sb("attn_s", [NS, 512], BF16)

    def sample_attention():
        S.task("dve", lambda e: e.tensor_scalar(out=gT_s[:, :, :], in0=QT_s[0:64, :, :], scalar1=cols[0:64, C_KNW:C_KNW + 1], scalar2=None, op0=ALU.mult),
               reads=[QT_s, cols], writes=[gT_s])

        def f(e):
            ins = None
            for h in range(8):
                ins = e.matmul(PB[7].ap[:, h * NS:(h + 1) * NS], lhsT=w_ukT[:, h, :], rhs=gT_s[:, h, :], start=True, stop=True)
            return ins
        S.task("pe", f, reads=[w_ukT, gT_s], writes=[PB[7]])
        S.task("act", lambda e: e.activation(out=qaT[:, :, :].rearrange("p h b -> p (h b)"), in_=PB[7].ap[:, 0:8 * NS], func=AF.Copy), reads=[PB[7]], writes=[qaT])
        S.task("dve", lambda e: e.tensor_copy(out=qr_rep[:, :, :].rearrange("p h (a r) -> p h a r", a=4),
                                              in_=Qcat_s[:, :, 64:96].unsqueeze(2).broadcast_to([NS, 8, 4, 32])), reads=[Qcat_s], writes=[qr_rep])

        def f(e):
            ins = None
            for h in range(8):
                ins = e.transpose(out=pbf(6)[:, h * NS:(h + 1) * NS], in_=qr_rep[:, h, :], identity=ident_b[:NS, :NS])
            return ins
        S.task("pe", f, reads=[qr_rep, ident_b], writes=[PB[6]])
        S.task("act", lambda e: e.activation(out=qr_repT[:, :, :].rearrange("p h b -> p (h b)"), in_=pbf(6)[:, 0:8 * NS], func=AF.Copy),
               reads=[PB[6]], writes=[qr_repT])

        def f(e):
            ins = None
            for a in range(4):
                ins = e.tensor_scalar(out=QR4[:, a, :, :], in0=qr_repT[:, :, :], scalar1=cols[:, C_BLK + a:C_BLK + a + 1], scalar2=None, op0=ALU.mult)
            return ins
        S.task("dve", f, reads=[qr_repT, cols], writes=[QR4])

        S.chain("dve", [
            lambda e: e.tensor_tensor(out=s_own[:, :, :], in0=Qcat_s[:, :, :], in1=Kcat_s[:, :, :], op=ALU.mult),
            lambda e: e.tensor_reduce(out=p_own[:, 0:8], in_=s_own[:, :, :], axis=AX.X, op=ALU.add),
        ], reads=[Qcat_s, Kcat_s], writes=[s_own, p_own])
        S.task("act", lambda e: e.activation(out=p_own[:, 8:16], in_=p_own[:, 0:8], func=AF.Exp, scale=SCALE), reads=[p_own], writes=[p_own])
        S.task("dve", lambda e: e.tensor_tensor(out=Pmask[:, :, :], in0=p_own[:, 8:16].unsqueeze(1).broadcast_to([NS, NS, 8]),
                                                in1=eye16[:, :].unsqueeze(2).broadcast_to([NS, NS, 8]), op=ALU.mult), reads=[p_own, eye16], writes=[Pmask])

        OB, DB = PB[6], PB[7]
        cslot = [Tile(pbf(0)[:, k * 128:(k + 1) * 128], "cslot%d" % k) for k in range(8)]
        rslot = [Tile(pbf(1)[:, k * 128:(k + 1) * 128], "rslot%d" % k) for k in range(4)]
        SNR = [PB[4], PB[5]]
        NTOT = NS * 128
        deferred = {}

        def defer(s, fn):
            deferred.setdefault(s, []).append(fn)

        def gather(b):
            cn, rn = c_nat[b % 2], r_nat[b % 2]
            S.dma("pool", lambda e: e.indirect_dma_start(out=cn[:, :], out_offset=None, in_=D["cache_c"][:, :],
                  in_offset=bass.IndirectOffsetOnAxis(ap=idx[:, b:b + 1], axis=0)), cn, reads=[idx], writes=[cn])
            S.dma("pool", lambda e: e.indirect_dma_start(out=rn[:, :], out_offset=None, in_=D["cache_r"][:, :],
                  in_offset=bass.IndirectOffsetOnAxis(ap=idx[:, b:b + 1], axis=0)), rn, reads=[idx], writes=[rn])

        def info(n):
            b, t = n // 128, n % 128
            qg = n // 32
            return b, t, t % 32, qg, n // 4

        def st_T(n):
            b, t, tl, qg, g = info(n)
            cn, rn = c_nat[b % 2], r_nat[b % 2]
            bank = PB[n % 2]
            bv = pbf(n % 2)

            def f(e):
                if n % 4 == 0:
                    e.transpose(out=bv[:, 128:256], in_=rn[:, t * 32:(t + 4) * 32], identity=ident_b[:, :])
                return e.transpose(out=bv[:, 0:128], in_=cn[:, t * 128:(t + 1) * 128], identity=ident_b[:, :])
            S.task("pe", f, reads=[cn, rn, ident_b], writes=[bank])

        def st_C(n):
            b, t, tl, qg, g = info(n)
            bank = PB[n % 2]
            bv = pbf(n % 2)
            if n % 4 == 0:
                krT_ = krT[g % 2]
                S.task("act", lambda e: e.activation(out=krT_[:, :], in_=bv[:, 128:256], func=AF.Copy), reads=[bank], writes=[krT_])
            cT_ = cT[n % 4]
            if n % 2 == 1:
                S.task("dve", lambda e: e.tensor_scalar(out=cT_[:, :], in0=bv[:, 0:128], scalar1=1.0, scalar2=None, op0=ALU.mult), reads=[bank], writes=[cT_])
            else:
                S.task("act", lambda e: e.activation(out=cT_[:, :], in_=bv[:, 0:128], func=AF.Copy), reads=[bank], writes=[cT_])

        def st_Y(n):
            b, t, tl, qg, g = info(n)
            cT_, krT_, yb, snr = cT[n % 4], krT[g % 2], PB[2 + n % 2], SNR[qg % 2]
            tt = n % 4

            def f(e):
                e.matmul(yb.ap[:, :], lhsT=cT_[:, :], rhs=w_uk[:, :], start=True, stop=True)
                e.matmul(snr.ap[:, tl * 8:(tl + 1) * 8], lhsT=cT_[:, :], rhs=qaT[:, :, b], start=True, stop=True)
                return e.matmul(snr.ap[:, 256 + tl * 8:256 + (tl + 1) * 8], lhsT=krT_[:, :], rhs=QR4[:, tt, :, b], start=True, stop=True)
            S.task("pe", f, reads=[cT_, w_uk, qaT, krT_, QR4], writes=[yb, snr])

        def st_Q(n):
            yb, ys_ = PB[2 + n % 2], ysq[n % 3]
            S.task("act", lambda e: e.activation(out=ys_[:, :], in_=yb.ap[:, :], func=AF.Square), reads=[yb], writes=[ys_])

        def st_R(n):
            b, t, tl, qg, g = info(n)
            ys_, ss_ = ysq[n % 3], ssq[qg % 2]
            S.task("dve", lambda e: e.tensor_reduce(out=ss_[:, tl * 8:(tl + 1) * 8], in_=ys_[:, :].rearrange("p (h d) -> p h d", h=8),
                                                    axis=AX.X, op=ALU.add), reads=[ys_], writes=[ss_])
            if tl == 31:
                quarter_end(qg, n + SK[3] + 1)

        def quarter_end(qg, s0):
            b, qtr = qg // 4, qg % 4
            snr, ss_, pts, cn = SNR[qg % 2], ssq[qg % 2], pTs[qg % 2], c_nat[b % 2]
            defer(s0 + 1, lambda: S.task("act", lambda e: e.activation(out=sq_t[:, :], in_=ss_[:, :], func=AF.Ln, bias=cols[:, C_EPS:C_EPS + 1], scale=1.0 / 64),
                                         reads=[ss_, cols], writes=[sq_t]))
            defer(s0 + 2, lambda: S.task("act", lambda e: e.activation(out=rsq[:, :], in_=sq_t[:, :], func=AF.Exp, scale=-0.5), reads=[sq_t], writes=[rsq]))
            defer(s0 + 3, lambda: S.task("dve", lambda e: e.tensor_tensor(out=rsq[:, :], in0=snr.ap[:, 0:256], in1=rsq[:, :], op=ALU.mult), reads=[snr, rsq], writes=[rsq]))
            defer(s0 + 4, lambda: S.task("dve", lambda e: e.tensor_tensor(out=rsq[:, :], in0=snr.ap[:, 256:512], in1=rsq[:, :], op=ALU.add), reads=[snr, rsq], writes=[rsq]))
            defer(s0 + 5, lambda: S.task("act", lambda e: e.activation(out=pts[:, :], in_=rsq[:, :], func=AF.Exp, scale=SCALE), reads=[rsq], writes=[pts]))

            def pv():
                def f(e):
                    ins = None
                    for tl in range(32):
                        t = qtr * 32 + tl
                        first = (qtr == 0 and tl == 0)
                        e.matmul(OB.ap[:8, 0:128], lhsT=pts[:, tl * 8:(tl + 1) * 8], rhs=cn[:, t * 128:(t + 1) * 128], start=first, stop=False)
                        ins = e.matmul(DB.ap[:8, 0:1], lhsT=pts[:, tl * 8:(tl + 1) * 8], rhs=ones_b[:, 0:1], start=first, stop=False)
                    return ins
                S.task("pe", f, reads=[pts, cn, ones_b], writes=[OB, DB])
            defer(s0 + 7, pv)
            if qtr == 3:
                def fin():
                    def f(e):
                        e.matmul(OB.ap[:8, 0:128], lhsT=Pmask[:, b, :], rhs=cown_b[:, :], start=False, stop=True)
                        return e.matmul(DB.ap[:8, 0:1], lhsT=Pmask[:, b, :], rhs=ones_b[:NS, 0:1], start=False, stop=True)
                    S.task("pe", f, reads=[Pmask, cown_b, ones_b], writes=[OB, DB])
                defer(s0 + 7, fin)
                defer(s0 + 9, lambda: S.task("dve", lambda e: e.reciprocal(out=rd_s[:, 0:1], in_=DB.ap[:8, 0:1]), reads=[DB], writes=[rd_s]))
                defer(s0 + 10, lambda: S.task("dve", lambda e: e.tensor_scalar(out=OL[:, b, :], in0=OB.ap[:8, 0:128], scalar1=rd_s[:, 0:1], scalar2=None, op0=ALU.mult),
                                              reads=[OB, rd_s], writes=[OL]))

        SK = [int(v) for v in os.environ.get('KSKEW', '1,2,3,4').split(',')]
        gather(0)
        for s in range(NTOT + 24):
            if s < NTOT:
                st_T(s)
            if SK[1] > SK[0] and 0 <= s - SK[1] < NTOT:
                st_Y(s - SK[1])
            if 0 <= s - SK[0] < NTOT:
                st_C(s - SK[0])
            if SK[1] <= SK[0] and 0 <= s - SK[1] < NTOT:
                st_Y(s - SK[1])
            if 0 <= s - SK[2] < NTOT:
                st_Q(s - SK[2])
            if 0 <= s - SK[3] < NTOT:
                st_R(s - SK[3])
            for fn in deferred.pop(s, []):
                fn()
            if s < NTOT and s % 128 == 12 and s // 128 + 1 < NS:
                gather(s // 128 + 1)
        assert not deferred, sorted(deferred)

        def f(e):
            ins = None
            for b in range(NS):
                ins = e.transpose(out=pbf(2)[:, b * 8:(b + 1) * 8], in_=OL[:, b, :], identity=ident_b[:8, :8])
            return ins
        S.task("pe", f, reads=[OL, ident_b], writes=[PB[2]])
        S.task("act", lambda e: e.activation(out=OLT[:, :, :].rearrange("p b h -> p (b h)"), in_=pbf(2)[:, 0:8 * NS], func=AF.Copy), reads=[PB[2]], writes=[OLT])

        def f(e):
            ins = None
            for h in range(8):
                ins = e.matmul(PB[3].ap[:NS, h * 64:(h + 1) * 64], lhsT=OLT[:, :, h], rhs=w_uv[:, h * 64:(h + 1) * 64], start=True, stop=True)
            return ins
        S.task("pe", f, reads=[OLT, w_uv], writes=[PB[3]])
        S.task("act", lambda e: e.activation(out=attn_s[:, :], in_=PB[3].ap[:NS, :], func=AF.Copy), reads=[PB[3]], writes=[attn_s])

        def f(e):
            ins = None
            for c in range(4):
                ins = e.transpose(out=pbf(2)[:, 512 + c * NS:512 + (c + 1) * NS], in_=attn_s[:, c * 128:(c + 1) * 128], identity=ident_b[:NS, :NS])
            return ins
        S.task("pe", f, reads=[attn_s, ident_b], writes=[PB[2]])
        S.task("act", lambda e: e.activation(out=attnT_s[:, :, :].rearrange("p c b -> p (c b)"), in_=pbf(2)[:, 512:512 + 4 * NS], func=AF.Copy),
               reads=[PB[2]], writes=[attnT_s])

    sample_attention()
    S.barrier()

    phase("U")
    wog = sb("wog", [128, 8, DM], BF16)
    w_proj = sb("w_proj", [128, 2, DM], BF16)
    w_fo = sb("w_fo", [128, 22, 512], BF16)
    GW = 256
    NG = DFF // GW
    wfi = [sb("wfi%d" % i, [128, 8, 2, GW], BF16) for i in range(2)]
    h_sb = [sb("h_sb%d" % i, [128, DM]) for i in range(4)] + [h_s]
    hn_b = sb("hn_b", [128, DM], BF16)
    NBX = 512 + NS
    hn2T = sb("hn2T", [128, 8, NBX], BF16)
    gTt = sb("gTt", [128, 22, NBX], BF16)
    a_sb = [sb("a_sb%d" % i, [128, 2, 2 + 512]) for i in range(2)]
    a_s = [sb("a_s%d" % i, [128, 2, NS]) for i in range(2)]
    cv = [sb("cv%d" % i, [128, 2, NBX]) for i in range(2)]
    sil = [sb("sil%d" % i, [128, NBX]) for i in range(2)]
    carry = sb("carry", [128, 44, 2])
    histT = sb("histT", [128, 44, 2 * NS])
    hist_pc = sb("hist_pc", [2 * NS, 1408])
    a_tok = [sb("a_tok%d" % i, [NS, 2, GW]) for i in range(2)]
    a_tok2 = [sb("a_tok2%d" % i, [2, 2, GW]) for i in range(2)]
    p_t = [sb("p_t%d" % i, [128, 256]) for i in range(2)]
    pTt = sb("pTt", [128, 2, 128], BF16)
    h3T = sb("h3T", [128, 8, 128], BF16)
    gate_f = sb("gate_f", [128, DM])
    e_f = [sb("e_f%d" % i, [128, DM]) for i in range(2)]

    def norm_transpose(hs, np_, stt, gcol, dstT, c0):
        S.task("act", lambda e: e.activation(out=junk[:np_, :], in_=hs[:np_, :], func=AF.Square, accum_out=stt[:np_, 0:1]), reads=[hs], writes=[junk, stt])
        S.task("dve", lambda e: e.tensor_scalar(out=stt[:np_, 1:2], in0=stt[:np_, 0:1], scalar1=1.0 / DM, scalar2=None, op0=ALU.mult), reads=[stt], writes=[stt])
        rstd_chain(stt, stt[:np_, 1:2], stt, stt[:np_, 2:3], stt, stt[:np_, 3:4], np_)
        S.task("act", lambda e: e.activation(out=hn_b[:np_, :], in_=hs[:np_, :], func=AF.Identity, scale=stt[:np_, 2:3]), reads=[hs, stt], writes=[hn_b])

        def f(e):
            ins = None
            for c in range(8):
                ins = e.transpose(out=pbf(7)[:, c * 128:c * 128 + np_], in_=hn_b[:np_, c * 128:(c + 1) * 128], identity=ident_b[:np_, :np_])
            return ins
        S.task("pe", f, reads=[hn_b, ident_b], writes=[PB[7]])
        S.task("dve", lambda e: e.tensor_tensor(out=dstT[:, :, c0:c0 + np_], in0=pbf(7)[:, :].rearrange("p (c t) -> p c t", c=8)[:, :, :np_],
                                                in1=cols[:, gcol:gcol + 8].unsqueeze(2).broadcast_to([128, 8, np_]), op=ALU.mult),
               reads=[PB[7], cols], writes=[dstT])

    def post_init():
        ld("pool", wog, wog[:, :, :], D["w_o"].rearrange("(c p) n -> p c n", p=128))
        ld("pool", w_proj, w_proj[:, :, :], D["w_proj"].rearrange("(c p) n -> p c n", p=128))
        S.task("pool", lambda e: e.memset(carry[:, :, :], 0.0), writes=[carry])
        S.dma("sp", lambda e: e.dma_start(out=D["conv_s"][:, 0, :], in_=D["hist"].rearrange("(b k) f -> b k f", k=2)[:, 1, :]), hist_pc, final=True)
        for pc in range(4):
            ld("sp", hist_pc, hist_pc[:, :], D["hist"][:, pc * 1408:(pc + 1) * 1408])

            def f(e, pc=pc):
                ins = None
                for k in range(11):
                    ins = e.transpose(out=PB[7].ap[:, k * 32:(k + 1) * 32], in_=hist_pc[:, k * 128:(k + 1) * 128], identity=ident_f[:2 * NS, :2 * NS])
                return ins
            S.task("pe", f, reads=[hist_pc, ident_f], writes=[PB[7]])
            S.task("act", lambda e, pc=pc: e.activation(out=histT[:, pc * 11:(pc + 1) * 11, :].rearrange("p c k -> p (c k)"), in_=PB[7].ap[:, 0:11 * 32], func=AF.Copy),
                   reads=[PB[7]], writes=[histT])
        xt = x_t[0]
        ld("sp", xt, xt[:NS, :], D["xs"][:, :])

        def f(e):
            ins = None
            for nb in range(2):
                chunks = [(attnT_s[:, c, :], wog[:, c, nb * 512:(nb + 1) * 512]) for c in range(4)]
                chunks += [(sgT_s[:, c, :], wog[:, 4 + c, nb * 512:(nb + 1) * 512]) for c in range(4)]
                for k, (l, r) in enumerate(chunks):
                    ins = e.matmul(PB[nb].ap[:NS, :], lhsT=l, rhs=r, start=(k == 0), stop=(k == 7))
            return ins
        S.task("pe", f, reads=[attnT_s, sgT_s, wog], writes=[PB[0], PB[1]])
        for nb in range(2):
            S.task("dve", lambda e, nb=nb: e.tensor_tensor(out=h_s[:, nb * 512:(nb + 1) * 512], in0=PB[nb].ap[:NS, :], in1=xt[:NS, nb * 512:(nb + 1) * 512], op=ALU.add),
                   reads=[PB[nb], xt], writes=[h_s])
        ld("pool", wog, wog[:, :, :], D["w_gate"].rearrange("(c p) n -> p c n", p=128))

    def post_tile_in(blk, ti):
        i = blk * 4 + ti
        hs = h_sb[ti]
        ld("sp", hs, hs[:, :], h_scr[i * 128:(i + 1) * 128, :], reads=[h_dr[i]])
        norm_transpose(hs, 128, stp[ti % 2], C_FNW, hn2T, ti * 128)

    def ffn_block(blk):
        last = (blk == NQB - 1)
        UPG = GW // 128
        NU = 22
        nb_ = NBX if last else 512

        def geo(n):
            g, j = n // UPG, n % UPG
            par = n % 2
            ab = (PB[4], PB[5]) if par == 0 else (PB[6], PB[7])
            return g, j, wfi[g % 2], ab, a_sb[par], a_s[par], cv[par], sil[par]

        def st_M(n):
            g, j, wt, ab, asb, ass, cvt, sl = geo(n)
            if j == 0:
                ld("pool", wt, wt[:, :, :, :].rearrange("p c h n -> p (c h n)"), D["w_ffi_r"][g])
                if last:
                    at, at2 = a_tok[g % 2], a_tok2[g % 2]

                    def f(e):
                        ins = None
                        for half in range(2):
                            for c in range(8):
                                ins = e.matmul(PB[2].ap[:NS, half * GW:(half + 1) * GW], lhsT=hn2T[:, c, 512:512 + NS], rhs=wt[:, c, half, :], start=(c == 0), stop=(c == 7))
                        for half in range(2):
                            for c in range(8):
                                ins = e.matmul(PB[3].ap[:2, half * GW:(half + 1) * GW], lhsT=hn2T[:, c, 510:512], rhs=wt[:, c, half, :], start=(c == 0), stop=(c == 7))
                        return ins
                    S.task("pe", f, reads=[hn2T, wt], writes=[PB[2], PB[3]])
                    S.task("act", lambda e: e.activation(out=at[:, :, :].rearrange("p a b -> p (a b)"), in_=PB[2].ap[:NS, 0:2 * GW], func=AF.Copy), reads=[PB[2]], writes=[at])
                    S.task("act", lambda e: e.activation(out=at2[:, :, :].rearrange("p a b -> p (a b)"), in_=PB[3].ap[:2, 0:2 * GW], func=AF.Copy), reads=[PB[3]], writes=[at2])
                    for half in range(2):
                        S.dma("sp", lambda e, half=half: e.dma_start(out=D["conv_s"][:, 1, half * DFF + g * GW:half * DFF + (g + 1) * GW], in_=at[:, half, :]),
                              at, reads=[at], final=True)
                        S.dma("sp", lambda e, half=half: e.dma_start(out=D["conv_p"][:, half * DFF + g * GW:half * DFF + (g + 1) * GW], in_=at2[:, half, :]),
                              at2, reads=[at2], final=True)

            def f(e):
                ins = None
                for half in range(2):
                    for c in range(8):
                        ins = e.matmul(ab[half].ap[:, :], lhsT=wt[:, c, half, j * 128:(j + 1) * 128], rhs=hn2T[:, c, 0:512], start=(c == 0), stop=(c == 7))
                return ins
            S.task("pe", f, reads=[wt, hn2T], writes=[ab[0], ab[1]])
            if last:
                def f2(e):
                    ins = None
                    for half in range(2):
                        for c in range(8):
                            ins = e.matmul(PB[n % 2].ap[:, half * NS:(half + 1) * NS], lhsT=wt[:, c, half, j * 128:(j + 1) * 128], rhs=hn2T[:, c, 512:512 + NS],
                                           start=(c == 0), stop=(c == 7))
                    return ins
                S.task("pe", f2, reads=[wt, hn2T], writes=[PB[n % 2]])

        def st_E(n):
            g, j, wt, ab, asb, ass, cvt, sl = geo(n)
            for half in range(2):
                ch = half * 22 + n
                S.task("act", lambda e, half=half: e.activation(out=asb[:, half, 2:514], in_=ab[half].ap[:, :], func=AF.Copy), reads=[ab[half]], writes=[asb])
                S.task("pool", lambda e, half=half, ch=ch: e.tensor_copy(out=asb[:, half, 0:2], in_=carry[:, ch, :]), reads=[carry], writes=[asb])
                S.task("pool", lambda e, half=half, ch=ch: e.tensor_copy(out=carry[:, ch, :], in_=asb[:, half, 512:514]), reads=[asb], writes=[carry])
            if last:
                S.task("act", lambda e: e.activation(out=ass[:, :, :].rearrange("p a b -> p (a b)"), in_=PB[n % 2].ap[:, 0:2 * NS], func=AF.Copy),
                       reads=[PB[n % 2]], writes=[ass])

        def st_V(n):
            g, j, wt, ab, asb, ass, cvt, sl = geo(n)
            W = []
            for half in range(2):
                ch = half * 22 + n
                W.append(tuple(cols[:, C_CW + k * 44 + ch:C_CW + k * 44 + ch + 1] for k in range(3)) + (cols[:, C_CB + ch:C_CB + ch + 1],))
            rd = [asb, cols]
            for step in range(3):
                for half in range(2):
                    w0, w1, w2, cb = W[half]
                    if step == 0:
                        fn = lambda e, half=half, w2=w2, cb=cb: e.tensor_scalar(out=cvt[:, half, 0:512], in0=asb[:, half, 2:514], scalar1=w2, scalar2=cb, op0=ALU.mult, op1=ALU.add)
                    elif step == 1:
                        fn = lambda e, half=half, w1=w1: e.scalar_tensor_tensor(out=cvt[:, half, 0:512], in0=asb[:, half, 1:513], scalar=w1, in1=cvt[:, half, 0:512], op0=ALU.mult, op1=ALU.add)
                    else:
                        fn = lambda e, half=half, w0=w0: e.scalar_tensor_tensor(out=cvt[:, half, 0:512], in0=asb[:, half, 0:512], scalar=w0, in1=cvt[:, half, 0:512], op0=ALU.mult, op1=ALU.add)
                    S.task("dve", fn, reads=rd + ([cvt] if step else []), writes=[cvt], nosync=(step > 0))
            if last:
                for half in range(2):
                    ch = half * 22 + n
                    w0, w1, w2, cb = W[half]
                    hv = histT[:, ch, :].rearrange("p (b k) -> p k b", k=2)
                    S.chain("dve", [
                        lambda e, half=half, w2=w2, cb=cb: e.tensor_scalar(out=cvt[:, half, 512:NBX], in0=ass[:, half, :], scalar1=w2, scalar2=cb, op0=ALU.mult, op1=ALU.add),
                        lambda e, half=half, w1=w1, hv=hv: e.scalar_tensor_tensor(out=cvt[:, half, 512:NBX], in0=hv[:, 1, :], scalar=w1, in1=cvt[:, half, 512:NBX], op0=ALU.mult, op1=ALU.add),
                        lambda e, half=half, w0=w0, hv=hv: e.scalar_tensor_tensor(out=cvt[:, half, 512:NBX], in0=hv[:, 0, :], scalar=w0, in1=cvt[:, half, 512:NBX], op0=ALU.mult, op1=ALU.add),
                    ], reads=[ass, cols, histT, cvt], writes=[cvt])

        def st_L(n):
            g, j, wt, ab, asb, ass, cvt, sl = geo(n)
            S.task("act", lambda e: e.activation(out=sl[:, 0:nb_], in_=cvt[:, 0, 0:nb_], func=AF.Silu), reads=[cvt], writes=[sl])

        def st_U(n):
            g, j, wt, ab, asb, ass, cvt, sl = geo(n)
            S.task("dve", lambda e: e.tensor_tensor(out=gTt[:, n, 0:nb_], in0=sl[:, 0:nb_], in1=cvt[:, 1, 0:nb_], op=ALU.mult), reads=[sl, cvt], writes=[gTt])

        for s in range(NU + 4):
            if s < NU:
                st_M(s)
            if 0 <= s - 1 < NU:
                st_E(s - 1)
            if 0 <= s - 4 < NU:
                st_U(s - 4)
            if 0 <= s - 2 < NU:
                st_V(s - 2)
            if 0 <= s - 3 < NU:
                st_L(s - 3)

    def ffn_out(blk):
        tl_list = [(ti, 128, ti * 128, h_sb[ti]) for ti in range(4)]
        if blk == NQB - 1:
            tl_list.append((4, NS, 512, h_s))
        for nb in range(2):
            ld("pool", w_fo, w_fo[:, :, :].rearrange("p c n -> p (c n)"), D["w_ffo_r"][nb])
            for k, (ti, np_, c0, hs) in enumerate(tl_list):
                bank = PB[2 + k % 2]

                def f(e, bank=bank, np_=np_, c0=c0):
                    ins = None
                    for fc in range(22):
                        ins = e.matmul(bank.ap[:np_, :], lhsT=gTt[:, fc, c0:c0 + np_], rhs=w_fo[:, fc, :], start=(fc == 0), stop=(fc == 21))
                    return ins
                S.task("pe", f, reads=[gTt, w_fo], writes=[bank])
                S.task("dve", lambda e, bank=bank, np_=np_, hs=hs, nb=nb: e.tensor_tensor(out=hs[:np_, nb * 512:(nb + 1) * 512], in0=bank.ap[:np_, :],
                                                                                         in1=hs[:np_, nb * 512:(nb + 1) * 512], op=ALU.add),
                       reads=[bank], writes=[hs])

    def post_tile_out(blk, ti, sample=False):
        np_ = NS if sample else 128
        hs = h_s if sample else h_sb[ti]
        stt = stp[ti % 2]
        pt_ = p_t[ti % 2]
        yt = e_f[ti % 2]
        src_p = D["ps"][:, :] if sample else D["pp"][(blk * 4 + ti) * 128:(blk * 4 + ti + 1) * 128, :]
        ld("sp", pt_, pt_[:np_, :], src_p)
        norm_transpose(hs, np_, stt, C_PNW, h3T, 0)

        def f(e):
            ins = None
            for nb in range(2):
                for c in range(8):
                    ins = e.matmul(PB[nb].ap[:np_, :], lhsT=h3T[:, c, :np_], rhs=wog[:, c, nb * 512:(nb + 1) * 512], start=(c == 0), stop=(c == 7))
            return ins
        S.task("pe", f, reads=[h3T, wog], writes=[PB[0], PB[1]])
        for nb in range(2):
            S.task("act", lambda e, nb=nb: e.activation(out=gate_f[:np_, nb * 512:(nb + 1) * 512], in_=PB[nb].ap[:np_, :], func=AF.Sigmoid), reads=[PB[nb]], writes=[gate_f])

        def f(e):
            ins = None
            for c in range(2):
                ins = e.transpose(out=PB[6].ap[:, c * 128:c * 128 + np_], in_=pt_[:np_, c * 128:(c + 1) * 128], identity=ident_f[:np_, :np_])
            return ins
        S.task("pe", f, reads=[pt_, ident_f], writes=[PB[6]])
        S.task("act", lambda e: e.activation(out=pTt[:, :, :np_], in_=PB[6].ap[:, 0:256].rearrange("p (c t) -> p c t", c=2)[:, :, :np_], func=AF.Copy), reads=[PB[6]], writes=[pTt])

        def f(e):
            ins = None
            for nb in range(2):
                for c in range(2):
                    ins = e.matmul(PB[4 + nb].ap[:np_, :], lhsT=pTt[:, c, :np_], rhs=w_proj[:, c, nb * 512:(nb + 1) * 512], start=(c == 0), stop=(c == 1))
            return ins
        S.task("pe", f, reads=[pTt, w_proj], writes=[PB[4], PB[5]])
        for nb in range(2):
            S.task("act", lambda e, nb=nb: e.activation(out=junk[:np_, 0:512], in_=PB[4 + nb].ap[:np_, :], func=AF.Square, accum_out=stt[:np_, 4 + nb:5 + nb]),
                   reads=[PB[4 + nb]], writes=[junk, stt])

        S.chain("dve", [
            lambda e: e.tensor_tensor(out=stt[:np_, 6:7], in0=stt[:np_, 4:5], in1=stt[:np_, 5:6], op=ALU.add),
            lambda e: e.tensor_scalar(out=stt[:np_, 6:7], in0=stt[:np_, 6:7], scalar1=1.0 / DM, scalar2=None, op0=ALU.mult),
        ], reads=[stt], writes=[stt])
        rstd_chain(stt, stt[:np_, 6:7], stt, stt[:np_, 7:8], stt, stt[:np_, 8:9], np_)
        for nb in range(2):
            sl = slice(nb * 512, (nb + 1) * 512)
            S.chain("dve", [
                lambda e, nb=nb, sl=sl: e.scalar_tensor_tensor(out=yt[:np_, sl], in0=PB[4 + nb].ap[:np_, :], scalar=stt[:np_, 7:8],
                                                               in1=rows[:np_, R_PPW + nb * 512:R_PPW + (nb + 1) * 512], op0=ALU.mult, op1=ALU.mult),
                lambda e, sl=sl: e.tensor_tensor(out=yt[:np_, sl], in0=yt[:np_, sl], in1=gate_f[:np_, sl], op=ALU.mult),
                lambda e, sl=sl: e.tensor_tensor(out=yt[:np_, sl], in0=yt[:np_, sl], in1=hs[:np_, sl], op=ALU.add),
            ], reads=[PB[4 + nb], stt, rows, gate_f, hs], writes=[yt], nosync=True)
        dst = D["y_s"][:, :] if sample else D["y_p"][(blk * 4 + ti) * 128:(blk * 4 + ti + 1) * 128, :]
        S.dma("sp", lambda e: e.dma_start(out=dst, in_=yt[:np_, :]), yt, reads=[yt], final=True)

    post_init()
    for blk in range(NQB):
        for ti in range(4):
            post_tile_in(blk, ti)
        if blk == NQB - 1:
            norm_transpose(h_s, NS, stp[0], C_FNW, hn2T, 512)
        ffn_block(blk)
        ffn_out(blk)
        for ti in range(4):
            post_tile_out(blk, ti)
        if blk == NQB - 1:
            post_tile_out(blk, 0, sample=True)

    S.check_deadlock()
    with nc.Block() as block:
        @block.tensor
        def _(e):
            S.replay("pe", e)

        @block.scalar
        def _(e):
            S.replay("act", e)

        @block.vector
        def _(e):
            S.replay("dve", e)

        @block.gpsimd
        def _(e):
            S.replay("pool", e)

        @block.sync
        def _(e):
            S.replay("sp", e)
    print("SBUF peak bytes/partition: G=%d U=%d sems=%d" % (peak["G"], peak["U"], S.nsem))
    es.close()
    return nc


def _host_consts(inp):
    f32 = np.float32
    cols = np.zeros((128, NCOL), f32)
    cols[:, C_ANW:C_ANW + 8] = inp["attn_norm_w"][0].reshape(8, 128).T
    cols[:, C_QNW:C_QNW + 2] = inp["q_norm_w"][0].reshape(2, 128).T
    cols[:, C_FNW:C_FNW + 8] = inp["ffn_norm_w"][0].reshape(8, 128).T
    cols[:, C_PNW:C_PNW + 8] = inp["ple_norm_w"][0].reshape(8, 128).T
    cw = inp["conv_w"][0]
    for k in range(3):
        cols[:, C_CW + k * 44:C_CW + (k + 1) * 44] = cw[k].reshape(44, 128).T
    cols[:, C_CB:C_CB + 44] = inp["conv_b"][0].reshape(44, 128).T
    cols[:, C_BST:C_BST + 8] = inp["b_s"][0].T
    cols[0:64, C_KNW] = inp["k_nope_norm_w"][0]
    for a in range(4):
        cols[a * 32:(a + 1) * 32, C_BLK + a] = 1.0
    cols[:, C_EPS] = EPS
    rows = np.zeros((1, NROW), f32)
    rows[0, R_KVW:R_KVW + 128] = inp["kv_norm_w"][0]
    rows[0, R_KRW:R_KRW + 32] = inp["k_rope_norm_w"][0]
    rows[0, R_QHW:R_QHW + 768] = np.tile(np.concatenate([inp["q_nope_norm_w"][0], inp["q_rope_norm_w"][0]]), 8)
    rows[0, R_KNW:R_KNW + 512] = np.tile(inp["k_nope_norm_w"][0], 8)
    rows[0, R_PPW:R_PPW + 1024] = inp["ple_post_norm_w"][0]
    rows[0, R_INVD:R_INVD + 3] = [1.0 / 256, 1.0 / 128, 1.0 / 32]
    rows[0, R_WS00:R_WS00 + 8] = inp["w_s"][0][:, 0, 0]
    rows[0, R_BS0:R_BS0 + 8] = inp["b_s"][0][:, 0]
    rows[0, R_INVH:R_INVH + 8] = 1.0 / 64
    rows[0, R_INVH + 8:R_INVH + 16] = 1.0 / 32
    inv = (10000.0 ** (-np.arange(16, dtype=np.float32) * (2.0 / 32))).astype(f32)
    pos = np.arange(T, dtype=f32)
    ang = pos[:, None] * inv[None, :]
    angs = np.full((NS, 1), 16384.0, f32) * inv[None, :]
    consts = {
        "cols": cols, "rows": rows, "ident": np.eye(128, dtype=f32),
        "tri": np.triu(np.ones((128, 128), f32)),
        "cosp": np.cos(ang).astype(f32), "sinp": np.sin(ang).astype(f32),
        "coss": np.cos(angs).astype(f32), "sins": np.sin(angs).astype(f32),
        "eye16": np.eye(NS, dtype=f32),
    }
    return consts


_NC_CACHE = {}


def kernel(**inp):
    inp = {k: np.asarray(v) for k, v in inp.items()}
    if "nc" not in _NC_CACHE:
        _NC_CACHE["nc"] = build_program()
    nc = _NC_CACHE["nc"]
    consts = _host_consts(inp)
    shared = {
        "cache_c": np.ascontiguousarray(inp["cache_ckv"][0].reshape(NPOOL, 128 * 128)),
        "cache_r": np.ascontiguousarray(inp["cache_krope"][0].reshape(NPOOL, 128 * 32)),
        "w_in": inp["w_in"][0], "w_uq": inp["w_uq"][0],
        "w_uk": np.ascontiguousarray(inp["w_uk"][0].reshape(128, 512)),
        "w_uv": np.ascontiguousarray(inp["w_uv"][0].reshape(128, 512)),
        "w_s": inp["w_s"][0], "w_o": inp["w_o"][0],
        "w_ffi_r": np.ascontiguousarray(inp["w_ff_in"][0].reshape(8, 128, 2, 11, 256).transpose(3, 1, 0, 2, 4).reshape(11, 128, 4096)),
        "w_ffo_r": np.ascontiguousarray(inp["w_ff_out"][0].reshape(22, 128, 2, 512).transpose(2, 1, 0, 3).reshape(2, 128, 22 * 512)),
        "w_gate": inp["w_ple_gate"][0], "w_proj": inp["w_ple_proj"][0],
    }
    shared.update(consts)
    in_maps = []
    for c in range(NCORES):
        m = dict(shared)
        m["xp"] = np.ascontiguousarray(inp["x_prompt"][c])
        m["pp"] = np.ascontiguousarray(inp["p_prompt"][0, c])
        m["xs"] = np.ascontiguousarray(inp["x_sample"][c * NS:(c + 1) * NS, 0])
        m["ps"] = np.ascontiguousarray(inp["p_sample"][0, c * NS:(c + 1) * NS, 0])
        m["pt"] = np.ascontiguousarray(inp["page_table"][c * NS:(c + 1) * NS].T.astype(np.int32))
        m["hist"] = np.ascontiguousarray(inp["state_conv"][0, c * NS:(c + 1) * NS].reshape(NS * 2, 2 * DFF))
        in_maps.append(m)
    res = run_bass_kernel_spmd(nc, in_maps, core_ids=list(range(NCORES)))
    R = res.results

    def cat(name):
        return np.concatenate([np.asarray(R[c][name]) for c in range(NCORES)], axis=0)

    y_p = cat("y_p").reshape(8, T, DM)
    y_s = cat("y_s").reshape(128, 1, DM)
    ckv_p = cat("ckv_p").reshape(1, 8, T, 128)
    kr_p = cat("kr_p").reshape(1, 8, T, 32)
    ckv_s = cat("ckv_s").reshape(1, 128, 1, 128)
    kr_s = cat("kr_s").reshape(1, 128, 1, 32)
    v_p = cat("v_p").reshape(1, 8, 128, 512)
    v_s = cat("v_s").reshape(1, 128, 1, 512)
    conv_p = cat("conv_p").reshape(1, 8, 2, 2 * DFF)
    conv_s = cat("conv_s").reshape(1, 128, 2, 2 * DFF)
    return tuple(np.ascontiguousarray(a, dtype=np.float32) for a in (y_p, y_s, ckv_p, kr_p, ckv_s, kr_s, v_p, v_s, conv_p, conv_s))
```
